# Optimizing a Trainium2 kernel written in Bass

```python
import jax
import jax.numpy as jnp
from jax import lax
import numpy as np

D_MODEL = 2048
BATCH = 8
SEQ = 2048
DEPTH = 2
DEC_BATCH = 128
DEC_SEQ = 1
PAST_LEN = 8192
PAGE_SIZE = 128

PLE_DIM = 256
D_MIX = D_MODEL
A_DK = 128
A_DV = 128
A_HEADS = (D_MIX // 2) // A_DV
A_WIDTH = A_HEADS * A_DV
HGRN_CHUNK = 64
B_DH = 64
B_HEADS = (D_MIX // 4) // B_DH
B_KV_HEADS = max(1, B_HEADS // 4)
B_GROUP = B_HEADS // B_KV_HEADS
B_WIDTH = B_HEADS * B_DH
WINDOW = 128
C_DG = 128
C_GROUPS = (D_MIX - A_WIDTH - B_WIDTH) // C_DG
C_WIDTH = C_GROUPS * C_DG
C_CHUNK = 128
D_FF = 256 * ((8 * D_MODEL // 3 + 255) // 256)
CONV_W = 3
EPS = 1e-6
D_IN = 4 * A_WIDTH + B_WIDTH + 2 * B_KV_HEADS * B_DH + 2 * C_WIDTH
SPLITS = (4 * A_WIDTH, 4 * A_WIDTH + B_WIDTH, 4 * A_WIDTH + B_WIDTH + B_KV_HEADS * B_DH, 4 * A_WIDTH + B_WIDTH + 2 * B_KV_HEADS * B_DH)

kernel_name = 'hymba_hgrn2_swa_gmlp_convffn_step'


def rms_norm(x, g):
    xf = x.astype(jnp.float32)
    y = xf * lax.rsqrt(jnp.mean(xf * xf, -1, keepdims=True) + EPS)
    return (y * g.astype(jnp.float32)).astype(x.dtype)


def layer_norm(x, g, b):
    xf = x.astype(jnp.float32)
    xc = xf - jnp.mean(xf, -1, keepdims=True)
    y = xc * lax.rsqrt(jnp.mean(xc * xc, -1, keepdims=True) + EPS)
    return (y * g.astype(jnp.float32) + b.astype(jnp.float32)).astype(x.dtype)


def hgrn2_lower_bounds(lb_logits):
    sm = jax.nn.softmax(lb_logits.astype(jnp.float32), axis=0)
    cs = jnp.cumsum(sm, axis=0)
    return cs - cs[0:1]


def hgrn2_chunked(q, log_f, k, v, s0):
    bsz, t, h, dk = q.shape
    dv = v.shape[-1]
    c = HGRN_CHUNK if t >= HGRN_CHUNK else t
    n = -(-t // c)
    pad = n * c - t

    def prep(a):
        a = jnp.pad(a, ((0, 0), (0, pad), (0, 0), (0, 0)))
        return a.reshape(bsz, n, c, h, a.shape[-1]).transpose(1, 0, 3, 2, 4)

    causal = jnp.tril(jnp.ones((c, c), bool))[:, :, None]

    def step(s, blk):
        qc, lfc, kc, vc = blk
        b = jnp.cumsum(lfc, axis=2)
        o_inter = jnp.einsum('bhtk,bhkv->bhtv', qc * jnp.exp(b), s)
        diff = b[:, :, :, None, :] - b[:, :, None, :, :]
        decay = jnp.exp(jnp.where(causal, diff, -jnp.inf))
        scores = jnp.einsum('bhtk,bhsk,bhtsk->bhts', qc, kc, decay)
        o_intra = jnp.einsum('bhts,bhsv->bhtv', scores, vc)
        b_last = b[:, :, -1:, :]
        s_new = jnp.exp(b_last[:, :, 0, :])[..., None] * s + jnp.einsum('bhsk,bhsv->bhkv', kc * jnp.exp(b_last - b), vc)
        return s_new, o_inter + o_intra

    s_fin, o = lax.scan(step, s0, (prep(q), prep(log_f), prep(k), prep(v)))
    o = o.transpose(1, 0, 3, 2, 4).reshape(bsz, n * c, h, dv)[:, :t]
    return o, s_fin


def mixer_hgrn2(za, lb, norm_g, s0):
    bsz, t, _ = za.shape
    zq, zf, zi, zg = jnp.split(za, 4, axis=-1)

    def heads(a):
        return a.reshape(bsz, t, A_HEADS, -1).astype(jnp.float32)

    q = jax.nn.silu(heads(zq))
    lbh = lb.reshape(A_HEADS, A_DK)
    log_f = jnp.logaddexp(jnp.log(lbh), jnp.log1p(-lbh) + jax.nn.log_sigmoid(heads(zf)))
    k = -jnp.expm1(log_f)
    v = heads(zi)
    o, s_fin = hgrn2_chunked(q, log_f, k, v, s0.astype(jnp.float32))
    o = rms_norm(o, norm_g) * jax.nn.silu(heads(zg))
    return o.reshape(bsz, t, A_WIDTH).astype(za.dtype), s_fin.astype(za.dtype)


def sink_attend(q, k, v, mask, sink):
    s = jnp.einsum('bnhgqd,bnhkd->bnhgqk', q.astype(jnp.float32), k.astype(jnp.float32)) * (B_DH ** -0.5)
    s = jnp.where(mask, s, -jnp.inf)
    sink_b = sink.astype(jnp.float32).reshape(B_KV_HEADS, B_GROUP)[:, :, None, None]
    m = jnp.maximum(jnp.max(s, -1, keepdims=True), sink_b)
    p = jnp.exp(s - m)
    denom = jnp.sum(p, -1, keepdims=True) + jnp.exp(sink_b - m)
    return jnp.einsum('bnhgqk,bnhkd->bnhgqd', p, v.astype(jnp.float32)) / denom


def swa_prompt(q, k, v, sink):
    bsz, t = q.shape[:2]
    nb = t // WINDOW
    qb = q.reshape(bsz, nb, WINDOW, B_KV_HEADS, B_GROUP, B_DH).transpose(0, 1, 3, 4, 2, 5)

    def blocks(a):
        a = a.reshape(bsz, nb, WINDOW, B_KV_HEADS, B_DH).transpose(0, 1, 3, 2, 4)
        prev = jnp.pad(a, ((0, 0), (1, 0), (0, 0), (0, 0), (0, 0)))[:, :-1]
        return jnp.concatenate([prev, a], axis=3)

    qi = jnp.arange(WINDOW)[:, None]
    kj = jnp.arange(2 * WINDOW)[None, :]
    band = (kj > qi) & (kj <= qi + WINDOW)
    not_first = jnp.arange(nb)[:, None, None] > 0
    mask = (band[None] & (not_first | (kj >= WINDOW)[None]))[:, None, None]
    o = sink_attend(qb, blocks(k), blocks(v), mask, sink)
    return o.transpose(0, 1, 4, 2, 3, 5).reshape(bsz, t, B_WIDTH)


def swa_decode(q, k, v, ck, cv, sink):
    bsz, t = q.shape[:2]
    kk = jnp.concatenate([ck.astype(k.dtype), k], axis=1)
    vv = jnp.concatenate([cv.astype(v.dtype), v], axis=1)
    qb = q.reshape(bsz, t, B_KV_HEADS, B_GROUP, B_DH).transpose(0, 2, 3, 1, 4)[:, None]
    kb = kk.transpose(0, 2, 1, 3)[:, None]
    vb = vv.transpose(0, 2, 1, 3)[:, None]
    qi = jnp.arange(t)[:, None]
    kj = jnp.arange(WINDOW + t)[None, :]
    mask = ((kj > qi) & (kj <= qi + WINDOW))[None, None, None]
    o = sink_attend(qb, kb, vb, mask, sink)
    o = o[:, 0].transpose(0, 3, 1, 2, 4).reshape(bsz, t, B_WIDTH)
    return o, kk[:, -WINDOW:], vv[:, -WINDOW:]


def chunk_mix(v, ws, bs):
    bsz, t = v.shape[:2]
    n = -(-t // C_CHUNK)
    pad = n * C_CHUNK - t
    vp = jnp.pad(v, ((0, 0), (0, pad), (0, 0), (0, 0))).reshape(bsz, n, C_CHUNK, C_GROUPS, C_DG)
    w = jnp.where(jnp.tril(jnp.ones((C_CHUNK, C_CHUNK), bool)), ws, 0.0).astype(v.dtype)
    mixed = jnp.einsum('gpq,bnqgc->bnpgc', w, vp) + bs.T[:, :, None].astype(v.dtype)
    return mixed.reshape(bsz, n * C_CHUNK, C_GROUPS, C_DG)[:, :t]


def mixer_gmlp(zc, ln_g, ln_b, ws, bs):
    bsz, t, _ = zc.shape
    u, vr = jnp.split(jax.nn.gelu(zc), 2, axis=-1)
    vn = layer_norm(vr, ln_g, ln_b).reshape(bsz, t, C_GROUPS, C_DG)
    out = u * chunk_mix(vn, ws, bs).reshape(bsz, t, C_WIDTH)
    return out, vn


def conv_ffn(h, prev, w_up, conv_w, conv_b, w_down):
    up = h @ w_up
    t = up.shape[1]
    xp = jnp.concatenate([prev.astype(up.dtype), up], axis=1)
    conv = conv_b + conv_w[0] * xp[:, 0:t]
    for j in range(1, CONV_W):
        conv = conv + conv_w[j] * xp[:, j:j + t]
    gate, val = jnp.split(conv, 2, axis=-1)
    return (jax.nn.silu(gate) * val) @ w_down, xp[:, -(CONV_W - 1):]


def trunk_layer(x, p, lb, st_hgrn, st_k, st_v, st_conv, w, decode):
    bsz, t, _ = x.shape
    h = rms_norm(x, w['norm1_g'])
    z = h @ w['w_in']
    za, zq, zk, zv, zc = jnp.split(z, SPLITS, axis=-1)
    o_a, hgrn_new = mixer_hgrn2(za, lb, w['hgrn_norm_g'], st_hgrn)
    q = rms_norm(zq.reshape(bsz, t, B_HEADS, B_DH), w['q_norm_g'])
    k = rms_norm(zk.reshape(bsz, t, B_KV_HEADS, B_DH), w['k_norm_g'])
    v = zv.reshape(bsz, t, B_KV_HEADS, B_DH)
    if decode:
        o_b, k_new, v_new = swa_decode(q, k, v, st_k, st_v, w['swa_sinks'])
    else:
        o_b = swa_prompt(q, k, v, w['swa_sinks'])
        k_new, v_new = k[:, -WINDOW:], v[:, -WINDOW:]
    o_c, c_rows = mixer_gmlp(zc, w['gmlp_ln_g'], w['gmlp_ln_b'], w['gmlp_ws'], w['gmlp_bs'])
    mix = jnp.concatenate([o_a, o_b.astype(x.dtype), o_c.astype(x.dtype)], axis=-1)
    x = x + mix @ w['w_out']
    f, conv_new = conv_ffn(rms_norm(x, w['norm2_g']), st_conv, w['w_up'], w['conv_w'], w['conv_b'], w['w_down'])
    x = x + f
    x = x + jax.nn.sigmoid(rms_norm(x, w['ple_norm_g']) @ w['w_ple_gate']) * (p @ w['w_ple_proj'])
    return x, hgrn_new, k_new, v_new, c_rows, conv_new


def setup_inputs(seed: int = 0) -> dict:
    key = jax.random.key(seed)
    ks = jax.random.split(key, 28)

    def nrm(k, shape, scale=1.0):
        return scale * jax.random.normal(k, shape, jnp.float32)

    return {
        'x_prompt': nrm(ks[0], (BATCH, SEQ, D_MODEL)),
        'x_sample': nrm(ks[1], (DEC_BATCH, DEC_SEQ, D_MODEL)),
        'state_hgrn': nrm(ks[2], (DEPTH, DEC_BATCH, A_HEADS, A_DK, A_DV), 0.5),
        'cache_swa_k': nrm(ks[3], (DEPTH, DEC_BATCH, WINDOW, B_KV_HEADS, B_DH)),
        'cache_swa_v': nrm(ks[4], (DEPTH, DEC_BATCH, WINDOW, B_KV_HEADS, B_DH)),
        'state_ffn_conv': nrm(ks[5], (DEPTH, DEC_BATCH, CONV_W - 1, 2 * D_FF)),
        'p_prompt': nrm(ks[6], (DEPTH, BATCH, SEQ, PLE_DIM)),
        'p_sample': nrm(ks[7], (DEPTH, DEC_BATCH, DEC_SEQ, PLE_DIM)),
        'norm1_g': 1.0 + nrm(ks[8], (DEPTH, D_MODEL), 0.1),
        'w_in': nrm(ks[9], (DEPTH, D_MODEL, D_IN), D_MODEL ** -0.5),
        'hgrn_lb_logits': nrm(ks[10], (DEPTH, A_HEADS * A_DK)),
        'hgrn_norm_g': 1.0 + nrm(ks[11], (DEPTH, A_DV), 0.1),
        'q_norm_g': 1.0 + nrm(ks[12], (DEPTH, B_DH), 0.1),
        'k_norm_g': 1.0 + nrm(ks[13], (DEPTH, B_DH), 0.1),
        'swa_sinks': nrm(ks[14], (DEPTH, B_HEADS), 0.5),
        'gmlp_ln_g': 1.0 + nrm(ks[15], (DEPTH, C_WIDTH), 0.1),
        'gmlp_ln_b': nrm(ks[16], (DEPTH, C_WIDTH), 0.02),
        'gmlp_ws': nrm(ks[17], (DEPTH, C_GROUPS, C_CHUNK, C_CHUNK), C_CHUNK ** -0.5),
        'gmlp_bs': 1.0 + nrm(ks[18], (DEPTH, C_GROUPS, C_CHUNK), 0.1),
        'w_out': nrm(ks[19], (DEPTH, D_MIX, D_MODEL), D_MIX ** -0.5),
        'norm2_g': 1.0 + nrm(ks[20], (DEPTH, D_MODEL), 0.1),
        'w_up': nrm(ks[21], (DEPTH, D_MODEL, 2 * D_FF), D_MODEL ** -0.5),
        'conv_w': nrm(ks[22], (DEPTH, CONV_W, 2 * D_FF), CONV_W ** -0.5),
        'conv_b': nrm(ks[23], (DEPTH, 2 * D_FF), 0.02),
        'w_down': nrm(ks[24], (DEPTH, D_FF, D_MODEL), D_FF ** -0.5),
        'ple_norm_g': 1.0 + nrm(ks[25], (DEPTH, D_MODEL), 0.1),
        'w_ple_gate': nrm(ks[26], (DEPTH, D_MODEL, D_MODEL), D_MODEL ** -0.5),
        'w_ple_proj': nrm(ks[27], (DEPTH, PLE_DIM, D_MODEL), PLE_DIM ** -0.5),
    }


def reference(x_prompt, x_sample, state_hgrn, cache_swa_k, cache_swa_v, state_ffn_conv, p_prompt, p_sample,
              norm1_g, w_in, hgrn_lb_logits, hgrn_norm_g, q_norm_g, k_norm_g, swa_sinks, gmlp_ln_g, gmlp_ln_b,
              gmlp_ws, gmlp_bs, w_out, norm2_g, w_up, conv_w, conv_b, w_down, ple_norm_g, w_ple_gate, w_ple_proj):
    lbs = hgrn2_lower_bounds(hgrn_lb_logits)
    xp, xs = x_prompt, x_sample
    hp, hs, kp, vp, ksm, vsm, gs, cp, cs = [], [], [], [], [], [], [], [], []
    for l in range(DEPTH):
        w = {
            'norm1_g': norm1_g[l], 'w_in': w_in[l], 'hgrn_norm_g': hgrn_norm_g[l],
            'q_norm_g': q_norm_g[l], 'k_norm_g': k_norm_g[l], 'swa_sinks': swa_sinks[l],
            'gmlp_ln_g': gmlp_ln_g[l], 'gmlp_ln_b': gmlp_ln_b[l], 'gmlp_ws': gmlp_ws[l], 'gmlp_bs': gmlp_bs[l],
            'w_out': w_out[l], 'norm2_g': norm2_g[l], 'w_up': w_up[l], 'conv_w': conv_w[l],
            'conv_b': conv_b[l], 'w_down': w_down[l], 'ple_norm_g': ple_norm_g[l],
            'w_ple_gate': w_ple_gate[l], 'w_ple_proj': w_ple_proj[l],
        }
        zero_h = jnp.zeros((xp.shape[0], A_HEADS, A_DK, A_DV), jnp.float32)
        zero_c = jnp.zeros((xp.shape[0], CONV_W - 1, 2 * D_FF), xp.dtype)
        xp, h_new, k_new, v_new, _, c_new = trunk_layer(xp, p_prompt[l], lbs[l], zero_h, None, None, zero_c, w, False)
        hp.append(h_new); kp.append(k_new); vp.append(v_new); cp.append(c_new)
        xs, h_new, k_new, v_new, g_new, c_new = trunk_layer(xs, p_sample[l], lbs[l], state_hgrn[l], cache_swa_k[l],
                                                            cache_swa_v[l], state_ffn_conv[l], w, True)
        hs.append(h_new); ksm.append(k_new); vsm.append(v_new); gs.append(g_new); cs.append(c_new)
    return (xp, xs, jnp.stack(hp), jnp.stack(hs), jnp.stack(kp), jnp.stack(vp), jnp.stack(ksm), jnp.stack(vsm),
            jnp.stack(gs), jnp.stack(cp), jnp.stack(cs))
```

```python
import contextlib
import numpy as np
import concourse.bass as bass
import concourse.mybir as mybir
from concourse.bass_utils import run_bass_kernel_spmd

F32 = mybir.dt.float32
BF16 = mybir.dt.bfloat16
AF = mybir.ActivationFunctionType
ALU = mybir.AluOpType
AX = mybir.AxisListType

NCORES = 8
D = 2048
T = 2048
NB = 16
DEPTH = 2
DIN = 5888
DFF = 5632
NJ = 44
PLE = 256
EPS = 1e-6
NG = 4
GW = 528
GELU_C = 1.5957691216057308


class Res:
    __slots__ = ("w", "rs")

    def __init__(self):
        self.w = None
        self.rs = {}


class Sched:
    ENG = ("pe", "act", "dve", "pool", "sp")

    def __init__(self, nc, st, ndsem=32):
        self.nc = nc
        self.engs = {"pe": nc.tensor, "act": nc.scalar, "dve": nc.vector, "pool": nc.gpsimd, "sp": nc.sync}
        self.sems = {}
        for e in self.ENG:
            self.sems[e] = st.enter_context(nc.semaphore("s_" + e))
        for i in range(ndsem):
            self.sems[("d", i)] = st.enter_context(nc.semaphore("s_d%d" % i))
        self.cnt = {e: 0 for e in self.ENG}
        self.seen = {e: {} for e in self.ENG}
        self.pend_r = []
        self.pend_w = []
        self.nd = ndsem
        self.duse = [0] * ndsem
        self.di = 0
        self.nins = 0
        self.npe = 0
        self.off = False

    def _wait(self, e, k, v):
        self.engs[e].wait_ge(self.sems[k], v)

    def _need(self, e, ev, waits):
        if ev is None:
            return
        k, v = ev
        if self.seen[e].get(k, 0) >= v:
            return
        if waits.get(k, 0) < v:
            waits[k] = v

    def _deps(self, e, reads, writes):
        waits = {}
        for r in reads:
            self._need(e, r.w, waits)
        for w in writes:
            if w.w is not None and w.w[0] != e:
                self._need(e, w.w, waits)
            for k, v in w.rs.items():
                if k != e:
                    self._need(e, (k, v), waits)
        for k, v in waits.items():
            self.seen[e][k] = v
            self._wait(e, k, v)

    def _commit(self, ev, reads, writes):
        for r in reads:
            if r.rs.get(ev[0], 0) < ev[1]:
                r.rs[ev[0]] = ev[1]
        for w in writes:
            w.w = ev
            w.rs = {}

    def op(self, e, fn, reads=(), writes=(), sig=True):
        if self.off:
            return
        reads = list(reads)
        writes = list(writes)
        if e != "pe":
            assert not self.pend_r and not self.pend_w, "PE group left open"
        self._deps(e, reads, writes)
        self.nins += 1
        if e == "pe":
            self.npe += 1
        ins = fn(self.engs[e])
        if sig:
            self.cnt[e] += 1
            ins.then_inc(self.sems[e], 1)
            ev = (e, self.cnt[e])
            if e == "pe":
                reads = reads + self.pend_r
                writes = writes + self.pend_w
                self.pend_r = []
                self.pend_w = []
            self._commit(ev, reads, writes)
        else:
            assert e == "pe"
            self.pend_r += reads
            self.pend_w += writes

    def dma(self, out, in_, reads=(), writes=(), q="sp", **kw):
        if self.off:
            return
        reads = list(reads)
        writes = list(writes)
        assert not self.pend_r and not self.pend_w
        i = self.di % self.nd
        self.di += 1
        k = ("d", i)
        prev = self.duse[i]
        self.duse[i] += 1
        self._deps(q, reads, writes)
        if prev > 0 and self.seen[q].get(k, 0) < 16 * prev:
            self.seen[q][k] = 16 * prev
            self._wait(q, k, 16 * prev)
        self.nins += 1
        self.engs[q].dma_start(out=out, in_=in_, **kw).then_inc(self.sems[k], 16)
        self._commit((k, 16 * self.duse[i]), reads, writes)

    def barrier(self):
        if self.off:
            return
        assert not self.pend_r and not self.pend_w
        for e in self.ENG:
            for o in self.ENG:
                if o != e and self.cnt[o] > self.seen[e].get(o, 0):
                    self.seen[e][o] = self.cnt[o]
                    self._wait(e, o, self.cnt[o])
            for i in range(self.nd):
                k = ("d", i)
                v = 16 * self.duse[i]
                if v > self.seen[e].get(k, 0):
                    self.seen[e][k] = v
                    self._wait(e, k, v)


class Buf:
    def __init__(self, t, nres=1):
        self.t = t
        self.r = Res()
        self.rq = [Res() for _ in range(nres)]


class Ring:
    def __init__(self, bufs):
        self.bufs = bufs
        self.i = 0

    def next(self):
        b = self.bufs[self.i % len(self.bufs)]
        self.i += 1
        return b


def build_program():
    nc = bass.Bass("TRN2", target_bir_lowering=False)

    def din(name, shape):
        return nc.dram_tensor(name, list(shape), F32, kind="ExternalInput").ap()

    def dout(name, shape):
        return nc.dram_tensor(name, list(shape), F32, kind="ExternalOutput").ap()

    xp = din("xp", [T, D]); xs = din("xs", [NB, D])
    sh = din("sh", [DEPTH, NB, 8, 128, 128])
    ck = din("ck", [DEPTH, NB, 128, 128]); cv = din("cv", [DEPTH, NB, 128, 128])
    scv = din("scv", [DEPTH, NB, 2, 2 * DFF])
    pp = din("pp", [DEPTH, T, PLE]); psm = din("psm", [DEPTH, NB, PLE])
    norm1_g = din("norm1_g", [DEPTH, D]); w_in = din("w_in", [DEPTH, D, DIN])
    lbl = din("hgrn_lb_logits", [DEPTH, 1024]); hgn = din("hgrn_norm_g", [DEPTH, 128])
    qng = din("q_norm_g", [DEPTH, 64]); kng = din("k_norm_g", [DEPTH, 64])
    sinks = din("swa_sinks", [DEPTH, 8])
    lng = din("gmlp_ln_g", [DEPTH, 512]); lnb = din("gmlp_ln_b", [DEPTH, 512])
    gws = din("gmlp_ws", [DEPTH, 4, 128, 128]); gbs = din("gmlp_bs", [DEPTH, 4, 128])
    w_out = din("w_out", [DEPTH, D, D]); norm2_g = din("norm2_g", [DEPTH, D])
    w_up = din("w_up", [DEPTH, D, 2 * DFF]); conv_w = din("conv_w", [DEPTH, 3, 2 * DFF])
    conv_b = din("conv_b", [DEPTH, 2 * DFF]); w_down = din("w_down", [DEPTH, DFF, D])
    ple_g = din("ple_norm_g", [DEPTH, D]); w_pg = din("w_ple_gate", [DEPTH, D, D])
    w_pp = din("w_ple_proj", [DEPTH, PLE, D])
    c_ident = din("c_ident", [128, 128]); c_caus = din("c_caus", [128, 128])
    c_tril = din("c_tril", [128, 128]); c_lows = din("c_lows", [128, 128])
    c_m2 = din("c_m2", [128, 64]); c_kmask = din("c_kmask", [128, 1])
    c_mask01 = din("c_mask01", [128, GW]); c_sel = din("c_sel", [16, 16 * 128])

    y_p = dout("y_p", [T, D]); y_s = dout("y_s", [NB, D])
    hp_o = dout("hp_o", [DEPTH, 8, 128, 128]); hs_o = dout("hs_o", [DEPTH, NB, 8, 128, 128])
    kp_o = dout("kp_o", [DEPTH, 128, 128]); vp_o = dout("vp_o", [DEPTH, 128, 128])
    ks_o = dout("ks_o", [DEPTH, NB, 128, 128]); vs_o = dout("vs_o", [DEPTH, NB, 128, 128])
    gv_o = dout("gv_o", [DEPTH, NB, 512])
    cp_o = dout("cp_o", [DEPTH, 2, 2 * DFF]); cs_o = dout("cs_o", [DEPTH, NB, 2, 2 * DFF])
    xbuf = nc.dram_tensor("xbuf", [T + NB, D], F32, kind="Internal").ap()
    import os
    KDBG = False
    if KDBG:
        dbg_x = nc.dram_tensor("dbg_x", [T + NB, D], F32, kind="ExternalOutput").ap()
        dbg_mix = nc.dram_tensor("dbg_mix", [NG, 128, 16, GW], BF16, kind="ExternalOutput").ap()

    with contextlib.ExitStack() as top:
        S = Sched(nc, top)

        uid = [0]

        def sbt(st, name, shape, dt=F32, nres=1):
            uid[0] += 1
            return Buf(st.enter_context(nc.sbuf_tensor("%s_%d" % (name, uid[0]), list(shape), dt)), nres)

        import os
        lim = 1000000
        cnt_ = [0]

        class _Stop(Exception):
            pass

        def mark(label):
            MARKS.append((label, S.npe))

        def chk():
            cnt_[0] += 1
            if cnt_[0] > lim:
                S.off = True

        def ACT(out, in_, func, reads, writes, scale=1.0, bias=None, accum=None):
            kw = {}
            if bias is not None:
                kw["bias"] = bias
            if accum is not None:
                kw["accum_out"] = accum
            S.op("act", lambda e: e.activation(out=out, in_=in_, func=func, scale=scale, **kw), reads, writes)

        def TS(eng, out, in0, s1, s2, op0, op1, reads, writes):
            if op1 is None:
                S.op(eng, lambda e: e.tensor_scalar(out=out, in0=in0, scalar1=s1, scalar2=None, op0=op0), reads, writes)
            else:
                S.op(eng, lambda e: e.tensor_scalar(out=out, in0=in0, scalar1=s1, scalar2=s2, op0=op0, op1=op1), reads, writes)

        def TT(eng, out, in0, in1, op, reads, writes):
            S.op(eng, lambda e: e.tensor_tensor(out=out, in0=in0, in1=in1, op=op), reads, writes)

        def STT(out, in0, scalar, in1, op0, op1, reads, writes):
            S.op("dve", lambda e: e.scalar_tensor_tensor(out=out, in0=in0, scalar=scalar, in1=in1, op0=op0, op1=op1), reads, writes)

        def CP(eng, out, in_, reads, writes):
            if eng == "act":
                ACT(out, in_, AF.Copy, reads, writes)
            else:
                S.op(eng, lambda e: e.tensor_copy(out=out, in_=in_), reads, writes)

        def RSQ(out, in_, scale, rows, cols, res):
            TS("pool", out, in_, scale, EPS, ALU.mult, ALU.add, [res], [res])
            TT("pool", out, out, mhalf.t[0:rows, 0:cols], ALU.pow, [res, mhalf.r], [res])

        def MM(out, lhsT, rhs, start, stop, reads, writes, sig=None):
            S.op("pe", lambda e: e.matmul(out=out, lhsT=lhsT, rhs=rhs, start=start, stop=stop), reads, writes,
                 sig=(stop if sig is None else sig))

        def TP(out, in_, rows, reads, writes, sig=True):
            S.op("pe", lambda e: e.transpose(out=out, in_=in_, identity=ident.t[0:rows, 0:rows]), reads + [ident.r], writes, sig=sig)

        ident = sbt(top, "ident", [128, 128])
        caus = sbt(top, "caus", [128, 128]); tril = sbt(top, "tril", [128, 128]); lows = sbt(top, "lows", [128, 128])
        m2 = sbt(top, "m2", [128, 64]); kmask = sbt(top, "kmask", [128, 1])
        mask01 = sbt(top, "mask01", [128, GW])
        hT = sbt(top, "hT", [128, 16, GW], BF16, nres=5)
        Sf = sbt(top, "Sf", [128, 8, 128], F32, nres=8); Sb = sbt(top, "Sb", [128, 8, 128], BF16, nres=8)
        KTall = sbt(top, "KTall", [128, T], BF16, nres=16)
        V1all = sbt(top, "V1all", [128, 16, 2, 65], BF16, nres=16)
        carry = sbt(top, "carry", [128, 2 * NJ, 2], F32, nres=2 * NJ)
        gam = sbt(top, "gam", [128, 3, 16]); lbt = sbt(top, "lbt", [128, 2, 8]); oml = sbt(top, "oml", [128, 2, 8])
        lgt = sbt(top, "lgt", [128, 2, 8])
        hgbc = sbt(top, "hgbc", [128, 128]); qgbc = sbt(top, "qgbc", [128, 64]); kgbc = sbt(top, "kgbc", [128, 64])
        esink = sbt(top, "esink", [128, 8]); lgbc = sbt(top, "lgbc", [128, 512]); lbbc = sbt(top, "lbbc", [128, 512])
        WsT = sbt(top, "WsT", [128, 4, 128], BF16); bsT = sbt(top, "bsT", [128, 4])
        w00 = sbt(top, "w00", [16, 4]); bs0 = sbt(top, "bs0", [16, 4])
        cw = sbt(top, "cw", [128, 3, 2 * NJ]); cb = sbt(top, "cb", [128, 2 * NJ])
        mhalf = sbt(top, "mhalf", [128, 16])
        stat = Ring([sbt(top, "stat%d" % i, [128, 16]) for i in range(4)])
        junk = sbt(top, "junk", [128, 128], BF16)
        stg_bufs = [sbt(top, "stg%d" % i, [128, 2048], F32, nres=4) for i in range(6)]
        for b_ in stg_bufs:
            b_.pending = False
        stg_ring = Ring(stg_bufs)
        cast_i = [0]

        def stg_take():
            sg = stg_ring.next()
            assert not sg.pending, "staging ring overrun"
            sg.pending = True
            return sg

        def cast_eng():
            cast_i[0] += 1
            return "act" if cast_i[0] % 2 else "dve"
        xo_ring = Ring([sbt(top, "xo%d" % i, [128, 512]) for i in range(5)])
        STQ = "pool"
        tA = Ring([sbt(top, "tA%d" % i, [128, 512]) for i in range(2)])
        pbanks = []
        for i in range(4):
            p = top.enter_context(nc.psum_tensor("ps%d" % i, [128, 1024], F32))
            for hh in range(2):
                pbanks.append((p, hh * 512, Res()))

        def bank(i):
            p, off, r = pbanks[i]
            return p[:, off:off + 512], r

        pring_acc = Ring([0, 1, 2, 3])
        pring_tp = Ring([4, 5])
        pring_x = Ring([6, 7])

        xb_r = [[Res() for _ in range(8)] for _ in range(17)]

        def cdma(buf, src, **kw):
            S.dma(buf.t[:], src, writes=[buf.r], **kw)

        S.op("pool", lambda e: e.memset(mhalf.t[:], -0.5), writes=[mhalf.r])
        cdma(ident, c_ident); cdma(caus, c_caus); cdma(tril, c_tril); cdma(lows, c_lows)
        cdma(m2, c_m2); cdma(kmask, c_kmask); cdma(mask01, c_mask01)
        for t in range(16):
            S.dma(xbuf[t * 128:(t + 1) * 128, :], xp[t * 128:(t + 1) * 128, :], writes=xb_r[t])
        S.dma(xbuf[T:T + NB, :], xs, writes=xb_r[16])
        S.dma(lgt.t[:], lbl.rearrange("l (h p) -> p l h", p=128), writes=[lgt.r], allow_slow_non_contiguous=True)
        S.op("dve", lambda e: e.memset(lbt.t[:], 0.0), writes=[lbt.r])
        TT("dve", lgt.t[:, 1, :], lgt.t[:, 1, :], lgt.t[:, 0, :], ALU.subtract, [lgt.r], [lgt.r])
        ACT(lbt.t[:, 1, :], lgt.t[:, 1, :], AF.Sigmoid, [lgt.r, lbt.r], [lbt.r])
        TS("dve", oml.t[:], lbt.t[:], -1.0, 1.0, ALU.mult, ALU.add, [lbt.r], [oml.r])
        S.op("dve", lambda e: e.memset(V1all.t[:], 1.0), writes=V1all.rq)

        def group_tiles(g):
            tl = [(g * 4 + i, (g * 4 + i) * 128, 128, i * 128, i) for i in range(4)]
            if g == NG - 1:
                tl.append((16, T, NB, 512, 4))
            return tl

        def moving(g):
            mv = [(0, 512, [0, 1, 2, 3])]
            if g == NG - 1:
                mv.append((512, NB, [4]))
            return mv

        def stage_cast(src, dst, dres, a, c):
            sg = stg_take()
            view = sg.t[:, 0:a * c].rearrange("p (a c) -> p a c", a=a)
            S.dma(view, src, writes=[sg.rq[0]])
            CP(cast_eng(), dst, view, [sg.rq[0]], [dres])
            sg.pending = False

        def wq(w_rows, a, nc_, dst, res):
            return {"view": lambda sg: sg.t[:, 0:a * nc_].rearrange("p (a c) -> p a c", a=a),
                    "dmas": lambda view: [(view, w_rows.rearrange("(a p) c -> p a c", p=128))],
                    "dst": dst, "res": res}

        def rq_gen(w_ap, kcn, ncols, qa, extra_q=None):
            nq_ = (kcn + qa - 1) // qa

            def quarters_(ds):
                ql = []
                for q in range(nq_):
                    a = min(qa, kcn - qa * q)
                    ql.append(wq(w_ap[q * qa * 128:(q * qa + a) * 128, ds * ncols:(ds + 1) * ncols], a, ncols,
                                 (lambda wb, q=q, a=a: wb.t[:, qa * q:qa * q + a, :]), (lambda wb, q=q: wb.rq[q])))
                if extra_q is not None:
                    ql += extra_q(ds)
                return ql
            return quarters_

        def fq_gen(l):
            def fq_(j):
                return [wq(w_up[l, :, gv * DFF + j * 128:gv * DFF + (j + 1) * 128], 16, 128,
                           (lambda wb, gv=gv: wb.t[:, gv, :, :]), (lambda wb, gv=gv: wb.rq[gv])) for gv in range(2)]
            return fq_

        def hq_gen(l):
            def hq_(h):
                ql = []
                for q in range(4):
                    def dmas(view, q=q):
                        return [(view[:, :, sgm, :],
                                 w_in[l, q * 512:(q + 1) * 512, sgm * 1024 + h * 128:sgm * 1024 + (h + 1) * 128].rearrange("(a p) c -> p a c", p=128))
                                for sgm in range(4)]
                    ql.append({"view": lambda sg: sg.t[:, 0:2048].rearrange("p (a s c) -> p a s c", a=4, s=4),
                               "dmas": dmas,
                               "dst": (lambda wb, q=q: wb.t[:, 4 * q:4 * q + 4, :].rearrange("p a (s c) -> p a s c", s=4)),
                               "res": (lambda wb, q=q: wb.rq[q])})
                return ql
            return hq_

        PRE = {}

        def prefetch(key, qlist):
            lst = []
            for q in qlist:
                sg = stg_take()
                view = q["view"](sg)
                for k_, (dsub, src) in enumerate(q["dmas"](view)):
                    S.dma(dsub, src, writes=[sg.rq[k_]])
                lst.append((sg, view))
            PRE[key] = lst

        def run_slabs(n, quarters, compute, wring, key=None, nxt=None):
            qs = {}
            stg = {}
            wbs = {}

            def Q(i):
                if i not in qs:
                    qs[i] = quarters(i)
                return qs[i]

            def issue(i, qi):
                q = Q(i)[qi]
                sg = stg_take()
                view = q["view"](sg)
                for k_, (dsub, src) in enumerate(q["dmas"](view)):
                    S.dma(dsub, src, writes=[sg.rq[k_]])
                stg[(i, qi)] = (sg, view)

            def cast(i, qi):
                q = Q(i)[qi]
                sg, view = stg.pop((i, qi))
                CP(cast_eng(), q["dst"](wbs[i]), view, sg.rq, [q["res"](wbs[i])])
                sg.pending = False

            def cast_and_prefetch(i):
                wbs[i] = wring.next()
                nq = len(Q(i))
                nq2 = len(Q(i + 1)) if i + 1 < n else 0
                for qi in range(max(nq, nq2)):
                    if qi < nq:
                        cast(i, qi)
                    if qi < nq2:
                        issue(i + 1, qi)

            if key is not None and key in PRE:
                for qi, ent in enumerate(PRE.pop(key)):
                    stg[(0, qi)] = ent
            else:
                for qi in range(len(Q(0))):
                    issue(0, qi)
            cast_and_prefetch(0)
            for i in range(n):
                if i + 1 < n:
                    cast_and_prefetch(i + 1)
                elif nxt is not None and not S.off:
                    prefetch(nxt[0], nxt[1])
                compute(i, wbs.pop(i))

        def transposes_to(dst_fn, src_buf, src_res, rows, nblk, dres, scale_fn=None):
            for b0 in range(0, nblk, 4):
                bi = pring_tp.next()
                bk, br = bank(bi)
                n = min(4, nblk - b0)
                for a in range(n):
                    TP(bk[:, a * 128:a * 128 + rows], src_buf[0:rows, (b0 + a) * 128:(b0 + a + 1) * 128], rows,
                       [src_res], [br], sig=(a == n - 1))
                for a in range(n):
                    i = b0 + a
                    eng = "act" if i % 2 == 0 else "dve"
                    src = bk[:, a * 128:a * 128 + rows]
                    if scale_fn is None:
                        CP(eng, dst_fn(i), src, [br], [dres])
                    elif eng == "act":
                        ACT(dst_fn(i), src, AF.Copy, [br, gam.r], [dres], scale=scale_fn(i))
                    else:
                        TS("dve", dst_fn(i), src, scale_fn(i), None, ALU.mult, None, [br, gam.r], [dres])

        def norm_pass(g, which):
            with contextlib.ExitStack() as ns:
                xt_ring = Ring([sbt(ns, "xt%d" % i, [128, D]) for i in range(2)])
                nj = sbt(ns, "nj", [128, D], BF16)
                for (gt, row0, rows, col0, li) in group_tiles(g):
                    xt = xt_ring.next()
                    S.dma(xt.t[0:rows, :], xbuf[row0:row0 + rows, :], reads=xb_r[gt], writes=[xt.r])
                    st = stat.next()
                    ACT(nj.t[0:rows, :], xt.t[0:rows, :], AF.Square, [xt.r], [nj.r, st.r], accum=st.t[0:rows, 0:1])
                    RSQ(st.t[0:rows, 2:3], st.t[0:rows, 0:1], 1.0 / D, rows, 1, st.r)
                    TS("dve", xt.t[0:rows, :], xt.t[0:rows, :], st.t[0:rows, 2:3], None, ALU.mult, None, [xt.r, st.r], [xt.r])
                    transposes_to(lambda i: hT.t[:, i, col0:col0 + rows], xt.t, xt.r, rows, 16, hT.rq[li],
                                  scale_fn=lambda i: gam.t[:, which, i:i + 1])
            S.barrier()

        def proj_tok(g, wb, ncols, kcn, lhs_buf, evac):
            for tl in group_tiles(g):
                (gt, row0, rows, col0, li) = tl
                bk, br = bank(pring_acc.next())
                for kc in range(kcn):
                    MM(bk[0:rows, 0:ncols], lhs_buf.t[:, kc, col0:col0 + rows], wb.t[:, kc, 0:ncols], kc == 0, kc == kcn - 1,
                       [lhs_buf.rq[li], wb.rq[kc // 4]], [br])
                evac(bk, br, tl)

        def resid_update(l, g, last, w_ap, kcn, lhs_buf, st_scope, ncols=512, extra=None, qa=4, key=None, nxt=None):
            nq = (kcn + qa - 1) // qa
            wring = Ring([sbt(st_scope, "wr%d" % i, [128, kcn, ncols], BF16, nres=nq) for i in range(2)])
            nsl = D // ncols

            quarters = rq_gen(w_ap, kcn, ncols, qa, extra[0] if extra is not None else None)

            def compute(ds, wb):
                tls = group_tiles(g)
                xos = {}
                blk_of = lambda gt: [xb_r[gt][b] for b in range(ds * ncols // 256, (ds + 1) * ncols // 256)]

                def load(i):
                    (gt, row0, rows, col0, li) = tls[i]
                    xo = xo_ring.next()
                    S.dma(xo.t[0:rows, 0:ncols], xbuf[row0:row0 + rows, ds * ncols:(ds + 1) * ncols], reads=blk_of(gt), writes=[xo.r], q=STQ)
                    xos[i] = xo
                PF = 3
                for i in range(min(PF, len(tls))):
                    load(i)
                for i, tl in enumerate(tls):
                    (gt, row0, rows, col0, li) = tl
                    bk, br = bank(pring_acc.next())
                    for kc in range(kcn):
                        MM(bk[0:rows, 0:ncols], lhs_buf.t[:, kc, col0:col0 + rows], wb.t[:, kc, 0:ncols], kc == 0, kc == kcn - 1,
                           [lhs_buf.rq[li], wb.rq[kc // qa]], [br])
                    if i + PF < len(tls):
                        load(i + PF)
                    xo = xos.pop(i)
                    if extra is None:
                        TT("dve", xo.t[0:rows, 0:ncols], xo.t[0:rows, 0:ncols], bk[0:rows, 0:ncols], ALU.add, [xo.r, br], [xo.r])
                    else:
                        extra[1](ds, bk, br, tl, xo)
                    if last:
                        dst = (y_p[row0:row0 + rows, ds * ncols:(ds + 1) * ncols] if gt < 16
                               else y_s[:, ds * ncols:(ds + 1) * ncols])
                        S.dma(dst, xo.t[0:rows, 0:ncols], reads=[xo.r], q=STQ)
                    else:
                        S.dma(xbuf[row0:row0 + rows, ds * ncols:(ds + 1) * ncols], xo.t[0:rows, 0:ncols], reads=[xo.r], writes=blk_of(gt), q=STQ)
            run_slabs(nsl, quarters, compute, wring, key=key, nxt=nxt)

        def mixers(l, g, st):
            has_dec = (g == NG - 1)
            tiles = group_tiles(g)
            mvs = moving(g)
            if False:
                has_dec = False
                tiles = tiles[0:4]
                mvs = mvs[0:1]
            mixT = sbt(st, "mixT", [128, 16, GW], BF16, nres=5)
            wring = Ring([sbt(st, "wi%d" % i, [128, 16, 512], BF16, nres=4) for i in range(2)])

            def epilog_tp(src, sres, rows, col0, li, fc0, nblk):
                transposes_to(lambda i: mixT.t[:, fc0 + i, col0:col0 + rows], src, sres, rows, nblk, mixT.rq[li])

            with contextlib.ExitStack() as hs:
                QT = sbt(hs, "QT", [128, GW]); FS = sbt(hs, "FS", [128, GW]); KTt = sbt(hs, "KTt", [128, GW])
                LF = sbt(hs, "LF", [128, GW]); Bc = sbt(hs, "Bc", [128, GW])
                Vb = sbt(hs, "Vb", [128, 5, 128], BF16, nres=5); GS = sbt(hs, "GS", [128, 5, 128], F32, nres=5)
                Vd = sbt(hs, "Vd", [16, 128]); sel = sbt(hs, "sel", [16, 16 * 128])
                e_ring = Ring([sbt(hs, "er%d" % i, [128, 128]) for i in range(6)])
                b_ring = Ring([sbt(hs, "br%d" % i, [128, 128], BF16) for i in range(30)])
                SC = [sbt(hs, "SC%d" % i, [128, 128], BF16) for i in range(6)]
                d_ring = Ring([sbt(hs, "d128%d" % i, [128, 4]) for i in range(2)])
                nb_ring = Ring([sbt(hs, "nb%d" % i, [128, 2]) for i in range(4)])
                oa_ring = Ring([sbt(hs, "oa%d" % i, [128, 128]) for i in range(2)])
                oa4 = [sbt(hs, "oaq%d" % i, [128, 128]) for i in range(4)]
                ei_ = [0]
                s0_ring = Ring([sbt(hs, "s0%d" % i, [128, 128]) for i in range(3)])
                sn_ring = Ring([sbt(hs, "sn%d" % i, [128, 128]) for i in range(3)])
                t1_ring = Ring([sbt(hs, "t1%d" % i, [128, 128]) for i in range(2)])
                snb_ring = Ring([sbt(hs, "snb%d" % i, [128, 128], BF16) for i in range(8)])
                QM = sbt(hs, "QM", [128, 16 * 16], BF16)
                for scb in SC:
                    S.op("pool", lambda e, scb=scb: e.memset(scb.t[:], 0.0), writes=[scb.r])
                S.op("pool", lambda e: e.memset(QM.t[:], 0.0), writes=[QM.r])
                KHX = 0
                if has_dec and not (KHX & 1):
                    S.dma(sel.t[:], c_sel, writes=[sel.r])
                sci = [0]

                hq = hq_gen(l)

                PB = [dict(QT=QT, FS=FS, Vb=Vb, GS=GS, Vd=Vd),
                      dict(QT=sbt(hs, "QT2", [128, GW]), FS=sbt(hs, "FS2", [128, GW]), Vb=sbt(hs, "Vb2", [128, 5, 128], BF16, nres=5),
                           GS=sbt(hs, "GS2", [128, 5, 128], F32, nres=5), Vd=sbt(hs, "Vd2", [16, 128]))]

                def proj_chunks(h, wb, P):
                    out = []
                    for blk, key_, fn in ((0, "QT", AF.Silu), (1, "FS", AF.Sigmoid)):
                        held = []

                        def mm(blk=blk, held=held):
                            for (c0, n, lis) in mvs:
                                bk, br = bank(pring_acc.next() if n == 512 else pring_x.next())
                                for kc in range(16):
                                    MM(bk[:, 0:n], wb.t[:, kc, blk * 128:(blk + 1) * 128], hT.t[:, kc, c0:c0 + n], kc == 0, kc == 15,
                                       [wb.rq[kc // 4]] + [hT.rq[i] for i in lis], [br])
                                held.append((bk, br, c0, n))

                        def ev(key_=key_, fn=fn, held=held):
                            dstb = P[key_]
                            for (bk, br, c0, n) in held:
                                ACT(dstb.t[:, c0:c0 + n], bk[:, 0:n], fn, [br], [dstb.r])
                        out.append((mm, ev))
                    for tsel in (tiles[0:2], tiles[2:]):
                        held = []

                        def mm(tsel=tsel, held=held):
                            for (gt, row0, rows, col0, li) in tsel:
                                bk, br = bank(pring_acc.next())
                                for kc in range(16):
                                    MM(bk[0:rows, 0:256], hT.t[:, kc, col0:col0 + rows], wb.t[:, kc, 256:512], kc == 0, kc == 15,
                                       [hT.rq[li], wb.rq[kc // 4]], [br])
                                held.append((bk, br, rows, li))

                        def ev(held=held):
                            Vb_, GS_, Vd_ = P["Vb"], P["GS"], P["Vd"]
                            for (bk, br, rows, li) in held:
                                CP("act", Vb_.t[0:rows, li, :], bk[0:rows, 0:128], [br], [Vb_.rq[li]])
                                if li == 4:
                                    CP("act", Vd_.t[0:rows, :], bk[0:rows, 0:128], [br], [Vd_.r])
                                ACT(GS_.t[0:rows, li, :], bk[0:rows, 128:256], AF.Silu, [br], [GS_.rq[li]])
                                TT("pool", GS_.t[0:rows, li, :], GS_.t[0:rows, li, :], hgbc.t[0:rows, :], ALU.mult, [GS_.rq[li], hgbc.r], [GS_.rq[li]])
                        out.append((mm, ev))
                    return out

                def rest_parts(h, P):
                    QT_, FS_, Vb_, GS_, Vd_ = P["QT"], P["FS"], P["Vb"], P["GS"], P["Vd"]
                    if g == 0:
                        S.op("pool", lambda e, h=h: e.memset(Sf.t[:, h, :], 0.0), writes=[Sf.rq[h]])
                        S.op("pool", lambda e, h=h: e.memset(Sb.t[:, h, :], 0.0), writes=[Sb.rq[h]])
                    W = GW if has_dec else 512
                    TS("dve", FS_.t[:, 0:W], FS_.t[:, 0:W], oml.t[:, l, h:h + 1], lbt.t[:, l, h:h + 1], ALU.mult, ALU.add, [FS_.r, oml.r, lbt.r], [FS_.r])
                    ACT(LF.t[:, 0:W], FS_.t[:, 0:W], AF.Ln, [FS_.r], [LF.r])
                    TS("pool", KTt.t[:, 0:W], FS_.t[:, 0:W], -1.0, 1.0, ALU.mult, ALU.add, [FS_.r], [KTt.r])
                    S.op("dve", lambda e: e.tensor_tensor_scan(out=Bc.t[:, 0:W], data0=mask01.t[:, 0:W], data1=LF.t[:, 0:W], initial=0.0,
                                                               op0=ALU.mult, op1=ALU.add), [mask01.r, LF.r], [Bc.r])
                    ptiles = [t_ for t_ in tiles if t_[4] < 4]
                    bkh, bkhr = bank(pring_tp.next())
                    bsc, bscr = bank(pring_x.next())
                    bst, bstr = bank(pring_tp.next())
                    d128 = d_ring.next()
                    per = []
                    for (gt, row0, rows, col0, li) in ptiles:
                        c0 = col0
                        cs_ = slice(li * 128, (li + 1) * 128)
                        E1 = e_ring.next(); EA = e_ring.next(); EK = e_ring.next()
                        Cb = b_ring.next(); Ab = b_ring.next(); Bb = b_ring.next(); Db = b_ring.next(); KH = b_ring.next()
                        KHT = e_ring.next()
                        nb = nb_ring.next()
                        ACT(E1.t[:], Bc.t[:, c0:c0 + 128], AF.Exp, [Bc.r], [E1.r])
                        TT("dve", Cb.t[:], QT_.t[:, c0:c0 + 128], E1.t[:], ALU.mult, [QT_.r, E1.r], [Cb.r])
                        CP("pool", d128.t[:, li:li + 1], E1.t[:, 127:128], [E1.r], [d128.r])
                        ACT(EA.t[:], Bc.t[:, c0:c0 + 128], AF.Exp, [Bc.r], [EA.r], scale=-1.0, bias=Bc.t[:, c0 + 63:c0 + 64])
                        TT("pool", Ab.t[:], KTt.t[:, c0:c0 + 128], EA.t[:], ALU.mult, [KTt.r, EA.r], [Ab.r])
                        EB = e_ring.next()
                        ACT(EB.t[:, 0:64], Bc.t[:, c0:c0 + 64], AF.Exp, [Bc.r], [EB.r], scale=-1.0)
                        TT("pool", Bb.t[:, 0:64], KTt.t[:, c0:c0 + 64], EB.t[:, 0:64], ALU.mult, [KTt.r, EB.r], [Bb.r])
                        TS("dve", nb.t[:, 0:1], Bc.t[:, c0 + 63:c0 + 64], -1.0, 0.0, ALU.mult, ALU.add, [Bc.r], [nb.r])
                        ACT(EB.t[:, 64:128], Bc.t[:, c0 + 64:c0 + 128], AF.Exp, [Bc.r, nb.r], [EB.r], bias=nb.t[:, 0:1])
                        TT("dve", Db.t[:, 0:64], QT_.t[:, c0 + 64:c0 + 128], EB.t[:, 64:128], ALU.mult, [QT_.r, EB.r], [Db.r])
                        ACT(EK.t[:], Bc.t[:, c0:c0 + 128], AF.Exp, [Bc.r], [EK.r], scale=-1.0, bias=Bc.t[:, c0 + 127:c0 + 128])
                        TT("pool", KHT.t[:], KTt.t[:, c0:c0 + 128], EK.t[:], ALU.mult, [KTt.r, EK.r], [KHT.r])
                        TP(bkh[:, cs_], KHT.t[:, :], 128, [KHT.r], [bkhr])
                        CP("act", KH.t[:], bkh[:, cs_], [bkhr], [KH.r])
                        MM(bsc[0:64, li * 128:li * 128 + 64], Bb.t[:, 0:64], Cb.t[:, 0:64], True, True, [Bb.r, Cb.r], [bscr], sig=False)
                        MM(bsc[:, li * 128 + 64:li * 128 + 128], Ab.t[:, :], Db.t[:, 0:64], True, True, [Ab.r, Db.r], [bscr])
                        scb = SC[sci[0] % len(SC)]; sci[0] += 1
                        TT("dve", scb.t[0:64, 0:64], bsc[0:64, li * 128:li * 128 + 64], caus.t[0:64, 0:64], ALU.mult, [bscr, caus.r], [scb.r])
                        TT("dve", scb.t[:, 64:128], bsc[:, li * 128 + 64:li * 128 + 128], m2.t[:, :], ALU.mult, [bscr, m2.r], [scb.r])
                        MM(bst[:, cs_], KH.t[:, :], Vb_.t[:, li, :], True, True, [KH.r, Vb_.rq[li]], [bstr])
                        per.append((Cb, scb, li, col0, gt))
                        if li in (0, 2):
                            yield
                    bo, bor = bank(pring_x.next())
                    for (Cb, scb, li, col0, gt) in per:
                        cs_ = slice(li * 128, (li + 1) * 128)
                        MM(bo[:, cs_], scb.t[:, :], Vb_.t[:, li, :], True, False, [scb.r, Vb_.rq[li]], [bor])
                        MM(bo[:, cs_], Cb.t[:, :], Sb.t[:, h, :], False, True, [Cb.r, Sb.rq[h]], [bor])
                        STT(Sf.t[:, h, :], Sf.t[:, h, :], d128.t[:, li:li + 1], bst[:, cs_], ALU.mult, ALU.add, [Sf.rq[h], d128.r, bstr], [Sf.rq[h]])
                        CP("act", Sb.t[:, h, :], Sf.t[:, h, :], [Sf.rq[h]], [Sb.rq[h]])
                        if gt == 15:
                            S.dma(hp_o[l, h], Sf.t[:, h, :], reads=[Sf.rq[h]], q=STQ)
                    yield
                    sts = [stat.next() for _ in per]
                    oas = [oa4[(ei_[0] + i_) % len(oa4)] for i_ in range(len(per))]
                    ei_[0] += len(per)
                    for st_, (Cb, scb, li, col0, gt) in zip(sts, per):
                        ACT(junk.t[:, 0:128], bo[:, li * 128:(li + 1) * 128], AF.Square, [bor], [junk.r, st_.r], accum=st_.t[:, 0:1])
                    for st_ in sts:
                        RSQ(st_.t[:, 2:3], st_.t[:, 0:1], 1.0 / 128, 128, 1, st_.r)
                    for st_, oa_, (Cb, scb, li, col0, gt) in zip(sts, oas, per):
                        STT(oa_.t[:, :], bo[:, li * 128:(li + 1) * 128], st_.t[:, 2:3], GS_.t[:, li, :], ALU.mult, ALU.mult, [bor, st_.r, GS_.rq[li]], [oa_.r])
                    bke, bker = bank(pring_tp.next())
                    for oa_, (Cb, scb, li, col0, gt) in zip(oas, per):
                        TP(bke[:, li * 128:(li + 1) * 128], oa_.t[:, :], 128, [oa_.r], [bker])
                    c0_ = per[0][3]
                    CP("act", mixT.t[:, h, c0_:c0_ + 128 * len(per)], bke[:, 0:128 * len(per)], [bker], [mixT.rq[p_[2]] for p_ in per])
                    if has_dec:
                        CP("dve", QM.t[:, 0:256:17], QT_.t[:, 512:528], [QT_.r], [QM.r])
                        bod, bodr = bank(pring_x.next())
                        HB = 8
                        for b0 in range(0, NB, HB):
                            snbs = []
                            for b in range(b0, b0 + HB):
                                s0 = s0_ring.next(); sn = sn_ring.next(); t1 = t1_ring.next(); snb = snb_ring.next()
                                S.dma(s0.t[:], sh[l, b, h], writes=[s0.r])
                                bb, bbr = bank(pring_tp.next())
                                MM(bb[:, 0:128], sel.t[0:16, b * 128:(b + 1) * 128], Vd_.t[0:16, :], True, True, [sel.r, Vd_.r], [bbr])
                                TS("dve", t1.t[:], bb[:, 0:128], KTt.t[:, 512 + b:513 + b], None, ALU.mult, None, [bbr, KTt.r], [t1.r])
                                STT(sn.t[:], s0.t[:], FS_.t[:, 512 + b:513 + b], t1.t[:], ALU.mult, ALU.add, [s0.r, FS_.r, t1.r], [sn.r])
                                S.dma(hs_o[l, b, h], sn.t[:], reads=[sn.r], q=STQ)
                                CP("pool", snb.t[:], sn.t[:], [sn.r], [snb.r])
                                snbs.append((b, snb))
                            for (b, snb) in snbs:
                                MM(bod[0:16, 0:128], QM.t[:, b * 16:(b + 1) * 16], snb.t[:, :], b == 0, b == NB - 1, [QM.r, snb.r], [bodr], sig=True)
                        hgrn_epilog(bod, bodr, NB, 512, 4, h, GS_, oa_ring, epilog_tp)
                    yield

                pend = [None]

                def hcompute(h, wb):
                    chunks = proj_chunks(h, wb, PB[h % 2])
                    rest = pend[0]
                    for (mm, ev) in chunks:
                        mm()
                        if rest is not None:
                            next(rest)
                        ev()
                    pend[0] = rest_parts(h, PB[h % 2])
                run_slabs(8, hq, hcompute, wring, key=("hgrn", l, g))
                for _ in pend[0]:
                    pass
            S.barrier()
            mark("hgrn")
            chk()
            with contextlib.ExitStack() as ss:
                swa(l, g, ss, wring, mixT, epilog_tp)
            S.barrier()
            mark("swa")
            chk()
            with contextlib.ExitStack() as gs:
                gmlp(l, g, gs, wring, mixT, epilog_tp)
            S.barrier()
            if KDBG:
                off = S.off
                S.off = False
                S.barrier()
                S.dma(dbg_mix[g], mixT.t[:], reads=mixT.rq)
                S.barrier()
                S.off = off
            return mixT

        def hgrn_epilog(bo, bor, rows, col0, li, h, GS, oa_ring, epilog_tp):
            st = stat.next()
            ACT(junk.t[0:rows, 0:128], bo[0:rows, 0:128], AF.Square, [bor], [junk.r, st.r], accum=st.t[0:rows, 0:1])
            RSQ(st.t[0:rows, 2:3], st.t[0:rows, 0:1], 1.0 / 128, rows, 1, st.r)
            oa = oa_ring.next()
            STT(oa.t[0:rows, :], bo[0:rows, 0:128], st.t[0:rows, 2:3], GS.t[0:rows, li, :], ALU.mult, ALU.mult, [bor, st.r, GS.rq[li]], [oa.r])
            epilog_tp(oa.t, oa.r, rows, col0, li, h, 1)

        def load_slab(l, wb, c0, ncols):
            for q in range(4):
                stage_cast(w_in[l, q * 512:(q + 1) * 512, c0:c0 + ncols].rearrange("(a p) c -> p a c", p=128),
                           wb.t[:, 4 * q:4 * q + 4, 0:ncols], wb.rq[q], 4, ncols)

        def swa(l, g, ss, wring, mixT, epilog_tp):
            has_dec = (g == NG - 1)
            tiles = group_tiles(g)
            QTp = sbt(ss, "QTp", [128, 8, GW], BF16, nres=5)
            QNP = [sbt(ss, "QNP%d" % i, [128, 8, 128]) for i in range(2)]
            KN = [sbt(ss, "KN%d" % i, [128, 128]) for i in range(2)]
            VR = [sbt(ss, "VR%d" % i, [128, 128]) for i in range(2)]
            tq = sbt(ss, "tq", [128, 8, 64]); tk = sbt(ss, "tk", [128, 128])
            KTd = sbt(ss, "KTd", [128, 16], BF16)
            pt_ring = Ring([sbt(ss, "pt%d" % i, [128, 512], BF16) for i in range(2)])
            pm_ring = Ring([sbt(ss, "pm%d" % i, [128, 4, 128], BF16) for i in range(4)])
            OB = Ring([sbt(ss, "OB%d" % i, [128, 8, 64]) for i in range(2)])
            dd = Ring([sbt(ss, "dd%d" % i, [128, 16]) for i in range(2)])
            for qn in QNP:
                S.op("pool", lambda e, qn=qn: e.memset(qn.t[:], 0.0), writes=[qn.r])
            wa = wring.next(); load_slab(l, wa, 4096, 512)
            wk = wring.next(); load_slab(l, wk, 4608, 256)
            ti = 0
            sprj = {}

            def sproj(ix):
                (gt, row0, rows, col0, li) = tiles[ix]
                bq, bqr = bank(pring_acc.next())
                for kc in range(16):
                    MM(bq[0:rows, 0:512], hT.t[:, kc, col0:col0 + rows], wa.t[:, kc, 0:512], kc == 0, kc == 15, [hT.rq[li], wa.rq[kc // 4]], [bqr])
                bk_, bkr = bank(pring_acc.next())
                for kc in range(16):
                    MM(bk_[0:rows, 0:256], hT.t[:, kc, col0:col0 + rows], wk.t[:, kc, 0:256], kc == 0, kc == 15, [hT.rq[li], wk.rq[kc // 4]], [bkr])
                sprj[ix] = (bq, bqr, bk_, bkr)
            sproj(0)
            for ix_, (gt, row0, rows, col0, li) in enumerate(tiles):
                qn = QNP[ti % 2]; kn = KN[ti % 2]; vr = VR[ti % 2]; ti += 1
                (bq, bqr, bk_, bkr) = sprj.pop(ix_)
                st = stat.next()
                ACT(tq.t[0:rows].rearrange("p a b -> p (a b)"), bq[0:rows, 0:512], AF.Square, [bqr], [tq.r])
                S.op("dve", lambda e: e.tensor_reduce(out=st.t[0:rows, 0:8], in_=tq.t[0:rows], axis=AX.X, op=ALU.add), [tq.r], [st.r])
                ACT(tk.t[0:rows, :], bk_[0:rows, 0:128], AF.Square, [bkr], [tk.r])
                S.op("dve", lambda e: e.tensor_reduce(out=st.t[0:rows, 8:10], in_=tk.t[0:rows, :].rearrange("p (a b) -> p a b", a=2), axis=AX.X, op=ALU.add), [tk.r, st.r], [st.r])
                RSQ(st.t[0:rows, 0:10], st.t[0:rows, 0:10], 1.0 / 64, rows, 10, st.r)
                TT("dve", tq.t[0:rows], bq[0:rows, 0:512].rearrange("p (a b) -> p a b", a=8), st.t[0:rows, 0:8].unsqueeze(2).to_broadcast([rows, 8, 64]), ALU.mult, [bqr, st.r], [tq.r])
                for j in range(2):
                    TT("pool", qn.t[0:rows, 4 * j:4 * j + 4, 64 * j:64 * j + 64], tq.t[0:rows, 4 * j:4 * j + 4, :],
                       qgbc.t[0:rows, :].unsqueeze(1).to_broadcast([rows, 4, 64]), ALU.mult, [tq.r, qgbc.r], [qn.r])
                TT("dve", tk.t[0:rows, :].rearrange("p (a b) -> p a b", a=2), bk_[0:rows, 0:128].rearrange("p (a b) -> p a b", a=2),
                   st.t[0:rows, 8:10].unsqueeze(2).to_broadcast([rows, 2, 64]), ALU.mult, [bkr, st.r], [tk.r])
                TT("pool", kn.t[0:rows, :].rearrange("p (a b) -> p a b", a=2), tk.t[0:rows, :].rearrange("p (a b) -> p a b", a=2),
                   kgbc.t[0:rows, :].unsqueeze(1).to_broadcast([rows, 2, 64]), ALU.mult, [tk.r, kgbc.r], [kn.r])
                CP("act", vr.t[0:rows, :], bk_[0:rows, 128:256], [bkr], [vr.r])
                if li < 4:
                    CP("pool", V1all.t[0:rows, gt, :, 0:64], vr.t[0:rows, :].rearrange("p (a b) -> p a b", a=2), [vr.r], [V1all.rq[gt]])
                if ix_ + 1 < len(tiles):
                    sproj(ix_ + 1)
                transposes_to(lambda i: QTp.t[:, i, col0:col0 + rows], qn.t[:].rearrange("p a b -> p (a b)"), qn.r, rows, 8, QTp.rq[li])
                if li < 4:
                    transposes_to(lambda i: KTall.t[:, gt * 128:gt * 128 + rows], kn.t, kn.r, rows, 1, KTall.rq[gt])
                else:
                    transposes_to(lambda i: KTd.t[:, 0:rows], kn.t, kn.r, rows, 1, KTd.r)
                if gt == 15:
                    S.dma(kp_o[l], kn.t[:], reads=[kn.r], q=STQ)
                    S.dma(vp_o[l], vr.t[:], reads=[vr.r], q=STQ)
                if li < 4:
                    ob = OB.next()
                    for j in range(2):
                        kts = ([gt - 1] if gt > 0 else []) + [gt]
                        pms = []
                        for kt in kts:
                            bs_, bsr = bank(pring_x.next())
                            MM(bs_[:, 0:512].rearrange("p (a b) -> p a b", a=4), KTall.t[:, kt * 128:(kt + 1) * 128], QTp.t[:, 4 * j:4 * j + 4, col0:col0 + 128], True, True,
                               [KTall.rq[kt], QTp.rq[li]], [bsr])
                            pt = pt_ring.next(); pm = pm_ring.next()
                            ACT(pt.t[:], bs_[:, 0:512], AF.Exp, [bsr], [pt.r], scale=0.125)
                            mk = caus if kt == gt else lows
                            TT("dve" if kt == gt else "pool", pm.t[:], pt.t[:].rearrange("p (a b) -> p a b", a=4),
                               mk.t[:].unsqueeze(1).to_broadcast([128, 4, 128]), ALU.mult, [pt.r, mk.r], [pm.r])
                            pms.append((pm, kt))
                        bo, bor = bank(pring_acc.next())
                        for gq in range(4):
                            for ii, (pm, kt) in enumerate(pms):
                                MM(bo[:, gq * 65:(gq + 1) * 65], pm.t[:, gq, :], V1all.t[:, kt, j, :], ii == 0, ii == len(pms) - 1,
                                   [pm.r, V1all.rq[kt]], [bor], sig=(gq == 3 and ii == len(pms) - 1))
                        d_ = dd.next()
                        bov = bo[:, 0:260].rearrange("p (a b) -> p a b", a=4)
                        TT("dve", d_.t[:, 0:4], bov[:, :, 64], esink.t[:, 4 * j:4 * j + 4], ALU.add, [bor, esink.r], [d_.r])
                        S.op("dve", lambda e, d_=d_: e.reciprocal(out=d_.t[:, 4:8], in_=d_.t[:, 0:4]), [d_.r], [d_.r])
                        TT("dve", ob.t[:, 4 * j:4 * j + 4, :], bov[:, :, 0:64], d_.t[:, 4:8].unsqueeze(2).to_broadcast([128, 4, 64]), ALU.mult, [bor, d_.r], [ob.r])
                    epilog_tp(ob.t[:].rearrange("p a b -> p (a b)"), ob.r, 128, col0, li, 8, 4)
                else:
                    swa_dec(l, ss, qn, kn, vr, QTp, KTd, dd, OB, epilog_tp)

        def swa_dec(l, ss, qn, kn, vr, QTp, KTd, dd, OB, epilog_tp):
            CKT = sbt(ss, "CKT", [128, NB, 128], BF16); V1c = sbt(ss, "V1c", [128, NB, 2, 65], BF16)
            ld_ring = Ring([sbt(ss, "ld%d" % i, [128, 128]) for i in range(3)])
            PTd = sbt(ss, "PTd", [128, 128]); PTM = sbt(ss, "PTM", [128, 8, NB * NB], BF16)
            prod = sbt(ss, "prod", [16, 8, 64]); psf = sbt(ss, "psf", [16, 16]); t1 = sbt(ss, "t1d", [16, 8, 64])
            S.op("pool", lambda e: e.memset(V1c.t[:], 1.0), writes=[V1c.r])
            S.op("pool", lambda e: e.memset(PTM.t[:], 0.0), writes=[PTM.r])
            S.dma(ks_o[l, :, 0:127, :], ck[l, :, 1:128, :])
            S.dma(vs_o[l, :, 0:127, :], cv[l, :, 1:128, :])
            S.dma(ks_o[l, :, 127, :], kn.t[0:NB, :], reads=[kn.r], q=STQ)
            S.dma(vs_o[l, :, 127, :], vr.t[0:NB, :], reads=[vr.r], q=STQ)
            for b in range(NB):
                ckt = ld_ring.next()
                S.dma(ckt.t[:], ck[l, b], writes=[ckt.r])
                bk, br = bank(pring_tp.next())
                TP(bk[:, 0:128], ckt.t[:, :], 128, [ckt.r], [br])
                CP("act", CKT.t[:, b, :], bk[:, 0:128], [br], [CKT.r])
                cvt = ld_ring.next()
                S.dma(cvt.t[:], cv[l, b], writes=[cvt.r])
                CP("pool", V1c.t[:, b, :, 0:64], cvt.t[:, :].rearrange("p (a c) -> p a c", a=2), [cvt.r], [V1c.r])
            bsd, bsdr = bank(pring_x.next())
            for b in range(NB):
                for j in range(2):
                    MM(bsd[:, b * 8 + 4 * j:b * 8 + 4 * j + 4], CKT.t[:, b, :], QTp.t[:, 4 * j:4 * j + 4, 512 + b], True, True,
                       [CKT.r, QTp.rq[4]], [bsdr], sig=(b == NB - 1 and j == 1))
            ACT(PTd.t[:], bsd[:, 0:128], AF.Exp, [bsdr], [PTd.r], scale=0.125)
            TS("dve", PTM.t[:, :, 0:256:17], PTd.t[:].rearrange("p (b h) -> p h b", h=8), kmask.t[:, 0:1], None, ALU.mult, None,
               [PTd.r, kmask.r], [PTM.r])
            for j in range(2):
                TT("dve", prod.t[:, 4 * j:4 * j + 4, :], qn.t[0:NB, 4 * j:4 * j + 4, 64 * j:64 * j + 64],
                   kn.t[0:NB, 64 * j:64 * j + 64].unsqueeze(1).to_broadcast([NB, 4, 64]), ALU.mult, [qn.r, kn.r], [prod.r])
            S.op("dve", lambda e: e.tensor_reduce(out=psf.t[:, 0:8], in_=prod.t[:], axis=AX.X, op=ALU.add), [prod.r], [psf.r])
            ACT(psf.t[:, 8:16], psf.t[:, 0:8], AF.Exp, [psf.r], [psf.r], scale=0.125)
            vrv = vr.t[0:NB, :].rearrange("p (a b) -> p a b", a=2)
            ob = OB.next()
            for j in range(2):
                bod, bodr = bank(pring_acc.next())
                for gq in range(4):
                    h = 4 * j + gq
                    for b in range(NB):
                        MM(bod[0:NB, gq * 65:(gq + 1) * 65], PTM.t[:, h, b * NB:(b + 1) * NB], V1c.t[:, b, j, :], b == 0, b == NB - 1,
                           [PTM.r, V1c.r], [bodr], sig=(gq == 3 and b == NB - 1))
                bov = bod[0:NB, 0:260].rearrange("p (a b) -> p a b", a=4)
                TT("dve", t1.t[:, 4 * j:4 * j + 4, :], vrv[:, j:j + 1, :].to_broadcast([NB, 4, 64]),
                   psf.t[:, 8 + 4 * j:12 + 4 * j].unsqueeze(2).to_broadcast([NB, 4, 64]), ALU.mult, [vr.r, psf.r], [t1.r])
                TT("dve", t1.t[:, 4 * j:4 * j + 4, :], t1.t[:, 4 * j:4 * j + 4, :], bov[:, :, 0:64], ALU.add, [t1.r, bodr], [t1.r])
                d_ = dd.next()
                TT("dve", d_.t[0:NB, 0:4], bov[:, :, 64], psf.t[:, 8 + 4 * j:12 + 4 * j], ALU.add, [bodr, psf.r], [d_.r])
                TT("dve", d_.t[0:NB, 0:4], d_.t[0:NB, 0:4], esink.t[0:NB, 4 * j:4 * j + 4], ALU.add, [d_.r, esink.r], [d_.r])
                S.op("dve", lambda e, d_=d_: e.reciprocal(out=d_.t[0:NB, 8:12], in_=d_.t[0:NB, 0:4]), [d_.r], [d_.r])
                TT("dve", ob.t[0:NB, 4 * j:4 * j + 4, :], t1.t[:, 4 * j:4 * j + 4, :], d_.t[0:NB, 8:12].unsqueeze(2).to_broadcast([NB, 4, 64]), ALU.mult, [t1.r, d_.r], [ob.r])
            epilog_tp(ob.t[:].rearrange("p a b -> p (a b)"), ob.r, NB, 512, 4, 8, 4)

        def gmlp(l, g, gs, wring, mixT, epilog_tp):
            tiles = group_tiles(g)
            GU = Ring([sbt(gs, "GU%d" % i, [128, 512]) for i in range(2)])
            GV = Ring([sbt(gs, "GV%d" % i, [128, 512]) for i in range(2)])
            VNb = Ring([sbt(gs, "VNb%d" % i, [128, 512], BF16) for i in range(2)])
            OC = Ring([sbt(gs, "OC%d" % i, [128, 512]) for i in range(2)])
            wu = wring.next(); load_slab(l, wu, 4864, 512)
            wv = wring.next(); load_slab(l, wv, 5376, 512)
            gprj = {}

            def gproj(ix):
                (gt, row0, rows, col0, li) = tiles[ix]
                lst = []
                for wb in (wu, wv):
                    bk, br = bank(pring_acc.next())
                    for kc in range(16):
                        MM(bk[0:rows, 0:512], hT.t[:, kc, col0:col0 + rows], wb.t[:, kc, 0:512], kc == 0, kc == 15, [hT.rq[li], wb.rq[kc // 4]], [br])
                    lst.append((bk, br))
                gprj[ix] = lst
            gproj(0)
            for ix_, (gt, row0, rows, col0, li) in enumerate(tiles):
                outs = []
                for (bk, br), ring in zip(gprj.pop(ix_), (GU, GV)):
                    t = tA.next(); go = ring.next()
                    ACT(t.t[0:rows, :], bk[0:rows, 0:512], AF.Square, [br], [t.r])
                    TS("pool", t.t[0:rows, :], t.t[0:rows, :], 0.044715, 1.0, ALU.mult, ALU.add, [t.r], [t.r])
                    TT("dve", t.t[0:rows, :], t.t[0:rows, :], bk[0:rows, 0:512], ALU.mult, [t.r, br], [t.r])
                    ACT(t.t[0:rows, :], t.t[0:rows, :], AF.Sigmoid, [t.r], [t.r], scale=GELU_C)
                    TT("dve", go.t[0:rows, :], t.t[0:rows, :], bk[0:rows, 0:512], ALU.mult, [t.r, br], [go.r])
                    outs.append(go)
                gu, gv = outs
                if ix_ + 1 < len(tiles):
                    gproj(ix_ + 1)
                st = stat.next()
                S.op("dve", lambda e: e.bn_stats(out=st.t[0:rows, 0:6], in_=gv.t[0:rows, :]), [gv.r], [st.r])
                S.op("dve", lambda e: e.bn_aggr(out=st.t[0:rows, 6:8], in_=st.t[0:rows, 0:6]), [st.r], [st.r])
                RSQ(st.t[0:rows, 9:10], st.t[0:rows, 7:8], 1.0, rows, 1, st.r)
                TS("dve", gv.t[0:rows, :], gv.t[0:rows, :], st.t[0:rows, 6:7], st.t[0:rows, 9:10], ALU.subtract, ALU.mult, [gv.r, st.r], [gv.r])
                TT("pool", gv.t[0:rows, :], gv.t[0:rows, :], lgbc.t[0:rows, :], ALU.mult, [gv.r, lgbc.r], [gv.r])
                TT("pool", gv.t[0:rows, :], gv.t[0:rows, :], lbbc.t[0:rows, :], ALU.add, [gv.r, lbbc.r], [gv.r])
                oc = OC.next()
                if li < 4:
                    vb = VNb.next()
                    CP("act", vb.t[:], gv.t[:], [gv.r], [vb.r])
                    bm, bmr = bank(pring_x.next())
                    for c in range(4):
                        MM(bm[:, c * 128:(c + 1) * 128], WsT.t[:, c, :], vb.t[:, c * 128:(c + 1) * 128], True, True, [WsT.r, vb.r], [bmr], sig=(c == 3))
                    t = tA.next()
                    TT("dve", t.t[:].rearrange("p (a b) -> p a b", a=4), bm[:, 0:512].rearrange("p (a b) -> p a b", a=4),
                       bsT.t[:, :].unsqueeze(2).to_broadcast([128, 4, 128]), ALU.add, [bmr, bsT.r], [t.r])
                    TT("pool", oc.t[:], t.t[:], gu.t[:], ALU.mult, [t.r, gu.r], [oc.r])
                else:
                    S.dma(gv_o[l], gv.t[0:NB, :], reads=[gv.r], q=STQ)
                    t = tA.next()
                    TT("dve", t.t[0:NB].rearrange("p (a b) -> p a b", a=4), gv.t[0:NB].rearrange("p (a b) -> p a b", a=4),
                       w00.t[:, :].unsqueeze(2).to_broadcast([NB, 4, 128]), ALU.mult, [gv.r, w00.r], [t.r])
                    TT("dve", t.t[0:NB].rearrange("p (a b) -> p a b", a=4), t.t[0:NB].rearrange("p (a b) -> p a b", a=4),
                       bs0.t[:, :].unsqueeze(2).to_broadcast([NB, 4, 128]), ALU.add, [t.r, bs0.r], [t.r])
                    TT("pool", oc.t[0:NB], t.t[0:NB], gu.t[0:NB], ALU.mult, [t.r, gu.r], [oc.r])
                epilog_tp(oc.t, oc.r, rows, col0, li, 12, 4)

        def ffn(l, g, st):
            has_dec = (g == NG - 1)
            hid = sbt(st, "hid", [128, NJ, GW], BF16, nres=5)
            with contextlib.ExitStack() as fa:
                wring = Ring([sbt(fa, "wu%d" % i, [128, 2, 16, 128], BF16, nres=2) for i in range(2)])
                U = Ring([sbt(fa, "U%d" % i, [128, 2, 514], F32, nres=2) for i in range(2)])
                tcv = Ring([sbt(fa, "tcv%d" % i, [128, 2, 512], F32, nres=2) for i in range(2)])
                Ud = Ring([sbt(fa, "Ud%d" % i, [128, 2, NB, 3], F32, nres=2) for i in range(2)])
                upc_ring = Ring([sbt(fa, "upc%d" % i, [128, 2, NB]) for i in range(3)])
                upl_ring = Ring([sbt(fa, "upl%d" % i, [128, 2, 2]) for i in range(3)])
                ost_ring = Ring([sbt(fa, "ost%d" % i, [16, 256]) for i in range(4)])
                stt_ = Ring([sbt(fa, "stt%d" % i, [16, 2, 2, 128], F32, nres=2) for i in range(3)])
                if has_dec:
                    S.dma(cs_o[l, :, 0, :], scv[l, :, 1, :])
                fq = fq_gen(l)

                tails = []
                sxm = {}
                udm = {}

                def st_dma(j2):
                    if has_dec and j2 < NJ:
                        sx = stt_.next()
                        for gv in range(2):
                            S.dma(sx.t[:, :, gv, :], scv[l, :, :, gv * DFF + j2 * 128:gv * DFF + (j2 + 1) * 128], writes=[sx.rq[gv]])
                        sxm[j2] = sx

                def st_prep(j2):
                    if has_dec and j2 < NJ:
                        sx = sxm.pop(j2)
                        ud = Ud.next()
                        bt, btr = bank(pring_tp.next())
                        for r in range(2):
                            for gv in range(2):
                                TP(bt[:, (r * 2 + gv) * 16:(r * 2 + gv) * 16 + 16], sx.t[0:16, r, gv, :], 16, [sx.rq[gv]], [btr], sig=(r == 1 and gv == 1))
                        for gv in range(2):
                            for r in range(2):
                                CP("dve", ud.t[:, gv, :, r], bt[:, (r * 2 + gv) * 16:(r * 2 + gv) * 16 + 16], [btr], [ud.rq[gv]])
                        udm[j2] = ud

                def fcompute(j, wb):
                    if j == 0:
                        st_dma(0); st_dma(1); st_prep(0)
                    allb = []
                    for (c0, n, lis) in moving(g):
                        bks = []
                        if n != 512:
                            bkd, bkdr = bank(pring_x.next())
                        for gv in range(2):
                            if n == 512:
                                bk, br = bank(pring_acc.next())
                            else:
                                bk, br = bkd[:, gv * 32:gv * 32 + 32], bkdr
                            for kc in range(16):
                                MM(bk[:, 0:n], wb.t[:, gv, kc, :], hT.t[:, kc, c0:c0 + n], kc == 0, kc == 15,
                                   [wb.rq[gv]] + [hT.rq[i] for i in lis], [br])
                            bks.append((bk, br))
                        allb.append((n, bks))
                    for t_ in tails:
                        t_()
                    del tails[:]
                    for (n, bks) in allb:
                        if n == 512:
                            u = U.next(); tc_ = tcv.next()
                            for gv in range(2):
                                jj = gv * NJ + j
                                bk, br = bks[gv]
                                if g == 0:
                                    S.op("pool", lambda e, u=u, gv=gv: e.memset(u.t[:, gv, 0:2], 0.0), writes=[u.rq[gv]])
                                else:
                                    CP("pool", u.t[:, gv, 0:2], carry.t[:, jj, :], [carry.rq[jj]], [u.rq[gv]])
                                CP("act", u.t[:, gv, 2:514], bk[:, 0:512], [br], [u.rq[gv]])
                                CP("pool", carry.t[:, jj, :], u.t[:, gv, 512:514], [u.rq[gv]], [carry.rq[jj]])
                                if g == NG - 1:
                                    if gv == 0:
                                        upl = upl_ring.next()
                                    CP("pool", upl.t[:, gv, :], u.t[:, gv, 512:514], [u.rq[gv]], [upl.r])
                                    if gv == 1:
                                        def tail_p(upl=upl, j=j):
                                            bt3, bt3r = bank(pring_tp.next())
                                            for g2 in range(2):
                                                TP(bt3[0:2, g2 * 128:(g2 + 1) * 128], upl.t[:, g2, :], 128, [upl.r], [bt3r], sig=(g2 == 1))
                                            ost = ost_ring.next()
                                            CP("dve", ost.t[0:2, :], bt3[0:2, 0:256], [bt3r], [ost.r])
                                            S.dma(cp_o[l].rearrange("r (gg f) -> r gg f", gg=2)[:, :, j * 128:(j + 1) * 128],
                                                  ost.t[0:2, :].rearrange("r (gg f) -> r gg f", gg=2), reads=[ost.r], q=STQ)
                                        tails.append(tail_p)
                                TS("dve", tc_.t[:, gv, :], u.t[:, gv, 0:512], cw.t[:, 0, jj:jj + 1], cb.t[:, jj:jj + 1], ALU.mult, ALU.add, [u.rq[gv], cw.r, cb.r], [tc_.rq[gv]])
                                STT(tc_.t[:, gv, :], u.t[:, gv, 1:513], cw.t[:, 1, jj:jj + 1], tc_.t[:, gv, :], ALU.mult, ALU.add, [u.rq[gv], cw.r, tc_.rq[gv]], [tc_.rq[gv]])
                                STT(tc_.t[:, gv, :], u.t[:, gv, 2:514], cw.t[:, 2, jj:jj + 1], tc_.t[:, gv, :], ALU.mult, ALU.add, [u.rq[gv], cw.r, tc_.rq[gv]], [tc_.rq[gv]])
                            ACT(tc_.t[:, 0, :], tc_.t[:, 0, :], AF.Silu, [tc_.rq[0]], [tc_.rq[0]])
                            TT("pool", hid.t[:, j, 0:512], tc_.t[:, 0, :], tc_.t[:, 1, :], ALU.mult, tc_.rq, hid.rq[0:4])
                        else:
                            ud = udm.pop(j); upc = upc_ring.next()
                            tc_ = tcv.next()
                            for gv in range(2):
                                jj = gv * NJ + j
                                bk, br = bks[gv]
                                CP("act", ud.t[:, gv, :, 2], bk[:, 0:NB], [br], [ud.rq[gv]])
                                CP("act", upc.t[:, gv, :], bk[:, 0:NB], [br], [upc.r])
                                TS("dve", tc_.t[:, gv, 0:NB], ud.t[:, gv, :, 0], cw.t[:, 0, jj:jj + 1], cb.t[:, jj:jj + 1], ALU.mult, ALU.add, [ud.rq[gv], cw.r, cb.r], [tc_.rq[gv]])
                                STT(tc_.t[:, gv, 0:NB], ud.t[:, gv, :, 1], cw.t[:, 1, jj:jj + 1], tc_.t[:, gv, 0:NB], ALU.mult, ALU.add, [ud.rq[gv], cw.r, tc_.rq[gv]], [tc_.rq[gv]])
                                STT(tc_.t[:, gv, 0:NB], ud.t[:, gv, :, 2], cw.t[:, 2, jj:jj + 1], tc_.t[:, gv, 0:NB], ALU.mult, ALU.add, [ud.rq[gv], cw.r, tc_.rq[gv]], [tc_.rq[gv]])

                            def tail_d(upc=upc, j=j):
                                bt2, bt2r = bank(pring_tp.next())
                                for gv in range(2):
                                    TP(bt2[0:NB, gv * 128:(gv + 1) * 128], upc.t[:, gv, :], 128, [upc.r], [bt2r], sig=(gv == 1))
                                ost = ost_ring.next()
                                CP("dve", ost.t[0:NB, :], bt2[0:NB, 0:256], [bt2r], [ost.r])
                                S.dma(cs_o[l, :, 1, :].rearrange("b (gg f) -> b gg f", gg=2)[:, :, j * 128:(j + 1) * 128],
                                      ost.t[0:NB, :].rearrange("b (gg f) -> b gg f", gg=2), reads=[ost.r], q=STQ)
                            tails.append(tail_d)
                            ACT(tc_.t[:, 0, 0:NB], tc_.t[:, 0, 0:NB], AF.Silu, [tc_.rq[0]], [tc_.rq[0]])
                            TT("pool", hid.t[:, j, 512:528], tc_.t[:, 0, 0:NB], tc_.t[:, 1, 0:NB], ALU.mult, tc_.rq, [hid.rq[4]])
                    st_dma(j + 2)
                    st_prep(j + 1)
                run_slabs(NJ, fq, fcompute, wring, key=("ffnA", l, g), nxt=(("ffnB", l, g), rq_gen(w_down[l], NJ, 256, 8)(0)))
                for t_ in tails:
                    t_()
                del tails[:]
            S.barrier()
            mark("ffnA")
            with contextlib.ExitStack() as fb:
                resid_update(l, g, False, w_down[l], NJ, hid, fb, ncols=256, qa=8, key=("ffnB", l, g),
                             nxt=(("ple", l, g), rq_gen(w_pg[l], 16, 512, 4, lambda ds: [wq(w_pp[l, :, ds * 512:(ds + 1) * 512], 2, 512, None, None)])(0)))
                S.barrier()

        def ple(l, g, st):
            pT = sbt(st, "pT", [128, 2, GW], BF16, nres=5)
            pl = Ring([sbt(st, "pl%d" % i, [128, PLE]) for i in range(2)])
            for (gt, row0, rows, col0, li) in group_tiles(g):
                p_ = pl.next()
                src = pp[l, row0:row0 + rows, :] if gt < 16 else psm[l]
                S.dma(p_.t[0:rows, :], src, writes=[p_.r])
                transposes_to(lambda i: pT.t[:, i, col0:col0 + rows], p_.t, p_.r, rows, 2, pT.rq[li])

            wps = [sbt(st, "wpb%d" % i, [128, 2, 512], BF16) for i in range(2)]

            def xq(ds):
                wp = wps[ds % 2]
                return [wq(w_pp[l, :, ds * 512:(ds + 1) * 512], 2, 512, (lambda wb, wp=wp: wp.t[:, :, :]), (lambda wb, wp=wp: wp.r))]

            def pre(ds, bk, br, tl, xo):
                wp = wps[ds % 2]
                (gt, row0, rows, col0, li) = tl
                bp, bpr = bank(pring_x.next())
                for a in range(2):
                    MM(bp[0:rows, 0:512], pT.t[:, a, col0:col0 + rows], wp.t[:, a, :], a == 0, a == 1, [pT.rq[li], wp.r], [bpr])
                t = tA.next()
                ACT(t.t[0:rows, :], bk[0:rows, 0:512], AF.Sigmoid, [br], [t.r])
                TT("dve", t.t[0:rows, :], t.t[0:rows, :], bp[0:rows, 0:512], ALU.mult, [t.r, bpr], [t.r])
                TT("dve", xo.t[0:rows, :], xo.t[0:rows, :], t.t[0:rows, :], ALU.add, [xo.r, t.r], [xo.r])
            nl, ng = (l, g + 1) if g + 1 < NG else (l + 1, 0)
            nx = (("hgrn", nl, ng), hq_gen(nl)(0)) if nl < DEPTH else None
            resid_update(l, g, l == DEPTH - 1, w_pg[l], 16, hT, st, ncols=512, extra=(xq, pre), key=("ple", l, g), nxt=nx)

        def load_params(l):
            for i, gsrc in enumerate((norm1_g, norm2_g, ple_g)):
                S.dma(gam.t[:, i, :], gsrc[l].rearrange("(a p) -> p a", p=128), writes=[gam.r], allow_slow_non_contiguous=True)
            S.dma(hgbc.t[:], hgn[l].partition_broadcast(128), writes=[hgbc.r])
            S.dma(qgbc.t[:], qng[l].partition_broadcast(128), writes=[qgbc.r])
            S.dma(kgbc.t[:], kng[l].partition_broadcast(128), writes=[kgbc.r])
            S.dma(esink.t[:], sinks[l].partition_broadcast(128), writes=[esink.r])
            ACT(esink.t[:], esink.t[:], AF.Exp, [esink.r], [esink.r])
            S.dma(lgbc.t[:], lng[l].partition_broadcast(128), writes=[lgbc.r])
            S.dma(lbbc.t[:], lnb[l].partition_broadcast(128), writes=[lbbc.r])
            S.dma(bsT.t[:], gbs[l].rearrange("c p -> p c"), writes=[bsT.r], allow_slow_non_contiguous=True)
            S.dma(w00.t[:], gws[l, :, 0, 0].partition_broadcast(16), writes=[w00.r], allow_slow_non_contiguous=True)
            S.dma(bs0.t[:], gbs[l, :, 0].partition_broadcast(16), writes=[bs0.r], allow_slow_non_contiguous=True)
            for r in range(3):
                S.dma(cw.t[:, r, :], conv_w[l, r].rearrange("(j p) -> p j", p=128), writes=[cw.r], allow_slow_non_contiguous=True)
            S.dma(cb.t[:], conv_b[l].rearrange("(j p) -> p j", p=128), writes=[cb.r], allow_slow_non_contiguous=True)
            for c in range(4):
                t = tA.next()
                S.dma(t.t[:, 0:128], gws[l, c], writes=[t.r])
                TT("dve", t.t[:, 0:128], t.t[:, 0:128], tril.t[:], ALU.mult, [t.r, tril.r], [t.r])
                transposes_to(lambda i: WsT.t[:, c, :], t.t, t.r, 128, 1, WsT.r)

        try:
            for l in range(DEPTH):
                chk()
                load_params(l)
                mark("params")
                for g in range(NG):
                    chk()
                    norm_pass(g, 0)
                    mark("norm0")
                    chk()
                    with contextlib.ExitStack() as st:
                        mixT = mixers(l, g, st)
                        S.barrier()
                        mark("gmlp")
                        chk()
                        with contextlib.ExitStack() as st2:
                            resid_update(l, g, False, w_out[l], 16, mixT, st2, key=("wout", l, g), nxt=(("ffnA", l, g), fq_gen(l)(0)))
                            S.barrier()
                            mark("wout")
                    S.barrier()
                    chk()
                    norm_pass(g, 1)
                    mark("norm1")
                    chk()
                    with contextlib.ExitStack() as st:
                        ffn(l, g, st)
                    S.barrier()
                    mark("ffnB")
                    chk()
                    norm_pass(g, 2)
                    mark("norm2")
                    chk()
                    with contextlib.ExitStack() as st:
                        ple(l, g, st)
                        S.barrier()
                    mark("ple")
                S.barrier()
        except _Stop:
            pass
        S.off = False
        S.barrier()
        if KDBG:
            S.dma(dbg_x, xbuf)
            S.barrier()
    return nc


_CACHE = {}
MARKS = []


def _consts():
    i = np.arange(128)
    caus = (i[:, None] <= i[None, :]).astype(np.float32)
    tril = (i[None, :] <= i[:, None]).astype(np.float32)
    lows = (i[:, None] > i[None, :]).astype(np.float32)
    i64 = np.arange(64)
    c64 = (i64[:, None] <= i64[None, :]).astype(np.float32)
    m2 = np.concatenate([np.ones((64, 64), np.float32), c64], axis=0)
    kmask = np.ones((128, 1), np.float32); kmask[0, 0] = 0.0
    mask01 = np.ones((128, GW), np.float32)
    mask01[:, 0:512:128] = 0.0
    mask01[:, 512:] = 0.0
    sel = np.zeros((16, 16, 128), np.float32)
    for b in range(16):
        sel[b, b, :] = 1.0
    return {"c_ident": np.eye(128, dtype=np.float32), "c_caus": caus, "c_tril": tril, "c_lows": lows, "c_m2": m2,
            "c_kmask": kmask, "c_mask01": mask01, "c_sel": sel.reshape(16, 2048)}


def kernel(**inp):
    if "nc" not in _CACHE:
        _CACHE["nc"] = build_program()
    nc = _CACHE["nc"]
    f = lambda a: np.ascontiguousarray(np.asarray(a, dtype=np.float32))
    wnames = ["norm1_g", "w_in", "hgrn_lb_logits", "hgrn_norm_g", "q_norm_g", "k_norm_g", "swa_sinks", "gmlp_ln_g", "gmlp_ln_b",
              "gmlp_ws", "gmlp_bs", "w_out", "norm2_g", "w_up", "conv_w", "conv_b", "w_down", "ple_norm_g", "w_ple_gate", "w_ple_proj"]
    shared = {k: f(inp[k]) for k in wnames}
    shared.update(_consts())
    x_prompt = f(inp["x_prompt"]); x_sample = f(inp["x_sample"])
    st_h = f(inp["state_hgrn"]); c_k = f(inp["cache_swa_k"]); c_v = f(inp["cache_swa_v"])
    st_c = f(inp["state_ffn_conv"]); p_p = f(inp["p_prompt"]); p_s = f(inp["p_sample"])
    in_maps = []
    for c in range(NCORES):
        sl = slice(c * NB, (c + 1) * NB)
        m = dict(shared)
        m["xp"] = x_prompt[c]
        m["xs"] = x_sample[sl, 0]
        m["sh"] = np.ascontiguousarray(st_h[:, sl])
        m["ck"] = np.ascontiguousarray(c_k[:, sl].reshape(DEPTH, NB, 128, 128))
        m["cv"] = np.ascontiguousarray(c_v[:, sl].reshape(DEPTH, NB, 128, 128))
        m["scv"] = np.ascontiguousarray(st_c[:, sl])
        m["pp"] = np.ascontiguousarray(p_p[:, c])
        m["psm"] = np.ascontiguousarray(p_s[:, sl, 0])
        in_maps.append(m)
    res = run_bass_kernel_spmd(nc, in_maps, core_ids=list(range(NCORES)))
    R = res.results
    cat = lambda k, ax: np.concatenate([np.asarray(R[c][k]) for c in range(NCORES)], axis=ax)
    stk = lambda k, ax: np.stack([np.asarray(R[c][k]) for c in range(NCORES)], axis=ax)
    y_prompt = stk("y_p", 0)
    y_sample = cat("y_s", 0).reshape(NCORES * NB, 1, D)
    hgrn_p = stk("hp_o", 1)
    hgrn_s = cat("hs_o", 1)
    kp = stk("kp_o", 1).reshape(DEPTH, NCORES, 128, 2, 64)
    vp = stk("vp_o", 1).reshape(DEPTH, NCORES, 128, 2, 64)
    ks = cat("ks_o", 1).reshape(DEPTH, NCORES * NB, 128, 2, 64)
    vs = cat("vs_o", 1).reshape(DEPTH, NCORES * NB, 128, 2, 64)
    gv = cat("gv_o", 1).reshape(DEPTH, NCORES * NB, 1, 4, 128)
    cp = stk("cp_o", 1)
    cs = cat("cs_o", 1)
    return tuple(np.ascontiguousarray(a, dtype=np.float32) for a in
                 (y_prompt, y_sample, hgrn_p, hgrn_s, kp, vp, ks, vs, gv, cp, cs))
```

```python
import contextlib
import numpy as np
import concourse.bass as bass
import concourse.mybir as mybir
from concourse.bass_utils import run_bass_kernel_spmd

F32 = mybir.dt.float32
BF16 = mybir.dt.bfloat16
AF = mybir.ActivationFunctionType
ALU = mybir.AluOpType
AX = mybir.AxisListType

NCORES = 8
D = 2048
T = 2048
NB = 16
DEPTH = 2
DIN = 5888
DFF = 5632
NJ = 44
PLE = 256
EPS = 1e-6
NG = 4
GW = 528
GELU_C = 1.5957691216057308


class Res:
    __slots__ = ("w", "rs")

    def __init__(self):
        self.w = None
        self.rs = {}


class Sched:
    ENG = ("pe", "act", "dve", "pool", "sp")

    def __init__(self, nc, st, ndsem=32):
        self.nc = nc
        self.engs = {"pe": nc.tensor, "act": nc.scalar, "dve": nc.vector, "pool": nc.gpsimd, "sp": nc.sync}
        self.sems = {}
        for e in self.ENG:
            self.sems[e] = st.enter_context(nc.semaphore("s_" + e))
        for i in range(ndsem):
            self.sems[("d", i)] = st.enter_context(nc.semaphore("s_d%d" % i))
        self.cnt = {e: 0 for e in self.ENG}
        self.seen = {e: {} for e in self.ENG}
        self.pend_r = []
        self.pend_w = []
        self.nd = ndsem
        self.duse = [0] * ndsem
        self.di = 0
        self.nins = 0
        self.npe = 0
        self.off = False

    def _wait(self, e, k, v):
        self.engs[e].wait_ge(self.sems[k], v)

    def _need(self, e, ev, waits):
        if ev is None:
            return
        k, v = ev
        if self.seen[e].get(k, 0) >= v:
            return
        if waits.get(k, 0) < v:
            waits[k] = v

    def _deps(self, e, reads, writes):
        waits = {}
        for r in reads:
            self._need(e, r.w, waits)
        for w in writes:
            if w.w is not None and w.w[0] != e:
                self._need(e, w.w, waits)
            for k, v in w.rs.items():
                if k != e:
                    self._need(e, (k, v), waits)
        for k, v in waits.items():
            self.seen[e][k] = v
            self._wait(e, k, v)

    def _commit(self, ev, reads, writes):
        for r in reads:
            if r.rs.get(ev[0], 0) < ev[1]:
                r.rs[ev[0]] = ev[1]
        for w in writes:
            w.w = ev
            w.rs = {}

    def op(self, e, fn, reads=(), writes=(), sig=True):
        if self.off:
            return
        reads = list(reads)
        writes = list(writes)
        if e != "pe":
            assert not self.pend_r and not self.pend_w, "PE group left open"
        self._deps(e, reads, writes)
        self.nins += 1
        if e == "pe":
            self.npe += 1
        ins = fn(self.engs[e])
        if sig:
            self.cnt[e] += 1
            ins.then_inc(self.sems[e], 1)
            ev = (e, self.cnt[e])
            if e == "pe":
                reads = reads + self.pend_r
                writes = writes + self.pend_w
                self.pend_r = []
                self.pend_w = []
            self._commit(ev, reads, writes)
        else:
            assert e == "pe"
            self.pend_r += reads
            self.pend_w += writes

    def dma(self, out, in_, reads=(), writes=(), q="sp", **kw):
        if self.off:
            return
        reads = list(reads)
        writes = list(writes)
        assert not self.pend_r and not self.pend_w
        i = self.di % self.nd
        self.di += 1
        k = ("d", i)
        prev = self.duse[i]
        self.duse[i] += 1
        self._deps(q, reads, writes)
        if prev > 0 and self.seen[q].get(k, 0) < 16 * prev:
            self.seen[q][k] = 16 * prev
            self._wait(q, k, 16 * prev)
        self.nins += 1
        self.engs[q].dma_start(out=out, in_=in_, **kw).then_inc(self.sems[k], 16)
        self._commit((k, 16 * self.duse[i]), reads, writes)

    def barrier(self):
        if self.off:
            return
        assert not self.pend_r and not self.pend_w
        for e in self.ENG:
            for o in self.ENG:
                if o != e and self.cnt[o] > self.seen[e].get(o, 0):
                    self.seen[e][o] = self.cnt[o]
                    self._wait(e, o, self.cnt[o])
            for i in range(self.nd):
                k = ("d", i)
                v = 16 * self.duse[i]
                if v > self.seen[e].get(k, 0):
                    self.seen[e][k] = v
                    self._wait(e, k, v)


class Buf:
    def __init__(self, t, nres=1):
        self.t = t
        self.r = Res()
        self.rq = [Res() for _ in range(nres)]


class Ring:
    def __init__(self, bufs):
        self.bufs = bufs
        self.i = 0

    def next(self):
        b = self.bufs[self.i % len(self.bufs)]
        self.i += 1
        return b


def build_program():
    nc = bass.Bass("TRN2", target_bir_lowering=False)

    def din(name, shape):
        return nc.dram_tensor(name, list(shape), F32, kind="ExternalInput").ap()

    def dout(name, shape):
        return nc.dram_tensor(name, list(shape), F32, kind="ExternalOutput").ap()

    xp = din("xp", [T, D]); xs = din("xs", [NB, D])
    sh = din("sh", [DEPTH, NB, 8, 128, 128])
    ck = din("ck", [DEPTH, NB, 128, 128]); cv = din("cv", [DEPTH, NB, 128, 128])
    scv = din("scv", [DEPTH, NB, 2, 2 * DFF])
    pp = din("pp", [DEPTH, T, PLE]); psm = din("psm", [DEPTH, NB, PLE])
    norm1_g = din("norm1_g", [DEPTH, D]); w_in = din("w_in", [DEPTH, D, DIN])
    lbl = din("hgrn_lb_logits", [DEPTH, 1024]); hgn = din("hgrn_norm_g", [DEPTH, 128])
    qng = din("q_norm_g", [DEPTH, 64]); kng = din("k_norm_g", [DEPTH, 64])
    sinks = din("swa_sinks", [DEPTH, 8])
    lng = din("gmlp_ln_g", [DEPTH, 512]); lnb = din("gmlp_ln_b", [DEPTH, 512])
    gws = din("gmlp_ws", [DEPTH, 4, 128, 128]); gbs = din("gmlp_bs", [DEPTH, 4, 128])
    w_out = din("w_out", [DEPTH, D, D]); norm2_g = din("norm2_g", [DEPTH, D])
    w_up = din("w_up", [DEPTH, D, 2 * DFF]); conv_w = din("conv_w", [DEPTH, 3, 2 * DFF])
    conv_b = din("conv_b", [DEPTH, 2 * DFF]); w_down = din("w_down", [DEPTH, DFF, D])
    ple_g = din("ple_norm_g", [DEPTH, D]); w_pg = din("w_ple_gate", [DEPTH, D, D])
    w_pp = din("w_ple_proj", [DEPTH, PLE, D])
    c_ident = din("c_ident", [128, 128]); c_caus = din("c_caus", [128, 128])
    c_tril = din("c_tril", [128, 128]); c_lows = din("c_lows", [128, 128])
    c_m2 = din("c_m2", [128, 64]); c_kmask = din("c_kmask", [128, 1])
    c_mask01 = din("c_mask01", [128, GW]); c_sel = din("c_sel", [16, 16 * 128])

    y_p = dout("y_p", [T, D]); y_s = dout("y_s", [NB, D])
    hp_o = dout("hp_o", [DEPTH, 8, 128, 128]); hs_o = dout("hs_o", [DEPTH, NB, 8, 128, 128])
    kp_o = dout("kp_o", [DEPTH, 128, 128]); vp_o = dout("vp_o", [DEPTH, 128, 128])
    ks_o = dout("ks_o", [DEPTH, NB, 128, 128]); vs_o = dout("vs_o", [DEPTH, NB, 128, 128])
    gv_o = dout("gv_o", [DEPTH, NB, 512])
    cp_o = dout("cp_o", [DEPTH, 2, 2 * DFF]); cs_o = dout("cs_o", [DEPTH, NB, 2, 2 * DFF])
    xbuf = nc.dram_tensor("xbuf", [T + NB, D], F32, kind="Internal").ap()
    import os
    KDBG = False
    if KDBG:
        dbg_x = nc.dram_tensor("dbg_x", [T + NB, D], F32, kind="ExternalOutput").ap()
        dbg_mix = nc.dram_tensor("dbg_mix", [NG, 128, 16, GW], BF16, kind="ExternalOutput").ap()

    with contextlib.ExitStack() as top:
        S = Sched(nc, top)

        uid = [0]

        def sbt(st, name, shape, dt=F32, nres=1):
            uid[0] += 1
            return Buf(st.enter_context(nc.sbuf_tensor("%s_%d" % (name, uid[0]), list(shape), dt)), nres)

        import os
        lim = 1000000
        cnt_ = [0]

        class _Stop(Exception):
            pass

        def mark(label):
            MARKS.append((label, S.npe))

        def chk():
            cnt_[0] += 1
            if cnt_[0] > lim:
                S.off = True

        def ACT(out, in_, func, reads, writes, scale=1.0, bias=None, accum=None):
            kw = {}
            if bias is not None:
                kw["bias"] = bias
            if accum is not None:
                kw["accum_out"] = accum
            S.op("act", lambda e: e.activation(out=out, in_=in_, func=func, scale=scale, **kw), reads, writes)

        def TS(eng, out, in0, s1, s2, op0, op1, reads, writes):
            if op1 is None:
                S.op(eng, lambda e: e.tensor_scalar(out=out, in0=in0, scalar1=s1, scalar2=None, op0=op0), reads, writes)
            else:
                S.op(eng, lambda e: e.tensor_scalar(out=out, in0=in0, scalar1=s1, scalar2=s2, op0=op0, op1=op1), reads, writes)

        def TT(eng, out, in0, in1, op, reads, writes):
            S.op(eng, lambda e: e.tensor_tensor(out=out, in0=in0, in1=in1, op=op), reads, writes)

        def STT(out, in0, scalar, in1, op0, op1, reads, writes):
            S.op("dve", lambda e: e.scalar_tensor_tensor(out=out, in0=in0, scalar=scalar, in1=in1, op0=op0, op1=op1), reads, writes)

        def CP(eng, out, in_, reads, writes):
            if eng == "act":
                ACT(out, in_, AF.Copy, reads, writes)
            else:
                S.op(eng, lambda e: e.tensor_copy(out=out, in_=in_), reads, writes)

        def RSQ(out, in_, scale, rows, cols, res):
            TS("pool", out, in_, scale, EPS, ALU.mult, ALU.add, [res], [res])
            TT("pool", out, out, mhalf.t[0:rows, 0:cols], ALU.pow, [res, mhalf.r], [res])

        def MM(out, lhsT, rhs, start, stop, reads, writes, sig=None):
            S.op("pe", lambda e: e.matmul(out=out, lhsT=lhsT, rhs=rhs, start=start, stop=stop), reads, writes,
                 sig=(stop if sig is None else sig))

        def TP(out, in_, rows, reads, writes, sig=True):
            S.op("pe", lambda e: e.transpose(out=out, in_=in_, identity=ident.t[0:rows, 0:rows]), reads + [ident.r], writes, sig=sig)

        ident = sbt(top, "ident", [128, 128])
        caus = sbt(top, "caus", [128, 128]); tril = sbt(top, "tril", [128, 128]); lows = sbt(top, "lows", [128, 128])
        m2 = sbt(top, "m2", [128, 64]); kmask = sbt(top, "kmask", [128, 1])
        mask01 = sbt(top, "mask01", [128, GW])
        hT = sbt(top, "hT", [128, 16, GW], BF16, nres=5)
        Sf = sbt(top, "Sf", [128, 8, 128], F32, nres=8); Sb = sbt(top, "Sb", [128, 8, 128], BF16, nres=8)
        KTall = sbt(top, "KTall", [128, T], BF16, nres=16)
        V1all = sbt(top, "V1all", [128, 16, 2, 65], BF16, nres=16)
        carry = sbt(top, "carry", [128, 2 * NJ, 2], F32, nres=2 * NJ)
        gam = sbt(top, "gam", [128, 3, 16]); lbt = sbt(top, "lbt", [128, 2, 8]); oml = sbt(top, "oml", [128, 2, 8])
        lgt = sbt(top, "lgt", [128, 2, 8])
        hgbc = sbt(top, "hgbc", [128, 128]); qgbc = sbt(top, "qgbc", [128, 64]); kgbc = sbt(top, "kgbc", [128, 64])
        esink = sbt(top, "esink", [128, 8]); lgbc = sbt(top, "lgbc", [128, 512]); lbbc = sbt(top, "lbbc", [128, 512])
        WsT = sbt(top, "WsT", [128, 4, 128], BF16); bsT = sbt(top, "bsT", [128, 4])
        w00 = sbt(top, "w00", [16, 4]); bs0 = sbt(top, "bs0", [16, 4])
        cw = sbt(top, "cw", [128, 3, 2 * NJ]); cb = sbt(top, "cb", [128, 2 * NJ])
        mhalf = sbt(top, "mhalf", [128, 16])
        stat = Ring([sbt(top, "stat%d" % i, [128, 16]) for i in range(4)])
        junk = sbt(top, "junk", [128, 128], BF16)
        stg_bufs = [sbt(top, "stg%d" % i, [128, 2048], F32, nres=4) for i in range(6)]
        for b_ in stg_bufs:
            b_.pending = False
        stg_ring = Ring(stg_bufs)
        cast_i = [0]

        def stg_take():
            sg = stg_ring.next()
            assert not sg.pending, "staging ring overrun"
            sg.pending = True
            return sg

        def cast_eng():
            cast_i[0] += 1
            return "act" if cast_i[0] % 2 else "dve"
        xo_ring = Ring([sbt(top, "xo%d" % i, [128, 512]) for i in range(5)])
        STQ = "pool"
        tA = Ring([sbt(top, "tA%d" % i, [128, 512]) for i in range(2)])
        pbanks = []
        for i in range(4):
            p = top.enter_context(nc.psum_tensor("ps%d" % i, [128, 1024], F32))
            for hh in range(2):
                pbanks.append((p, hh * 512, Res()))

        def bank(i):
            p, off, r = pbanks[i]
            return p[:, off:off + 512], r

        pring_acc = Ring([0, 1, 2, 3])
        pring_tp = Ring([4, 5])
        pring_x = Ring([6, 7])

        xb_r = [[Res() for _ in range(8)] for _ in range(17)]

        def cdma(buf, src, **kw):
            S.dma(buf.t[:], src, writes=[buf.r], **kw)

        S.op("pool", lambda e: e.memset(mhalf.t[:], -0.5), writes=[mhalf.r])
        cdma(ident, c_ident); cdma(caus, c_caus); cdma(tril, c_tril); cdma(lows, c_lows)
        cdma(m2, c_m2); cdma(kmask, c_kmask); cdma(mask01, c_mask01)
        for t in range(16):
            S.dma(xbuf[t * 128:(t + 1) * 128, :], xp[t * 128:(t + 1) * 128, :], writes=xb_r[t])
        S.dma(xbuf[T:T + NB, :], xs, writes=xb_r[16])
        S.dma(lgt.t[:], lbl.rearrange("l (h p) -> p l h", p=128), writes=[lgt.r], allow_slow_non_contiguous=True)
        S.op("dve", lambda e: e.memset(lbt.t[:], 0.0), writes=[lbt.r])
        TT("dve", lgt.t[:, 1, :], lgt.t[:, 1, :], lgt.t[:, 0, :], ALU.subtract, [lgt.r], [lgt.r])
        ACT(lbt.t[:, 1, :], lgt.t[:, 1, :], AF.Sigmoid, [lgt.r, lbt.r], [lbt.r])
        TS("dve", oml.t[:], lbt.t[:], -1.0, 1.0, ALU.mult, ALU.add, [lbt.r], [oml.r])
        S.op("dve", lambda e: e.memset(V1all.t[:], 1.0), writes=V1all.rq)

        def group_tiles(g):
            tl = [(g * 4 + i, (g * 4 + i) * 128, 128, i * 128, i) for i in range(4)]
            if g == NG - 1:
                tl.append((16, T, NB, 512, 4))
            return tl

        def moving(g):
            mv = [(0, 512, [0, 1, 2, 3])]
            if g == NG - 1:
                mv.append((512, NB, [4]))
            return mv

        def stage_cast(src, dst, dres, a, c):
            sg = stg_take()
            view = sg.t[:, 0:a * c].rearrange("p (a c) -> p a c", a=a)
            S.dma(view, src, writes=[sg.rq[0]])
            CP(cast_eng(), dst, view, [sg.rq[0]], [dres])
            sg.pending = False

        def wq(w_rows, a, nc_, dst, res):
            return {"view": lambda sg: sg.t[:, 0:a * nc_].rearrange("p (a c) -> p a c", a=a),
                    "dmas": lambda view: [(view, w_rows.rearrange("(a p) c -> p a c", p=128))],
                    "dst": dst, "res": res}

        def rq_gen(w_ap, kcn, ncols, qa, extra_q=None):
            nq_ = (kcn + qa - 1) // qa

            def quarters_(ds):
                ql = []
                for q in range(nq_):
                    a = min(qa, kcn - qa * q)
                    ql.append(wq(w_ap[q * qa * 128:(q * qa + a) * 128, ds * ncols:(ds + 1) * ncols], a, ncols,
                                 (lambda wb, q=q, a=a: wb.t[:, qa * q:qa * q + a, :]), (lambda wb, q=q: wb.rq[q])))
                if extra_q is not None:
                    ql += extra_q(ds)
                return ql
            return quarters_

        def fq_gen(l):
            def fq_(j):
                return [wq(w_up[l, :, gv * DFF + j * 128:gv * DFF + (j + 1) * 128], 16, 128,
                           (lambda wb, gv=gv: wb.t[:, gv, :, :]), (lambda wb, gv=gv: wb.rq[gv])) for gv in range(2)]
            return fq_

        def hq_gen(l):
            def hq_(h):
                ql = []
                for q in range(4):
                    def dmas(view, q=q):
                        return [(view[:, :, sgm, :],
                                 w_in[l, q * 512:(q + 1) * 512, sgm * 1024 + h * 128:sgm * 1024 + (h + 1) * 128].rearrange("(a p) c -> p a c", p=128))
                                for sgm in range(4)]
                    ql.append({"view": lambda sg: sg.t[:, 0:2048].rearrange("p (a s c) -> p a s c", a=4, s=4),
                               "dmas": dmas,
                               "dst": (lambda wb, q=q: wb.t[:, 4 * q:4 * q + 4, :].rearrange("p a (s c) -> p a s c", s=4)),
                               "res": (lambda wb, q=q: wb.rq[q])})
                return ql
            return hq_

        PRE = {}

        def prefetch(key, qlist):
            lst = []
            for q in qlist:
                sg = stg_take()
                view = q["view"](sg)
                for k_, (dsub, src) in enumerate(q["dmas"](view)):
                    S.dma(dsub, src, writes=[sg.rq[k_]])
                lst.append((sg, view))
            PRE[key] = lst

        def run_slabs(n, quarters, compute, wring, key=None, nxt=None, ceng=None):
            qs = {}
            stg = {}
            wbs = {}

            def Q(i):
                if i not in qs:
                    qs[i] = quarters(i)
                return qs[i]

            def issue(i, qi):
                q = Q(i)[qi]
                sg = stg_take()
                view = q["view"](sg)
                for k_, (dsub, src) in enumerate(q["dmas"](view)):
                    S.dma(dsub, src, writes=[sg.rq[k_]])
                stg[(i, qi)] = (sg, view)

            def cast(i, qi):
                q = Q(i)[qi]
                sg, view = stg.pop((i, qi))
                CP(ceng if ceng is not None else cast_eng(), q["dst"](wbs[i]), view, sg.rq, [q["res"](wbs[i])])
                sg.pending = False

            def cast_and_prefetch(i):
                wbs[i] = wring.next()
                nq = len(Q(i))
                nq2 = len(Q(i + 1)) if i + 1 < n else 0
                for qi in range(max(nq, nq2)):
                    if qi < nq:
                        cast(i, qi)
                    if qi < nq2:
                        issue(i + 1, qi)

            if key is not None and key in PRE:
                for qi, ent in enumerate(PRE.pop(key)):
                    stg[(0, qi)] = ent
            else:
                for qi in range(len(Q(0))):
                    issue(0, qi)
            cast_and_prefetch(0)
            for i in range(n):
                if i + 1 < n:
                    cast_and_prefetch(i + 1)
                elif nxt is not None and not S.off:
                    prefetch(nxt[0], nxt[1])
                compute(i, wbs.pop(i))

        def transposes_to(dst_fn, src_buf, src_res, rows, nblk, dres, scale_fn=None):
            for b0 in range(0, nblk, 4):
                bi = pring_tp.next()
                bk, br = bank(bi)
                n = min(4, nblk - b0)
                for a in range(n):
                    TP(bk[:, a * 128:a * 128 + rows], src_buf[0:rows, (b0 + a) * 128:(b0 + a + 1) * 128], rows,
                       [src_res], [br], sig=(a == n - 1))
                for a in range(n):
                    i = b0 + a
                    eng = "act" if i % 2 == 0 else "dve"
                    src = bk[:, a * 128:a * 128 + rows]
                    if scale_fn is None:
                        CP(eng, dst_fn(i), src, [br], [dres])
                    elif eng == "act":
                        ACT(dst_fn(i), src, AF.Copy, [br, gam.r], [dres], scale=scale_fn(i))
                    else:
                        TS("dve", dst_fn(i), src, scale_fn(i), None, ALU.mult, None, [br, gam.r], [dres])

        def norm_pass(g, which):
            with contextlib.ExitStack() as ns:
                xt_ring = Ring([sbt(ns, "xt%d" % i, [128, D]) for i in range(2)])
                nj = sbt(ns, "nj", [128, D], BF16)
                for (gt, row0, rows, col0, li) in group_tiles(g):
                    xt = xt_ring.next()
                    S.dma(xt.t[0:rows, :], xbuf[row0:row0 + rows, :], reads=xb_r[gt], writes=[xt.r])
                    st = stat.next()
                    ACT(nj.t[0:rows, :], xt.t[0:rows, :], AF.Square, [xt.r], [nj.r, st.r], accum=st.t[0:rows, 0:1])
                    RSQ(st.t[0:rows, 2:3], st.t[0:rows, 0:1], 1.0 / D, rows, 1, st.r)
                    TS("dve", xt.t[0:rows, :], xt.t[0:rows, :], st.t[0:rows, 2:3], None, ALU.mult, None, [xt.r, st.r], [xt.r])
                    transposes_to(lambda i: hT.t[:, i, col0:col0 + rows], xt.t, xt.r, rows, 16, hT.rq[li],
                                  scale_fn=lambda i: gam.t[:, which, i:i + 1])
            S.barrier()

        def proj_tok(g, wb, ncols, kcn, lhs_buf, evac):
            for tl in group_tiles(g):
                (gt, row0, rows, col0, li) = tl
                bk, br = bank(pring_acc.next())
                for kc in range(kcn):
                    MM(bk[0:rows, 0:ncols], lhs_buf.t[:, kc, col0:col0 + rows], wb.t[:, kc, 0:ncols], kc == 0, kc == kcn - 1,
                       [lhs_buf.rq[li], wb.rq[kc // 4]], [br])
                evac(bk, br, tl)

        def resid_update(l, g, last, w_ap, kcn, lhs_buf, st_scope, ncols=512, extra=None, qa=4, key=None, nxt=None):
            nq = (kcn + qa - 1) // qa
            wring = Ring([sbt(st_scope, "wr%d" % i, [128, kcn, ncols], BF16, nres=nq) for i in range(2)])
            nsl = D // ncols

            quarters = rq_gen(w_ap, kcn, ncols, qa, extra[0] if extra is not None else None)

            def compute(ds, wb):
                tls = group_tiles(g)
                xos = {}
                blk_of = lambda gt: [xb_r[gt][b] for b in range(ds * ncols // 256, (ds + 1) * ncols // 256)]

                def load(i):
                    (gt, row0, rows, col0, li) = tls[i]
                    xo = xo_ring.next()
                    S.dma(xo.t[0:rows, 0:ncols], xbuf[row0:row0 + rows, ds * ncols:(ds + 1) * ncols], reads=blk_of(gt), writes=[xo.r], q=STQ)
                    xos[i] = xo
                PF = 3
                for i in range(min(PF, len(tls))):
                    load(i)
                for i, tl in enumerate(tls):
                    (gt, row0, rows, col0, li) = tl
                    bk, br = bank(pring_acc.next())
                    for kc in range(kcn):
                        MM(bk[0:rows, 0:ncols], lhs_buf.t[:, kc, col0:col0 + rows], wb.t[:, kc, 0:ncols], kc == 0, kc == kcn - 1,
                           [lhs_buf.rq[li], wb.rq[kc // qa]], [br])
                    if i + PF < len(tls):
                        load(i + PF)
                    xo = xos.pop(i)
                    if extra is None:
                        TT("dve", xo.t[0:rows, 0:ncols], xo.t[0:rows, 0:ncols], bk[0:rows, 0:ncols], ALU.add, [xo.r, br], [xo.r])
                    else:
                        extra[1](ds, bk, br, tl, xo)
                    if last:
                        dst = (y_p[row0:row0 + rows, ds * ncols:(ds + 1) * ncols] if gt < 16
                               else y_s[:, ds * ncols:(ds + 1) * ncols])
                        S.dma(dst, xo.t[0:rows, 0:ncols], reads=[xo.r], q=STQ)
                    else:
                        S.dma(xbuf[row0:row0 + rows, ds * ncols:(ds + 1) * ncols], xo.t[0:rows, 0:ncols], reads=[xo.r], writes=blk_of(gt), q=STQ)
            run_slabs(nsl, quarters, compute, wring, key=key, nxt=nxt)

        def mixers(l, g, st):
            has_dec = (g == NG - 1)
            tiles = group_tiles(g)
            mvs = moving(g)
            if False:
                has_dec = False
                tiles = tiles[0:4]
                mvs = mvs[0:1]
            mixT = sbt(st, "mixT", [128, 16, GW], BF16, nres=5)
            wring = Ring([sbt(st, "wi%d" % i, [128, 16, 512], BF16, nres=4) for i in range(2)])

            def epilog_tp(src, sres, rows, col0, li, fc0, nblk):
                transposes_to(lambda i: mixT.t[:, fc0 + i, col0:col0 + rows], src, sres, rows, nblk, mixT.rq[li])

            with contextlib.ExitStack() as hs:
                QT = sbt(hs, "QT", [128, GW]); FS = sbt(hs, "FS", [128, GW]); KTt = sbt(hs, "KTt", [128, GW])
                LF = sbt(hs, "LF", [128, GW]); Bc = sbt(hs, "Bc", [128, GW])
                Vb = sbt(hs, "Vb", [128, 5, 128], BF16, nres=5); GS = sbt(hs, "GS", [128, 5, 128], F32, nres=5)
                Vd = sbt(hs, "Vd", [16, 128]); sel = sbt(hs, "sel", [16, 16 * 128])
                e_ring = Ring([sbt(hs, "er%d" % i, [128, 128]) for i in range(6)])
                b_ring = Ring([sbt(hs, "br%d" % i, [128, 128], BF16) for i in range(30)])
                SC = [sbt(hs, "SC%d" % i, [128, 128], BF16) for i in range(6)]
                d_ring = Ring([sbt(hs, "d128%d" % i, [128, 4]) for i in range(2)])
                nb_ring = Ring([sbt(hs, "nb%d" % i, [128, 2]) for i in range(4)])
                oa_ring = Ring([sbt(hs, "oa%d" % i, [128, 128]) for i in range(2)])
                oa4 = [sbt(hs, "oaq%d" % i, [128, 128]) for i in range(4)]
                ei_ = [0]
                s0_ring = Ring([sbt(hs, "s0%d" % i, [128, 128]) for i in range(3)])
                sn_ring = Ring([sbt(hs, "sn%d" % i, [128, 128]) for i in range(3)])
                t1_ring = Ring([sbt(hs, "t1%d" % i, [128, 128]) for i in range(2)])
                snb_ring = Ring([sbt(hs, "snb%d" % i, [128, 128], BF16) for i in range(8)])
                QM = sbt(hs, "QM", [128, 16 * 16], BF16)
                for scb in SC:
                    S.op("pool", lambda e, scb=scb: e.memset(scb.t[:], 0.0), writes=[scb.r])
                S.op("pool", lambda e: e.memset(QM.t[:], 0.0), writes=[QM.r])
                KHX = 0
                if has_dec and not (KHX & 1):
                    S.dma(sel.t[:], c_sel, writes=[sel.r])
                sci = [0]

                hq = hq_gen(l)

                PB = [dict(QT=QT, FS=FS, Vb=Vb, GS=GS, Vd=Vd),
                      dict(QT=sbt(hs, "QT2", [128, GW]), FS=sbt(hs, "FS2", [128, GW]), Vb=sbt(hs, "Vb2", [128, 5, 128], BF16, nres=5),
                           GS=sbt(hs, "GS2", [128, 5, 128], F32, nres=5), Vd=sbt(hs, "Vd2", [16, 128]))]

                def proj_chunks(h, wb, P):
                    out = []
                    for blk, key_, fn in ((0, "QT", AF.Silu), (1, "FS", AF.Sigmoid)):
                        held = []

                        def mm(blk=blk, held=held):
                            for (c0, n, lis) in mvs:
                                bk, br = bank(pring_acc.next() if n == 512 else pring_x.next())
                                for kc in range(16):
                                    MM(bk[:, 0:n], wb.t[:, kc, blk * 128:(blk + 1) * 128], hT.t[:, kc, c0:c0 + n], kc == 0, kc == 15,
                                       [wb.rq[kc // 4]] + [hT.rq[i] for i in lis], [br])
                                held.append((bk, br, c0, n))

                        def ev(key_=key_, fn=fn, held=held):
                            dstb = P[key_]
                            for (bk, br, c0, n) in held:
                                ACT(dstb.t[:, c0:c0 + n], bk[:, 0:n], fn, [br], [dstb.r])
                        out.append((mm, ev))
                    for tsel in (tiles[0:2], tiles[2:]):
                        held = []

                        def mm(tsel=tsel, held=held):
                            for (gt, row0, rows, col0, li) in tsel:
                                bk, br = bank(pring_acc.next())
                                for kc in range(16):
                                    MM(bk[0:rows, 0:256], hT.t[:, kc, col0:col0 + rows], wb.t[:, kc, 256:512], kc == 0, kc == 15,
                                       [hT.rq[li], wb.rq[kc // 4]], [br])
                                held.append((bk, br, rows, li))

                        def ev(held=held):
                            Vb_, GS_, Vd_ = P["Vb"], P["GS"], P["Vd"]
                            for (bk, br, rows, li) in held:
                                CP("act", Vb_.t[0:rows, li, :], bk[0:rows, 0:128], [br], [Vb_.rq[li]])
                                if li == 4:
                                    CP("act", Vd_.t[0:rows, :], bk[0:rows, 0:128], [br], [Vd_.r])
                                ACT(GS_.t[0:rows, li, :], bk[0:rows, 128:256], AF.Silu, [br], [GS_.rq[li]])
                                TT("pool", GS_.t[0:rows, li, :], GS_.t[0:rows, li, :], hgbc.t[0:rows, :], ALU.mult, [GS_.rq[li], hgbc.r], [GS_.rq[li]])
                        out.append((mm, ev))
                    return out

                def rest_parts(h, P):
                    QT_, FS_, Vb_, GS_, Vd_ = P["QT"], P["FS"], P["Vb"], P["GS"], P["Vd"]
                    if g == 0:
                        S.op("pool", lambda e, h=h: e.memset(Sf.t[:, h, :], 0.0), writes=[Sf.rq[h]])
                        S.op("pool", lambda e, h=h: e.memset(Sb.t[:, h, :], 0.0), writes=[Sb.rq[h]])
                    W = GW if has_dec else 512
                    TS("dve", FS_.t[:, 0:W], FS_.t[:, 0:W], oml.t[:, l, h:h + 1], lbt.t[:, l, h:h + 1], ALU.mult, ALU.add, [FS_.r, oml.r, lbt.r], [FS_.r])
                    ACT(LF.t[:, 0:W], FS_.t[:, 0:W], AF.Ln, [FS_.r], [LF.r])
                    TS("pool", KTt.t[:, 0:W], FS_.t[:, 0:W], -1.0, 1.0, ALU.mult, ALU.add, [FS_.r], [KTt.r])
                    S.op("dve", lambda e: e.tensor_tensor_scan(out=Bc.t[:, 0:W], data0=mask01.t[:, 0:W], data1=LF.t[:, 0:W], initial=0.0,
                                                               op0=ALU.mult, op1=ALU.add), [mask01.r, LF.r], [Bc.r])
                    ptiles = [t_ for t_ in tiles if t_[4] < 4]
                    bkh, bkhr = bank(pring_tp.next())
                    bsc, bscr = bank(pring_x.next())
                    bst, bstr = bank(pring_tp.next())
                    d128 = d_ring.next()
                    per = []
                    for (gt, row0, rows, col0, li) in ptiles:
                        c0 = col0
                        cs_ = slice(li * 128, (li + 1) * 128)
                        E1 = e_ring.next(); EA = e_ring.next(); EK = e_ring.next()
                        Cb = b_ring.next(); Ab = b_ring.next(); Bb = b_ring.next(); Db = b_ring.next(); KH = b_ring.next()
                        KHT = e_ring.next()
                        nb = nb_ring.next()
                        ACT(E1.t[:], Bc.t[:, c0:c0 + 128], AF.Exp, [Bc.r], [E1.r])
                        TT("dve", Cb.t[:], QT_.t[:, c0:c0 + 128], E1.t[:], ALU.mult, [QT_.r, E1.r], [Cb.r])
                        CP("pool", d128.t[:, li:li + 1], E1.t[:, 127:128], [E1.r], [d128.r])
                        ACT(EA.t[:], Bc.t[:, c0:c0 + 128], AF.Exp, [Bc.r], [EA.r], scale=-1.0, bias=Bc.t[:, c0 + 63:c0 + 64])
                        TT("pool", Ab.t[:], KTt.t[:, c0:c0 + 128], EA.t[:], ALU.mult, [KTt.r, EA.r], [Ab.r])
                        EB = e_ring.next()
                        ACT(EB.t[:, 0:64], Bc.t[:, c0:c0 + 64], AF.Exp, [Bc.r], [EB.r], scale=-1.0)
                        TT("pool", Bb.t[:, 0:64], KTt.t[:, c0:c0 + 64], EB.t[:, 0:64], ALU.mult, [KTt.r, EB.r], [Bb.r])
                        TS("dve", nb.t[:, 0:1], Bc.t[:, c0 + 63:c0 + 64], -1.0, 0.0, ALU.mult, ALU.add, [Bc.r], [nb.r])
                        ACT(EB.t[:, 64:128], Bc.t[:, c0 + 64:c0 + 128], AF.Exp, [Bc.r, nb.r], [EB.r], bias=nb.t[:, 0:1])
                        TT("dve", Db.t[:, 0:64], QT_.t[:, c0 + 64:c0 + 128], EB.t[:, 64:128], ALU.mult, [QT_.r, EB.r], [Db.r])
                        ACT(EK.t[:], Bc.t[:, c0:c0 + 128], AF.Exp, [Bc.r], [EK.r], scale=-1.0, bias=Bc.t[:, c0 + 127:c0 + 128])
                        TT("pool", KHT.t[:], KTt.t[:, c0:c0 + 128], EK.t[:], ALU.mult, [KTt.r, EK.r], [KHT.r])
                        TP(bkh[:, cs_], KHT.t[:, :], 128, [KHT.r], [bkhr])
                        CP("act", KH.t[:], bkh[:, cs_], [bkhr], [KH.r])
                        MM(bsc[0:64, li * 128:li * 128 + 64], Bb.t[:, 0:64], Cb.t[:, 0:64], True, True, [Bb.r, Cb.r], [bscr], sig=False)
                        MM(bsc[:, li * 128 + 64:li * 128 + 128], Ab.t[:, :], Db.t[:, 0:64], True, True, [Ab.r, Db.r], [bscr])
                        scb = SC[sci[0] % len(SC)]; sci[0] += 1
                        TT("dve", scb.t[0:64, 0:64], bsc[0:64, li * 128:li * 128 + 64], caus.t[0:64, 0:64], ALU.mult, [bscr, caus.r], [scb.r])
                        TT("dve", scb.t[:, 64:128], bsc[:, li * 128 + 64:li * 128 + 128], m2.t[:, :], ALU.mult, [bscr, m2.r], [scb.r])
                        MM(bst[:, cs_], KH.t[:, :], Vb_.t[:, li, :], True, True, [KH.r, Vb_.rq[li]], [bstr])
                        per.append((Cb, scb, li, col0, gt))
                        if li in (0, 2):
                            yield
                    bo, bor = bank(pring_x.next())
                    for (Cb, scb, li, col0, gt) in per:
                        cs_ = slice(li * 128, (li + 1) * 128)
                        MM(bo[:, cs_], scb.t[:, :], Vb_.t[:, li, :], True, False, [scb.r, Vb_.rq[li]], [bor])
                        MM(bo[:, cs_], Cb.t[:, :], Sb.t[:, h, :], False, True, [Cb.r, Sb.rq[h]], [bor])
                        STT(Sf.t[:, h, :], Sf.t[:, h, :], d128.t[:, li:li + 1], bst[:, cs_], ALU.mult, ALU.add, [Sf.rq[h], d128.r, bstr], [Sf.rq[h]])
                        CP("act", Sb.t[:, h, :], Sf.t[:, h, :], [Sf.rq[h]], [Sb.rq[h]])
                        if gt == 15:
                            S.dma(hp_o[l, h], Sf.t[:, h, :], reads=[Sf.rq[h]], q=STQ)
                    yield
                    sts = [stat.next() for _ in per]
                    oas = [oa4[(ei_[0] + i_) % len(oa4)] for i_ in range(len(per))]
                    ei_[0] += len(per)
                    for st_, (Cb, scb, li, col0, gt) in zip(sts, per):
                        ACT(junk.t[:, 0:128], bo[:, li * 128:(li + 1) * 128], AF.Square, [bor], [junk.r, st_.r], accum=st_.t[:, 0:1])
                    for st_ in sts:
                        RSQ(st_.t[:, 2:3], st_.t[:, 0:1], 1.0 / 128, 128, 1, st_.r)
                    for st_, oa_, (Cb, scb, li, col0, gt) in zip(sts, oas, per):
                        STT(oa_.t[:, :], bo[:, li * 128:(li + 1) * 128], st_.t[:, 2:3], GS_.t[:, li, :], ALU.mult, ALU.mult, [bor, st_.r, GS_.rq[li]], [oa_.r])
                    bke, bker = bank(pring_tp.next())
                    for oa_, (Cb, scb, li, col0, gt) in zip(oas, per):
                        TP(bke[:, li * 128:(li + 1) * 128], oa_.t[:, :], 128, [oa_.r], [bker])
                    c0_ = per[0][3]
                    CP("act", mixT.t[:, h, c0_:c0_ + 128 * len(per)], bke[:, 0:128 * len(per)], [bker], [mixT.rq[p_[2]] for p_ in per])
                    if has_dec:
                        CP("dve", QM.t[:, 0:256:17], QT_.t[:, 512:528], [QT_.r], [QM.r])
                        bod, bodr = bank(pring_x.next())
                        HB = 8
                        for b0 in range(0, NB, HB):
                            snbs = []
                            for b in range(b0, b0 + HB):
                                s0 = s0_ring.next(); sn = sn_ring.next(); t1 = t1_ring.next(); snb = snb_ring.next()
                                S.dma(s0.t[:], sh[l, b, h], writes=[s0.r])
                                bb, bbr = bank(pring_tp.next())
                                MM(bb[:, 0:128], sel.t[0:16, b * 128:(b + 1) * 128], Vd_.t[0:16, :], True, True, [sel.r, Vd_.r], [bbr])
                                TS("dve", t1.t[:], bb[:, 0:128], KTt.t[:, 512 + b:513 + b], None, ALU.mult, None, [bbr, KTt.r], [t1.r])
                                STT(sn.t[:], s0.t[:], FS_.t[:, 512 + b:513 + b], t1.t[:], ALU.mult, ALU.add, [s0.r, FS_.r, t1.r], [sn.r])
                                S.dma(hs_o[l, b, h], sn.t[:], reads=[sn.r], q=STQ)
                                CP("pool", snb.t[:], sn.t[:], [sn.r], [snb.r])
                                snbs.append((b, snb))
                            for (b, snb) in snbs:
                                MM(bod[0:16, 0:128], QM.t[:, b * 16:(b + 1) * 16], snb.t[:, :], b == 0, b == NB - 1, [QM.r, snb.r], [bodr], sig=True)
                        hgrn_epilog(bod, bodr, NB, 512, 4, h, GS_, oa_ring, epilog_tp)
                    yield

                pend = [None]

                def hcompute(h, wb):
                    chunks = proj_chunks(h, wb, PB[h % 2])
                    rest = pend[0]
                    for (mm, ev) in chunks:
                        mm()
                        if rest is not None:
                            next(rest)
                        ev()
                    pend[0] = rest_parts(h, PB[h % 2])
                run_slabs(8, hq, hcompute, wring, key=("hgrn", l, g), ceng="dve")
                for _ in pend[0]:
                    pass
            S.barrier()
            mark("hgrn")
            chk()
            with contextlib.ExitStack() as ss:
                swa(l, g, ss, wring, mixT, epilog_tp)
            S.barrier()
            mark("swa")
            chk()
            with contextlib.ExitStack() as gs:
                gmlp(l, g, gs, wring, mixT, epilog_tp)
            S.barrier()
            if KDBG:
                off = S.off
                S.off = False
                S.barrier()
                S.dma(dbg_mix[g], mixT.t[:], reads=mixT.rq)
                S.barrier()
                S.off = off
            return mixT

        def hgrn_epilog(bo, bor, rows, col0, li, h, GS, oa_ring, epilog_tp):
            st = stat.next()
            ACT(junk.t[0:rows, 0:128], bo[0:rows, 0:128], AF.Square, [bor], [junk.r, st.r], accum=st.t[0:rows, 0:1])
            RSQ(st.t[0:rows, 2:3], st.t[0:rows, 0:1], 1.0 / 128, rows, 1, st.r)
            oa = oa_ring.next()
            STT(oa.t[0:rows, :], bo[0:rows, 0:128], st.t[0:rows, 2:3], GS.t[0:rows, li, :], ALU.mult, ALU.mult, [bor, st.r, GS.rq[li]], [oa.r])
            epilog_tp(oa.t, oa.r, rows, col0, li, h, 1)

        def load_slab(l, wb, c0, ncols):
            for q in range(4):
                stage_cast(w_in[l, q * 512:(q + 1) * 512, c0:c0 + ncols].rearrange("(a p) c -> p a c", p=128),
                           wb.t[:, 4 * q:4 * q + 4, 0:ncols], wb.rq[q], 4, ncols)

        def swa(l, g, ss, wring, mixT, epilog_tp):
            has_dec = (g == NG - 1)
            tiles = group_tiles(g)
            QTp = sbt(ss, "QTp", [128, 8, GW], BF16, nres=5)
            QNP = [sbt(ss, "QNP%d" % i, [128, 8, 128]) for i in range(2)]
            KN = [sbt(ss, "KN%d" % i, [128, 128]) for i in range(2)]
            VR = [sbt(ss, "VR%d" % i, [128, 128]) for i in range(2)]
            tq = sbt(ss, "tq", [128, 8, 64]); tk = sbt(ss, "tk", [128, 128])
            KTd = sbt(ss, "KTd", [128, 16], BF16)
            pt_ring = Ring([sbt(ss, "pt%d" % i, [128, 512], BF16) for i in range(2)])
            pm_ring = Ring([sbt(ss, "pm%d" % i, [128, 4, 128], BF16) for i in range(4)])
            OB = Ring([sbt(ss, "OB%d" % i, [128, 8, 64]) for i in range(2)])
            dd = Ring([sbt(ss, "dd%d" % i, [128, 16]) for i in range(2)])
            for qn in QNP:
                S.op("pool", lambda e, qn=qn: e.memset(qn.t[:], 0.0), writes=[qn.r])
            wa = wring.next(); load_slab(l, wa, 4096, 512)
            wk = wring.next(); load_slab(l, wk, 4608, 256)
            ti = 0
            sprj = {}

            def sproj(ix):
                (gt, row0, rows, col0, li) = tiles[ix]
                bq, bqr = bank(pring_acc.next())
                for kc in range(16):
                    MM(bq[0:rows, 0:512], hT.t[:, kc, col0:col0 + rows], wa.t[:, kc, 0:512], kc == 0, kc == 15, [hT.rq[li], wa.rq[kc // 4]], [bqr])
                bk_, bkr = bank(pring_acc.next())
                for kc in range(16):
                    MM(bk_[0:rows, 0:256], hT.t[:, kc, col0:col0 + rows], wk.t[:, kc, 0:256], kc == 0, kc == 15, [hT.rq[li], wk.rq[kc // 4]], [bkr])
                sprj[ix] = (bq, bqr, bk_, bkr)
            sproj(0)
            for ix_, (gt, row0, rows, col0, li) in enumerate(tiles):
                qn = QNP[ti % 2]; kn = KN[ti % 2]; vr = VR[ti % 2]; ti += 1
                (bq, bqr, bk_, bkr) = sprj.pop(ix_)
                st = stat.next()
                ACT(tq.t[0:rows].rearrange("p a b -> p (a b)"), bq[0:rows, 0:512], AF.Square, [bqr], [tq.r])
                S.op("dve", lambda e: e.tensor_reduce(out=st.t[0:rows, 0:8], in_=tq.t[0:rows], axis=AX.X, op=ALU.add), [tq.r], [st.r])
                ACT(tk.t[0:rows, :], bk_[0:rows, 0:128], AF.Square, [bkr], [tk.r])
                S.op("dve", lambda e: e.tensor_reduce(out=st.t[0:rows, 8:10], in_=tk.t[0:rows, :].rearrange("p (a b) -> p a b", a=2), axis=AX.X, op=ALU.add), [tk.r, st.r], [st.r])
                RSQ(st.t[0:rows, 0:10], st.t[0:rows, 0:10], 1.0 / 64, rows, 10, st.r)
                TT("dve", tq.t[0:rows], bq[0:rows, 0:512].rearrange("p (a b) -> p a b", a=8), st.t[0:rows, 0:8].unsqueeze(2).to_broadcast([rows, 8, 64]), ALU.mult, [bqr, st.r], [tq.r])
                for j in range(2):
                    TT("pool", qn.t[0:rows, 4 * j:4 * j + 4, 64 * j:64 * j + 64], tq.t[0:rows, 4 * j:4 * j + 4, :],
                       qgbc.t[0:rows, :].unsqueeze(1).to_broadcast([rows, 4, 64]), ALU.mult, [tq.r, qgbc.r], [qn.r])
                TT("dve", tk.t[0:rows, :].rearrange("p (a b) -> p a b", a=2), bk_[0:rows, 0:128].rearrange("p (a b) -> p a b", a=2),
                   st.t[0:rows, 8:10].unsqueeze(2).to_broadcast([rows, 2, 64]), ALU.mult, [bkr, st.r], [tk.r])
                TT("pool", kn.t[0:rows, :].rearrange("p (a b) -> p a b", a=2), tk.t[0:rows, :].rearrange("p (a b) -> p a b", a=2),
                   kgbc.t[0:rows, :].unsqueeze(1).to_broadcast([rows, 2, 64]), ALU.mult, [tk.r, kgbc.r], [kn.r])
                CP("act", vr.t[0:rows, :], bk_[0:rows, 128:256], [bkr], [vr.r])
                if li < 4:
                    CP("pool", V1all.t[0:rows, gt, :, 0:64], vr.t[0:rows, :].rearrange("p (a b) -> p a b", a=2), [vr.r], [V1all.rq[gt]])
                if ix_ + 1 < len(tiles):
                    sproj(ix_ + 1)
                transposes_to(lambda i: QTp.t[:, i, col0:col0 + rows], qn.t[:].rearrange("p a b -> p (a b)"), qn.r, rows, 8, QTp.rq[li])
                if li < 4:
                    transposes_to(lambda i: KTall.t[:, gt * 128:gt * 128 + rows], kn.t, kn.r, rows, 1, KTall.rq[gt])
                else:
                    transposes_to(lambda i: KTd.t[:, 0:rows], kn.t, kn.r, rows, 1, KTd.r)
                if gt == 15:
                    S.dma(kp_o[l], kn.t[:], reads=[kn.r], q=STQ)
                    S.dma(vp_o[l], vr.t[:], reads=[vr.r], q=STQ)
                if li < 4:
                    ob = OB.next()
                    for j in range(2):
                        kts = ([gt - 1] if gt > 0 else []) + [gt]
                        pms = []
                        for kt in kts:
                            bs_, bsr = bank(pring_x.next())
                            MM(bs_[:, 0:512].rearrange("p (a b) -> p a b", a=4), KTall.t[:, kt * 128:(kt + 1) * 128], QTp.t[:, 4 * j:4 * j + 4, col0:col0 + 128], True, True,
                               [KTall.rq[kt], QTp.rq[li]], [bsr])
                            pt = pt_ring.next(); pm = pm_ring.next()
                            ACT(pt.t[:], bs_[:, 0:512], AF.Exp, [bsr], [pt.r], scale=0.125)
                            mk = caus if kt == gt else lows
                            TT("dve" if kt == gt else "pool", pm.t[:], pt.t[:].rearrange("p (a b) -> p a b", a=4),
                               mk.t[:].unsqueeze(1).to_broadcast([128, 4, 128]), ALU.mult, [pt.r, mk.r], [pm.r])
                            pms.append((pm, kt))
                        bo, bor = bank(pring_acc.next())
                        for gq in range(4):
                            for ii, (pm, kt) in enumerate(pms):
                                MM(bo[:, gq * 65:(gq + 1) * 65], pm.t[:, gq, :], V1all.t[:, kt, j, :], ii == 0, ii == len(pms) - 1,
                                   [pm.r, V1all.rq[kt]], [bor], sig=(gq == 3 and ii == len(pms) - 1))
                        d_ = dd.next()
                        bov = bo[:, 0:260].rearrange("p (a b) -> p a b", a=4)
                        TT("dve", d_.t[:, 0:4], bov[:, :, 64], esink.t[:, 4 * j:4 * j + 4], ALU.add, [bor, esink.r], [d_.r])
                        S.op("dve", lambda e, d_=d_: e.reciprocal(out=d_.t[:, 4:8], in_=d_.t[:, 0:4]), [d_.r], [d_.r])
                        TT("dve", ob.t[:, 4 * j:4 * j + 4, :], bov[:, :, 0:64], d_.t[:, 4:8].unsqueeze(2).to_broadcast([128, 4, 64]), ALU.mult, [bor, d_.r], [ob.r])
                    epilog_tp(ob.t[:].rearrange("p a b -> p (a b)"), ob.r, 128, col0, li, 8, 4)
                else:
                    swa_dec(l, ss, qn, kn, vr, QTp, KTd, dd, OB, epilog_tp)

        def swa_dec(l, ss, qn, kn, vr, QTp, KTd, dd, OB, epilog_tp):
            CKT = sbt(ss, "CKT", [128, NB, 128], BF16); V1c = sbt(ss, "V1c", [128, NB, 2, 65], BF16)
            ld_ring = Ring([sbt(ss, "ld%d" % i, [128, 128]) for i in range(3)])
            PTd = sbt(ss, "PTd", [128, 128]); PTM = sbt(ss, "PTM", [128, 8, NB * NB], BF16)
            prod = sbt(ss, "prod", [16, 8, 64]); psf = sbt(ss, "psf", [16, 16]); t1 = sbt(ss, "t1d", [16, 8, 64])
            S.op("pool", lambda e: e.memset(V1c.t[:], 1.0), writes=[V1c.r])
            S.op("pool", lambda e: e.memset(PTM.t[:], 0.0), writes=[PTM.r])
            S.dma(ks_o[l, :, 0:127, :], ck[l, :, 1:128, :])
            S.dma(vs_o[l, :, 0:127, :], cv[l, :, 1:128, :])
            S.dma(ks_o[l, :, 127, :], kn.t[0:NB, :], reads=[kn.r], q=STQ)
            S.dma(vs_o[l, :, 127, :], vr.t[0:NB, :], reads=[vr.r], q=STQ)
            for b in range(NB):
                ckt = ld_ring.next()
                S.dma(ckt.t[:], ck[l, b], writes=[ckt.r])
                bk, br = bank(pring_tp.next())
                TP(bk[:, 0:128], ckt.t[:, :], 128, [ckt.r], [br])
                CP("act", CKT.t[:, b, :], bk[:, 0:128], [br], [CKT.r])
                cvt = ld_ring.next()
                S.dma(cvt.t[:], cv[l, b], writes=[cvt.r])
                CP("pool", V1c.t[:, b, :, 0:64], cvt.t[:, :].rearrange("p (a c) -> p a c", a=2), [cvt.r], [V1c.r])
            bsd, bsdr = bank(pring_x.next())
            for b in range(NB):
                for j in range(2):
                    MM(bsd[:, b * 8 + 4 * j:b * 8 + 4 * j + 4], CKT.t[:, b, :], QTp.t[:, 4 * j:4 * j + 4, 512 + b], True, True,
                       [CKT.r, QTp.rq[4]], [bsdr], sig=(b == NB - 1 and j == 1))
            ACT(PTd.t[:], bsd[:, 0:128], AF.Exp, [bsdr], [PTd.r], scale=0.125)
            TS("dve", PTM.t[:, :, 0:256:17], PTd.t[:].rearrange("p (b h) -> p h b", h=8), kmask.t[:, 0:1], None, ALU.mult, None,
               [PTd.r, kmask.r], [PTM.r])
            for j in range(2):
                TT("dve", prod.t[:, 4 * j:4 * j + 4, :], qn.t[0:NB, 4 * j:4 * j + 4, 64 * j:64 * j + 64],
                   kn.t[0:NB, 64 * j:64 * j + 64].unsqueeze(1).to_broadcast([NB, 4, 64]), ALU.mult, [qn.r, kn.r], [prod.r])
            S.op("dve", lambda e: e.tensor_reduce(out=psf.t[:, 0:8], in_=prod.t[:], axis=AX.X, op=ALU.add), [prod.r], [psf.r])
            ACT(psf.t[:, 8:16], psf.t[:, 0:8], AF.Exp, [psf.r], [psf.r], scale=0.125)
            vrv = vr.t[0:NB, :].rearrange("p (a b) -> p a b", a=2)
            ob = OB.next()
            for j in range(2):
                bod, bodr = bank(pring_acc.next())
                for gq in range(4):
                    h = 4 * j + gq
                    for b in range(NB):
                        MM(bod[0:NB, gq * 65:(gq + 1) * 65], PTM.t[:, h, b * NB:(b + 1) * NB], V1c.t[:, b, j, :], b == 0, b == NB - 1,
                           [PTM.r, V1c.r], [bodr], sig=(gq == 3 and b == NB - 1))
                bov = bod[0:NB, 0:260].rearrange("p (a b) -> p a b", a=4)
                TT("dve", t1.t[:, 4 * j:4 * j + 4, :], vrv[:, j:j + 1, :].to_broadcast([NB, 4, 64]),
                   psf.t[:, 8 + 4 * j:12 + 4 * j].unsqueeze(2).to_broadcast([NB, 4, 64]), ALU.mult, [vr.r, psf.r], [t1.r])
                TT("dve", t1.t[:, 4 * j:4 * j + 4, :], t1.t[:, 4 * j:4 * j + 4, :], bov[:, :, 0:64], ALU.add, [t1.r, bodr], [t1.r])
                d_ = dd.next()
                TT("dve", d_.t[0:NB, 0:4], bov[:, :, 64], psf.t[:, 8 + 4 * j:12 + 4 * j], ALU.add, [bodr, psf.r], [d_.r])
                TT("dve", d_.t[0:NB, 0:4], d_.t[0:NB, 0:4], esink.t[0:NB, 4 * j:4 * j + 4], ALU.add, [d_.r, esink.r], [d_.r])
                S.op("dve", lambda e, d_=d_: e.reciprocal(out=d_.t[0:NB, 8:12], in_=d_.t[0:NB, 0:4]), [d_.r], [d_.r])
                TT("dve", ob.t[0:NB, 4 * j:4 * j + 4, :], t1.t[:, 4 * j:4 * j + 4, :], d_.t[0:NB, 8:12].unsqueeze(2).to_broadcast([NB, 4, 64]), ALU.mult, [t1.r, d_.r], [ob.r])
            epilog_tp(ob.t[:].rearrange("p a b -> p (a b)"), ob.r, NB, 512, 4, 8, 4)

        def gmlp(l, g, gs, wring, mixT, epilog_tp):
            tiles = group_tiles(g)
            GU = Ring([sbt(gs, "GU%d" % i, [128, 512]) for i in range(2)])
            GV = Ring([sbt(gs, "GV%d" % i, [128, 512]) for i in range(2)])
            VNb = Ring([sbt(gs, "VNb%d" % i, [128, 512], BF16) for i in range(2)])
            OC = Ring([sbt(gs, "OC%d" % i, [128, 512]) for i in range(2)])
            wu = wring.next(); load_slab(l, wu, 4864, 512)
            wv = wring.next(); load_slab(l, wv, 5376, 512)
            gprj = {}

            def gproj(ix):
                (gt, row0, rows, col0, li) = tiles[ix]
                lst = []
                for wb in (wu, wv):
                    bk, br = bank(pring_acc.next())
                    for kc in range(16):
                        MM(bk[0:rows, 0:512], hT.t[:, kc, col0:col0 + rows], wb.t[:, kc, 0:512], kc == 0, kc == 15, [hT.rq[li], wb.rq[kc // 4]], [br])
                    lst.append((bk, br))
                gprj[ix] = lst
            gproj(0)
            for ix_, (gt, row0, rows, col0, li) in enumerate(tiles):
                outs = []
                for (bk, br), ring in zip(gprj.pop(ix_), (GU, GV)):
                    t = tA.next(); go = ring.next()
                    ACT(t.t[0:rows, :], bk[0:rows, 0:512], AF.Square, [br], [t.r])
                    TS("pool", t.t[0:rows, :], t.t[0:rows, :], 0.044715, 1.0, ALU.mult, ALU.add, [t.r], [t.r])
                    TT("dve", t.t[0:rows, :], t.t[0:rows, :], bk[0:rows, 0:512], ALU.mult, [t.r, br], [t.r])
                    ACT(t.t[0:rows, :], t.t[0:rows, :], AF.Sigmoid, [t.r], [t.r], scale=GELU_C)
                    TT("dve", go.t[0:rows, :], t.t[0:rows, :], bk[0:rows, 0:512], ALU.mult, [t.r, br], [go.r])
                    outs.append(go)
                gu, gv = outs
                if ix_ + 1 < len(tiles):
                    gproj(ix_ + 1)
                st = stat.next()
                S.op("dve", lambda e: e.bn_stats(out=st.t[0:rows, 0:6], in_=gv.t[0:rows, :]), [gv.r], [st.r])
                S.op("dve", lambda e: e.bn_aggr(out=st.t[0:rows, 6:8], in_=st.t[0:rows, 0:6]), [st.r], [st.r])
                RSQ(st.t[0:rows, 9:10], st.t[0:rows, 7:8], 1.0, rows, 1, st.r)
                TS("dve", gv.t[0:rows, :], gv.t[0:rows, :], st.t[0:rows, 6:7], st.t[0:rows, 9:10], ALU.subtract, ALU.mult, [gv.r, st.r], [gv.r])
                TT("pool", gv.t[0:rows, :], gv.t[0:rows, :], lgbc.t[0:rows, :], ALU.mult, [gv.r, lgbc.r], [gv.r])
                TT("pool", gv.t[0:rows, :], gv.t[0:rows, :], lbbc.t[0:rows, :], ALU.add, [gv.r, lbbc.r], [gv.r])
                oc = OC.next()
                if li < 4:
                    vb = VNb.next()
                    CP("act", vb.t[:], gv.t[:], [gv.r], [vb.r])
                    bm, bmr = bank(pring_x.next())
                    for c in range(4):
                        MM(bm[:, c * 128:(c + 1) * 128], WsT.t[:, c, :], vb.t[:, c * 128:(c + 1) * 128], True, True, [WsT.r, vb.r], [bmr], sig=(c == 3))
                    t = tA.next()
                    TT("dve", t.t[:].rearrange("p (a b) -> p a b", a=4), bm[:, 0:512].rearrange("p (a b) -> p a b", a=4),
                       bsT.t[:, :].unsqueeze(2).to_broadcast([128, 4, 128]), ALU.add, [bmr, bsT.r], [t.r])
                    TT("pool", oc.t[:], t.t[:], gu.t[:], ALU.mult, [t.r, gu.r], [oc.r])
                else:
                    S.dma(gv_o[l], gv.t[0:NB, :], reads=[gv.r], q=STQ)
                    t = tA.next()
                    TT("dve", t.t[0:NB].rearrange("p (a b) -> p a b", a=4), gv.t[0:NB].rearrange("p (a b) -> p a b", a=4),
                       w00.t[:, :].unsqueeze(2).to_broadcast([NB, 4, 128]), ALU.mult, [gv.r, w00.r], [t.r])
                    TT("dve", t.t[0:NB].rearrange("p (a b) -> p a b", a=4), t.t[0:NB].rearrange("p (a b) -> p a b", a=4),
                       bs0.t[:, :].unsqueeze(2).to_broadcast([NB, 4, 128]), ALU.add, [t.r, bs0.r], [t.r])
                    TT("pool", oc.t[0:NB], t.t[0:NB], gu.t[0:NB], ALU.mult, [t.r, gu.r], [oc.r])
                epilog_tp(oc.t, oc.r, rows, col0, li, 12, 4)

        def ffn(l, g, st):
            has_dec = (g == NG - 1)
            hid = sbt(st, "hid", [128, NJ, GW], BF16, nres=5)
            with contextlib.ExitStack() as fa:
                wring = Ring([sbt(fa, "wu%d" % i, [128, 2, 16, 128], BF16, nres=2) for i in range(2)])
                U = Ring([sbt(fa, "U%d" % i, [128, 2, 514], F32, nres=2) for i in range(2)])
                tcv = Ring([sbt(fa, "tcv%d" % i, [128, 2, 512], F32, nres=2) for i in range(2)])
                Ud = Ring([sbt(fa, "Ud%d" % i, [128, 2, NB, 3], F32, nres=2) for i in range(2)])
                upc_ring = Ring([sbt(fa, "upc%d" % i, [128, 2, NB]) for i in range(3)])
                upl_ring = Ring([sbt(fa, "upl%d" % i, [128, 2, 2]) for i in range(3)])
                ost_ring = Ring([sbt(fa, "ost%d" % i, [16, 256]) for i in range(4)])
                stt_ = Ring([sbt(fa, "stt%d" % i, [16, 2, 2, 128], F32, nres=2) for i in range(3)])
                if has_dec:
                    S.dma(cs_o[l, :, 0, :], scv[l, :, 1, :])
                fq = fq_gen(l)

                tails = []
                sxm = {}
                udm = {}

                def st_dma(j2):
                    if has_dec and j2 < NJ:
                        sx = stt_.next()
                        for gv in range(2):
                            S.dma(sx.t[:, :, gv, :], scv[l, :, :, gv * DFF + j2 * 128:gv * DFF + (j2 + 1) * 128], writes=[sx.rq[gv]])
                        sxm[j2] = sx

                def st_prep(j2):
                    if has_dec and j2 < NJ:
                        sx = sxm.pop(j2)
                        ud = Ud.next()
                        bt, btr = bank(pring_tp.next())
                        for r in range(2):
                            for gv in range(2):
                                TP(bt[:, (r * 2 + gv) * 16:(r * 2 + gv) * 16 + 16], sx.t[0:16, r, gv, :], 16, [sx.rq[gv]], [btr], sig=(r == 1 and gv == 1))
                        for gv in range(2):
                            for r in range(2):
                                CP("dve", ud.t[:, gv, :, r], bt[:, (r * 2 + gv) * 16:(r * 2 + gv) * 16 + 16], [btr], [ud.rq[gv]])
                        udm[j2] = ud

                def fcompute(j, wb):
                    if j == 0:
                        st_dma(0); st_dma(1); st_prep(0)
                    allb = []
                    for (c0, n, lis) in moving(g):
                        bks = []
                        if n != 512:
                            bkd, bkdr = bank(pring_x.next())
                        for gv in range(2):
                            if n == 512:
                                bk, br = bank(pring_acc.next())
                            else:
                                bk, br = bkd[:, gv * 32:gv * 32 + 32], bkdr
                            for kc in range(16):
                                MM(bk[:, 0:n], wb.t[:, gv, kc, :], hT.t[:, kc, c0:c0 + n], kc == 0, kc == 15,
                                   [wb.rq[gv]] + [hT.rq[i] for i in lis], [br])
                            bks.append((bk, br))
                        allb.append((n, bks))
                    for t_ in tails:
                        t_()
                    del tails[:]
                    for (n, bks) in allb:
                        if n == 512:
                            u = U.next(); tc_ = tcv.next()
                            for gv in range(2):
                                jj = gv * NJ + j
                                bk, br = bks[gv]
                                if g == 0:
                                    S.op("pool", lambda e, u=u, gv=gv: e.memset(u.t[:, gv, 0:2], 0.0), writes=[u.rq[gv]])
                                else:
                                    CP("pool", u.t[:, gv, 0:2], carry.t[:, jj, :], [carry.rq[jj]], [u.rq[gv]])
                                CP("act", u.t[:, gv, 2:514], bk[:, 0:512], [br], [u.rq[gv]])
                                CP("pool", carry.t[:, jj, :], u.t[:, gv, 512:514], [u.rq[gv]], [carry.rq[jj]])
                                if g == NG - 1:
                                    if gv == 0:
                                        upl = upl_ring.next()
                                    CP("pool", upl.t[:, gv, :], u.t[:, gv, 512:514], [u.rq[gv]], [upl.r])
                                    if gv == 1:
                                        def tail_p(upl=upl, j=j):
                                            bt3, bt3r = bank(pring_tp.next())
                                            for g2 in range(2):
                                                TP(bt3[0:2, g2 * 128:(g2 + 1) * 128], upl.t[:, g2, :], 128, [upl.r], [bt3r], sig=(g2 == 1))
                                            ost = ost_ring.next()
                                            CP("dve", ost.t[0:2, :], bt3[0:2, 0:256], [bt3r], [ost.r])
                                            S.dma(cp_o[l].rearrange("r (gg f) -> r gg f", gg=2)[:, :, j * 128:(j + 1) * 128],
                                                  ost.t[0:2, :].rearrange("r (gg f) -> r gg f", gg=2), reads=[ost.r], q=STQ)
                                        tails.append(tail_p)
                                TS("dve", tc_.t[:, gv, :], u.t[:, gv, 0:512], cw.t[:, 0, jj:jj + 1], cb.t[:, jj:jj + 1], ALU.mult, ALU.add, [u.rq[gv], cw.r, cb.r], [tc_.rq[gv]])
                                STT(tc_.t[:, gv, :], u.t[:, gv, 1:513], cw.t[:, 1, jj:jj + 1], tc_.t[:, gv, :], ALU.mult, ALU.add, [u.rq[gv], cw.r, tc_.rq[gv]], [tc_.rq[gv]])
                                STT(tc_.t[:, gv, :], u.t[:, gv, 2:514], cw.t[:, 2, jj:jj + 1], tc_.t[:, gv, :], ALU.mult, ALU.add, [u.rq[gv], cw.r, tc_.rq[gv]], [tc_.rq[gv]])
                            ACT(tc_.t[:, 0, :], tc_.t[:, 0, :], AF.Silu, [tc_.rq[0]], [tc_.rq[0]])
                            TT("pool", hid.t[:, j, 0:512], tc_.t[:, 0, :], tc_.t[:, 1, :], ALU.mult, tc_.rq, hid.rq[0:4])
                        else:
                            ud = udm.pop(j); upc = upc_ring.next()
                            tc_ = tcv.next()
                            for gv in range(2):
                                jj = gv * NJ + j
                                bk, br = bks[gv]
                                CP("act", ud.t[:, gv, :, 2], bk[:, 0:NB], [br], [ud.rq[gv]])
                                CP("act", upc.t[:, gv, :], bk[:, 0:NB], [br], [upc.r])
                                TS("dve", tc_.t[:, gv, 0:NB], ud.t[:, gv, :, 0], cw.t[:, 0, jj:jj + 1], cb.t[:, jj:jj + 1], ALU.mult, ALU.add, [ud.rq[gv], cw.r, cb.r], [tc_.rq[gv]])
                                STT(tc_.t[:, gv, 0:NB], ud.t[:, gv, :, 1], cw.t[:, 1, jj:jj + 1], tc_.t[:, gv, 0:NB], ALU.mult, ALU.add, [ud.rq[gv], cw.r, tc_.rq[gv]], [tc_.rq[gv]])
                                STT(tc_.t[:, gv, 0:NB], ud.t[:, gv, :, 2], cw.t[:, 2, jj:jj + 1], tc_.t[:, gv, 0:NB], ALU.mult, ALU.add, [ud.rq[gv], cw.r, tc_.rq[gv]], [tc_.rq[gv]])

                            def tail_d(upc=upc, j=j):
                                bt2, bt2r = bank(pring_tp.next())
                                for gv in range(2):
                                    TP(bt2[0:NB, gv * 128:(gv + 1) * 128], upc.t[:, gv, :], 128, [upc.r], [bt2r], sig=(gv == 1))
                                ost = ost_ring.next()
                                CP("dve", ost.t[0:NB, :], bt2[0:NB, 0:256], [bt2r], [ost.r])
                                S.dma(cs_o[l, :, 1, :].rearrange("b (gg f) -> b gg f", gg=2)[:, :, j * 128:(j + 1) * 128],
                                      ost.t[0:NB, :].rearrange("b (gg f) -> b gg f", gg=2), reads=[ost.r], q=STQ)
                            tails.append(tail_d)
                            ACT(tc_.t[:, 0, 0:NB], tc_.t[:, 0, 0:NB], AF.Silu, [tc_.rq[0]], [tc_.rq[0]])
                            TT("pool", hid.t[:, j, 512:528], tc_.t[:, 0, 0:NB], tc_.t[:, 1, 0:NB], ALU.mult, tc_.rq, [hid.rq[4]])
                    st_dma(j + 2)
                    st_prep(j + 1)
                run_slabs(NJ, fq, fcompute, wring, key=("ffnA", l, g), nxt=(("ffnB", l, g), rq_gen(w_down[l], NJ, 256, 8)(0)))
                for t_ in tails:
                    t_()
                del tails[:]
            S.barrier()
            mark("ffnA")
            with contextlib.ExitStack() as fb:
                resid_update(l, g, False, w_down[l], NJ, hid, fb, ncols=256, qa=8, key=("ffnB", l, g),
                             nxt=(("ple", l, g), rq_gen(w_pg[l], 16, 512, 4, lambda ds: [wq(w_pp[l, :, ds * 512:(ds + 1) * 512], 2, 512, None, None)])(0)))
                S.barrier()

        def ple(l, g, st):
            pT = sbt(st, "pT", [128, 2, GW], BF16, nres=5)
            pl = Ring([sbt(st, "pl%d" % i, [128, PLE]) for i in range(2)])
            for (gt, row0, rows, col0, li) in group_tiles(g):
                p_ = pl.next()
                src = pp[l, row0:row0 + rows, :] if gt < 16 else psm[l]
                S.dma(p_.t[0:rows, :], src, writes=[p_.r])
                transposes_to(lambda i: pT.t[:, i, col0:col0 + rows], p_.t, p_.r, rows, 2, pT.rq[li])

            wps = [sbt(st, "wpb%d" % i, [128, 2, 512], BF16) for i in range(2)]

            def xq(ds):
                wp = wps[ds % 2]
                return [wq(w_pp[l, :, ds * 512:(ds + 1) * 512], 2, 512, (lambda wb, wp=wp: wp.t[:, :, :]), (lambda wb, wp=wp: wp.r))]

            def pre(ds, bk, br, tl, xo):
                wp = wps[ds % 2]
                (gt, row0, rows, col0, li) = tl
                bp, bpr = bank(pring_x.next())
                for a in range(2):
                    MM(bp[0:rows, 0:512], pT.t[:, a, col0:col0 + rows], wp.t[:, a, :], a == 0, a == 1, [pT.rq[li], wp.r], [bpr])
                t = tA.next()
                ACT(t.t[0:rows, :], bk[0:rows, 0:512], AF.Sigmoid, [br], [t.r])
                TT("dve", t.t[0:rows, :], t.t[0:rows, :], bp[0:rows, 0:512], ALU.mult, [t.r, bpr], [t.r])
                TT("dve", xo.t[0:rows, :], xo.t[0:rows, :], t.t[0:rows, :], ALU.add, [xo.r, t.r], [xo.r])
            nl, ng = (l, g + 1) if g + 1 < NG else (l + 1, 0)
            nx = (("hgrn", nl, ng), hq_gen(nl)(0)) if nl < DEPTH else None
            resid_update(l, g, l == DEPTH - 1, w_pg[l], 16, hT, st, ncols=512, extra=(xq, pre), key=("ple", l, g), nxt=nx)

        def load_params(l):
            for i, gsrc in enumerate((norm1_g, norm2_g, ple_g)):
                S.dma(gam.t[:, i, :], gsrc[l].rearrange("(a p) -> p a", p=128), writes=[gam.r], allow_slow_non_contiguous=True)
            S.dma(hgbc.t[:], hgn[l].partition_broadcast(128), writes=[hgbc.r])
            S.dma(qgbc.t[:], qng[l].partition_broadcast(128), writes=[qgbc.r])
            S.dma(kgbc.t[:], kng[l].partition_broadcast(128), writes=[kgbc.r])
            S.dma(esink.t[:], sinks[l].partition_broadcast(128), writes=[esink.r])
            ACT(esink.t[:], esink.t[:], AF.Exp, [esink.r], [esink.r])
            S.dma(lgbc.t[:], lng[l].partition_broadcast(128), writes=[lgbc.r])
            S.dma(lbbc.t[:], lnb[l].partition_broadcast(128), writes=[lbbc.r])
            S.dma(bsT.t[:], gbs[l].rearrange("c p -> p c"), writes=[bsT.r], allow_slow_non_contiguous=True)
            S.dma(w00.t[:], gws[l, :, 0, 0].partition_broadcast(16), writes=[w00.r], allow_slow_non_contiguous=True)
            S.dma(bs0.t[:], gbs[l, :, 0].partition_broadcast(16), writes=[bs0.r], allow_slow_non_contiguous=True)
            for r in range(3):
                S.dma(cw.t[:, r, :], conv_w[l, r].rearrange("(j p) -> p j", p=128), writes=[cw.r], allow_slow_non_contiguous=True)
            S.dma(cb.t[:], conv_b[l].rearrange("(j p) -> p j", p=128), writes=[cb.r], allow_slow_non_contiguous=True)
            for c in range(4):
                t = tA.next()
                S.dma(t.t[:, 0:128], gws[l, c], writes=[t.r])
                TT("dve", t.t[:, 0:128], t.t[:, 0:128], tril.t[:], ALU.mult, [t.r, tril.r], [t.r])
                transposes_to(lambda i: WsT.t[:, c, :], t.t, t.r, 128, 1, WsT.r)

        try:
            for l in range(DEPTH):
                chk()
                load_params(l)
                mark("params")
                for g in range(NG):
                    chk()
                    norm_pass(g, 0)
                    mark("norm0")
                    chk()
                    with contextlib.ExitStack() as st:
                        mixT = mixers(l, g, st)
                        S.barrier()
                        mark("gmlp")
                        chk()
                        with contextlib.ExitStack() as st2:
                            resid_update(l, g, False, w_out[l], 16, mixT, st2, key=("wout", l, g), nxt=(("ffnA", l, g), fq_gen(l)(0)))
                            S.barrier()
                            mark("wout")
                    S.barrier()
                    chk()
                    norm_pass(g, 1)
                    mark("norm1")
                    chk()
                    with contextlib.ExitStack() as st:
                        ffn(l, g, st)
                    S.barrier()
                    mark("ffnB")
                    chk()
                    norm_pass(g, 2)
                    mark("norm2")
                    chk()
                    with contextlib.ExitStack() as st:
                        ple(l, g, st)
                        S.barrier()
                    mark("ple")
                S.barrier()
        except _Stop:
            pass
        S.off = False
        S.barrier()
        if KDBG:
            S.dma(dbg_x, xbuf)
            S.barrier()
    return nc


_CACHE = {}
MARKS = []


def _consts():
    i = np.arange(128)
    caus = (i[:, None] <= i[None, :]).astype(np.float32)
    tril = (i[None, :] <= i[:, None]).astype(np.float32)
    lows = (i[:, None] > i[None, :]).astype(np.float32)
    i64 = np.arange(64)
    c64 = (i64[:, None] <= i64[None, :]).astype(np.float32)
    m2 = np.concatenate([np.ones((64, 64), np.float32), c64], axis=0)
    kmask = np.ones((128, 1), np.float32); kmask[0, 0] = 0.0
    mask01 = np.ones((128, GW), np.float32)
    mask01[:, 0:512:128] = 0.0
    mask01[:, 512:] = 0.0
    sel = np.zeros((16, 16, 128), np.float32)
    for b in range(16):
        sel[b, b, :] = 1.0
    return {"c_ident": np.eye(128, dtype=np.float32), "c_caus": caus, "c_tril": tril, "c_lows": lows, "c_m2": m2,
            "c_kmask": kmask, "c_mask01": mask01, "c_sel": sel.reshape(16, 2048)}


def kernel(**inp):
    if "nc" not in _CACHE:
        _CACHE["nc"] = build_program()
    nc = _CACHE["nc"]
    f = lambda a: np.ascontiguousarray(np.asarray(a, dtype=np.float32))
    wnames = ["norm1_g", "w_in", "hgrn_lb_logits", "hgrn_norm_g", "q_norm_g", "k_norm_g", "swa_sinks", "gmlp_ln_g", "gmlp_ln_b",
              "gmlp_ws", "gmlp_bs", "w_out", "norm2_g", "w_up", "conv_w", "conv_b", "w_down", "ple_norm_g", "w_ple_gate", "w_ple_proj"]
    shared = {k: f(inp[k]) for k in wnames}
    shared.update(_consts())
    x_prompt = f(inp["x_prompt"]); x_sample = f(inp["x_sample"])
    st_h = f(inp["state_hgrn"]); c_k = f(inp["cache_swa_k"]); c_v = f(inp["cache_swa_v"])
    st_c = f(inp["state_ffn_conv"]); p_p = f(inp["p_prompt"]); p_s = f(inp["p_sample"])
    in_maps = []
    for c in range(NCORES):
        sl = slice(c * NB, (c + 1) * NB)
        m = dict(shared)
        m["xp"] = x_prompt[c]
        m["xs"] = x_sample[sl, 0]
        m["sh"] = np.ascontiguousarray(st_h[:, sl])
        m["ck"] = np.ascontiguousarray(c_k[:, sl].reshape(DEPTH, NB, 128, 128))
        m["cv"] = np.ascontiguousarray(c_v[:, sl].reshape(DEPTH, NB, 128, 128))
        m["scv"] = np.ascontiguousarray(st_c[:, sl])
        m["pp"] = np.ascontiguousarray(p_p[:, c])
        m["psm"] = np.ascontiguousarray(p_s[:, sl, 0])
        in_maps.append(m)
    res = run_bass_kernel_spmd(nc, in_maps, core_ids=list(range(NCORES)))
    R = res.results
    cat = lambda k, ax: np.concatenate([np.asarray(R[c][k]) for c in range(NCORES)], axis=ax)
    stk = lambda k, ax: np.stack([np.asarray(R[c][k]) for c in range(NCORES)], axis=ax)
    y_prompt = stk("y_p", 0)
    y_sample = cat("y_s", 0).reshape(NCORES * NB, 1, D)
    hgrn_p = stk("hp_o", 1)
    hgrn_s = cat("hs_o", 1)
    kp = stk("kp_o", 1).reshape(DEPTH, NCORES, 128, 2, 64)
    vp = stk("vp_o", 1).reshape(DEPTH, NCORES, 128, 2, 64)
    ks = cat("ks_o", 1).reshape(DEPTH, NCORES * NB, 128, 2, 64)
    vs = cat("vs_o", 1).reshape(DEPTH, NCORES * NB, 128, 2, 64)
    gv = cat("gv_o", 1).reshape(DEPTH, NCORES * NB, 1, 4, 128)
    cp = stk("cp_o", 1)
    cs = cat("cs_o", 1)
    return tuple(np.ascontiguousarray(a, dtype=np.float32) for a in
                 (y_prompt, y_sample, hgrn_p, hgrn_s, kp, vp, ks, vs, gv, cp, cs))
```

```python
import contextlib
import numpy as np
import concourse.bass as bass
import concourse.mybir as mybir
from concourse.bass_utils import run_bass_kernel_spmd

F32 = mybir.dt.float32
BF16 = mybir.dt.bfloat16
AF = mybir.ActivationFunctionType
ALU = mybir.AluOpType
AX = mybir.AxisListType

NCORES = 8
D = 2048
T = 2048
NB = 16
DEPTH = 2
DIN = 5888
DFF = 5632
NJ = 44
PLE = 256
EPS = 1e-6
NG = 4
GW = 528
GELU_C = 1.5957691216057308


class Res:
    __slots__ = ("w", "rs")

    def __init__(self):
        self.w = None
        self.rs = {}


class Sched:
    ENG = ("pe", "act", "dve", "pool", "sp")

    def __init__(self, nc, st, ndsem=32):
        self.nc = nc
        self.engs = {"pe": nc.tensor, "act": nc.scalar, "dve": nc.vector, "pool": nc.gpsimd, "sp": nc.sync}
        self.sems = {}
        for e in self.ENG:
            self.sems[e] = st.enter_context(nc.semaphore("s_" + e))
        for i in range(ndsem):
            self.sems[("d", i)] = st.enter_context(nc.semaphore("s_d%d" % i))
        self.cnt = {e: 0 for e in self.ENG}
        self.seen = {e: {} for e in self.ENG}
        self.pend_r = []
        self.pend_w = []
        self.nd = ndsem
        self.duse = [0] * ndsem
        self.di = 0
        self.nins = 0
        self.npe = 0
        self.off = False

    def _wait(self, e, k, v):
        self.engs[e].wait_ge(self.sems[k], v)

    def _need(self, e, ev, waits):
        if ev is None:
            return
        k, v = ev
        if self.seen[e].get(k, 0) >= v:
            return
        if waits.get(k, 0) < v:
            waits[k] = v

    def _deps(self, e, reads, writes):
        waits = {}
        for r in reads:
            self._need(e, r.w, waits)
        for w in writes:
            if w.w is not None and w.w[0] != e:
                self._need(e, w.w, waits)
            for k, v in w.rs.items():
                if k != e:
                    self._need(e, (k, v), waits)
        for k, v in waits.items():
            self.seen[e][k] = v
            self._wait(e, k, v)

    def _commit(self, ev, reads, writes):
        for r in reads:
            if r.rs.get(ev[0], 0) < ev[1]:
                r.rs[ev[0]] = ev[1]
        for w in writes:
            w.w = ev
            w.rs = {}

    def op(self, e, fn, reads=(), writes=(), sig=True):
        if self.off:
            return
        reads = list(reads)
        writes = list(writes)
        if e != "pe":
            assert not self.pend_r and not self.pend_w, "PE group left open"
        self._deps(e, reads, writes)
        self.nins += 1
        if e == "pe":
            self.npe += 1
        ins = fn(self.engs[e])
        if sig:
            self.cnt[e] += 1
            ins.then_inc(self.sems[e], 1)
            ev = (e, self.cnt[e])
            if e == "pe":
                reads = reads + self.pend_r
                writes = writes + self.pend_w
                self.pend_r = []
                self.pend_w = []
            self._commit(ev, reads, writes)
        else:
            assert e == "pe"
            self.pend_r += reads
            self.pend_w += writes

    def dma(self, out, in_, reads=(), writes=(), q="sp", **kw):
        if self.off:
            return
        reads = list(reads)
        writes = list(writes)
        assert not self.pend_r and not self.pend_w
        i = self.di % self.nd
        self.di += 1
        k = ("d", i)
        prev = self.duse[i]
        self.duse[i] += 1
        self._deps(q, reads, writes)
        if prev > 0 and self.seen[q].get(k, 0) < 16 * prev:
            self.seen[q][k] = 16 * prev
            self._wait(q, k, 16 * prev)
        self.nins += 1
        self.engs[q].dma_start(out=out, in_=in_, **kw).then_inc(self.sems[k], 16)
        self._commit((k, 16 * self.duse[i]), reads, writes)

    def barrier(self):
        if self.off:
            return
        assert not self.pend_r and not self.pend_w
        for e in self.ENG:
            for o in self.ENG:
                if o != e and self.cnt[o] > self.seen[e].get(o, 0):
                    self.seen[e][o] = self.cnt[o]
                    self._wait(e, o, self.cnt[o])
            for i in range(self.nd):
                k = ("d", i)
                v = 16 * self.duse[i]
                if v > self.seen[e].get(k, 0):
                    self.seen[e][k] = v
                    self._wait(e, k, v)


class Buf:
    def __init__(self, t, nres=1):
        self.t = t
        self.r = Res()
        self.rq = [Res() for _ in range(nres)]


class Ring:
    def __init__(self, bufs):
        self.bufs = bufs
        self.i = 0

    def next(self):
        b = self.bufs[self.i % len(self.bufs)]
        self.i += 1
        return b


def build_program():
    nc = bass.Bass("TRN2", target_bir_lowering=False)

    def din(name, shape):
        return nc.dram_tensor(name, list(shape), F32, kind="ExternalInput").ap()

    def dout(name, shape):
        return nc.dram_tensor(name, list(shape), F32, kind="ExternalOutput").ap()

    xp = din("xp", [T, D]); xs = din("xs", [NB, D])
    sh = din("sh", [DEPTH, NB, 8, 128, 128])
    ck = din("ck", [DEPTH, NB, 128, 128]); cv = din("cv", [DEPTH, NB, 128, 128])
    scv = din("scv", [DEPTH, NB, 2, 2 * DFF])
    pp = din("pp", [DEPTH, T, PLE]); psm = din("psm", [DEPTH, NB, PLE])
    norm1_g = din("norm1_g", [DEPTH, D]); w_in = din("w_in", [DEPTH, D, DIN])
    lbl = din("hgrn_lb_logits", [DEPTH, 1024]); hgn = din("hgrn_norm_g", [DEPTH, 128])
    qng = din("q_norm_g", [DEPTH, 64]); kng = din("k_norm_g", [DEPTH, 64])
    sinks = din("swa_sinks", [DEPTH, 8])
    lng = din("gmlp_ln_g", [DEPTH, 512]); lnb = din("gmlp_ln_b", [DEPTH, 512])
    gws = din("gmlp_ws", [DEPTH, 4, 128, 128]); gbs = din("gmlp_bs", [DEPTH, 4, 128])
    w_out = din("w_out", [DEPTH, D, D]); norm2_g = din("norm2_g", [DEPTH, D])
    w_up = din("w_up", [DEPTH, D, 2 * DFF]); conv_w = din("conv_w", [DEPTH, 3, 2 * DFF])
    conv_b = din("conv_b", [DEPTH, 2 * DFF]); w_down = din("w_down", [DEPTH, DFF, D])
    ple_g = din("ple_norm_g", [DEPTH, D]); w_pg = din("w_ple_gate", [DEPTH, D, D])
    w_pp = din("w_ple_proj", [DEPTH, PLE, D])
    c_ident = din("c_ident", [128, 128]); c_caus = din("c_caus", [128, 128])
    c_tril = din("c_tril", [128, 128]); c_lows = din("c_lows", [128, 128])
    c_m2 = din("c_m2", [128, 64]); c_kmask = din("c_kmask", [128, 1])
    c_mask01 = din("c_mask01", [128, GW]); c_sel = din("c_sel", [16, 16 * 128])

    y_p = dout("y_p", [T, D]); y_s = dout("y_s", [NB, D])
    hp_o = dout("hp_o", [DEPTH, 8, 128, 128]); hs_o = dout("hs_o", [DEPTH, NB, 8, 128, 128])
    kp_o = dout("kp_o", [DEPTH, 128, 128]); vp_o = dout("vp_o", [DEPTH, 128, 128])
    ks_o = dout("ks_o", [DEPTH, NB, 128, 128]); vs_o = dout("vs_o", [DEPTH, NB, 128, 128])
    gv_o = dout("gv_o", [DEPTH, NB, 512])
    cp_o = dout("cp_o", [DEPTH, 2, 2 * DFF]); cs_o = dout("cs_o", [DEPTH, NB, 2, 2 * DFF])
    xbuf = nc.dram_tensor("xbuf", [T + NB, D], F32, kind="Internal").ap()
    import os
    KDBG = False
    if KDBG:
        dbg_x = nc.dram_tensor("dbg_x", [T + NB, D], F32, kind="ExternalOutput").ap()
        dbg_mix = nc.dram_tensor("dbg_mix", [NG, 128, 16, GW], BF16, kind="ExternalOutput").ap()

    with contextlib.ExitStack() as top:
        S = Sched(nc, top)

        uid = [0]

        def sbt(st, name, shape, dt=F32, nres=1):
            uid[0] += 1
            return Buf(st.enter_context(nc.sbuf_tensor("%s_%d" % (name, uid[0]), list(shape), dt)), nres)

        import os
        lim = 1000000
        cnt_ = [0]

        class _Stop(Exception):
            pass

        def mark(label):
            MARKS.append((label, S.npe))

        def chk():
            cnt_[0] += 1
            if cnt_[0] > lim:
                S.off = True

        def ACT(out, in_, func, reads, writes, scale=1.0, bias=None, accum=None):
            kw = {}
            if bias is not None:
                kw["bias"] = bias
            if accum is not None:
                kw["accum_out"] = accum
            S.op("act", lambda e: e.activation(out=out, in_=in_, func=func, scale=scale, **kw), reads, writes)

        def TS(eng, out, in0, s1, s2, op0, op1, reads, writes):
            if op1 is None:
                S.op(eng, lambda e: e.tensor_scalar(out=out, in0=in0, scalar1=s1, scalar2=None, op0=op0), reads, writes)
            else:
                S.op(eng, lambda e: e.tensor_scalar(out=out, in0=in0, scalar1=s1, scalar2=s2, op0=op0, op1=op1), reads, writes)

        def TT(eng, out, in0, in1, op, reads, writes):
            S.op(eng, lambda e: e.tensor_tensor(out=out, in0=in0, in1=in1, op=op), reads, writes)

        def STT(out, in0, scalar, in1, op0, op1, reads, writes):
            S.op("dve", lambda e: e.scalar_tensor_tensor(out=out, in0=in0, scalar=scalar, in1=in1, op0=op0, op1=op1), reads, writes)

        def CP(eng, out, in_, reads, writes):
            if eng == "act":
                ACT(out, in_, AF.Copy, reads, writes)
            else:
                S.op(eng, lambda e: e.tensor_copy(out=out, in_=in_), reads, writes)

        def RSQ(out, in_, scale, rows, cols, res):
            TS("pool", out, in_, scale, EPS, ALU.mult, ALU.add, [res], [res])
            TT("pool", out, out, mhalf.t[0:rows, 0:cols], ALU.pow, [res, mhalf.r], [res])

        def MM(out, lhsT, rhs, start, stop, reads, writes, sig=None):
            S.op("pe", lambda e: e.matmul(out=out, lhsT=lhsT, rhs=rhs, start=start, stop=stop), reads, writes,
                 sig=(stop if sig is None else sig))

        def TP(out, in_, rows, reads, writes, sig=True):
            S.op("pe", lambda e: e.transpose(out=out, in_=in_, identity=ident.t[0:rows, 0:rows]), reads + [ident.r], writes, sig=sig)

        ident = sbt(top, "ident", [128, 128])
        caus = sbt(top, "caus", [128, 128]); tril = sbt(top, "tril", [128, 128]); lows = sbt(top, "lows", [128, 128])
        m2 = sbt(top, "m2", [128, 64]); kmask = sbt(top, "kmask", [128, 1])
        mask01 = sbt(top, "mask01", [128, GW])
        hT = sbt(top, "hT", [128, 16, GW], BF16, nres=5)
        Sf = sbt(top, "Sf", [128, 8, 128], F32, nres=8); Sb = sbt(top, "Sb", [128, 8, 128], BF16, nres=8)
        KTall = sbt(top, "KTall", [128, T], BF16, nres=16)
        V1all = sbt(top, "V1all", [128, 16, 2, 65], BF16, nres=16)
        carry = sbt(top, "carry", [128, 2 * NJ, 2], F32, nres=2 * NJ)
        gam = sbt(top, "gam", [128, 3, 16]); lbt = sbt(top, "lbt", [128, 2, 8]); oml = sbt(top, "oml", [128, 2, 8])
        lgt = sbt(top, "lgt", [128, 2, 8])
        hgbc = sbt(top, "hgbc", [128, 128]); qgbc = sbt(top, "qgbc", [128, 64]); kgbc = sbt(top, "kgbc", [128, 64])
        esink = sbt(top, "esink", [128, 8]); lgbc = sbt(top, "lgbc", [128, 512]); lbbc = sbt(top, "lbbc", [128, 512])
        WsT = sbt(top, "WsT", [128, 4, 128], BF16); bsT = sbt(top, "bsT", [128, 4])
        w00 = sbt(top, "w00", [16, 4]); bs0 = sbt(top, "bs0", [16, 4])
        cw = sbt(top, "cw", [128, 3, 2 * NJ]); cb = sbt(top, "cb", [128, 2 * NJ])
        mhalf = sbt(top, "mhalf", [128, 16])
        stat = Ring([sbt(top, "stat%d" % i, [128, 16]) for i in range(4)])
        junk = sbt(top, "junk", [128, 128], BF16)
        stg_bufs = [sbt(top, "stg%d" % i, [128, 2048], F32, nres=4) for i in range(6)]
        for b_ in stg_bufs:
            b_.pending = False
        stg_ring = Ring(stg_bufs)
        cast_i = [0]

        def stg_take():
            sg = stg_ring.next()
            assert not sg.pending, "staging ring overrun"
            sg.pending = True
            return sg

        def cast_eng():
            cast_i[0] += 1
            return "act" if cast_i[0] % 2 else "dve"
        xo_ring = Ring([sbt(top, "xo%d" % i, [128, 512]) for i in range(5)])
        STQ = "pool"
        tA = Ring([sbt(top, "tA%d" % i, [128, 512]) for i in range(2)])
        pbanks = []
        for i in range(4):
            p = top.enter_context(nc.psum_tensor("ps%d" % i, [128, 1024], F32))
            for hh in range(2):
                pbanks.append((p, hh * 512, Res()))

        def bank(i):
            p, off, r = pbanks[i]
            return p[:, off:off + 512], r

        pring_acc = Ring([0, 1, 2, 3])
        pring_tp = Ring([4, 5])
        pring_x = Ring([6, 7])

        xb_r = [[Res() for _ in range(8)] for _ in range(17)]

        def cdma(buf, src, **kw):
            S.dma(buf.t[:], src, writes=[buf.r], **kw)

        S.op("pool", lambda e: e.memset(mhalf.t[:], -0.5), writes=[mhalf.r])
        cdma(ident, c_ident); cdma(caus, c_caus); cdma(tril, c_tril); cdma(lows, c_lows)
        cdma(m2, c_m2); cdma(kmask, c_kmask); cdma(mask01, c_mask01)
        for t in range(16):
            S.dma(xbuf[t * 128:(t + 1) * 128, :], xp[t * 128:(t + 1) * 128, :], writes=xb_r[t])
        S.dma(xbuf[T:T + NB, :], xs, writes=xb_r[16])
        S.dma(lgt.t[:], lbl.rearrange("l (h p) -> p l h", p=128), writes=[lgt.r], allow_slow_non_contiguous=True)
        S.op("dve", lambda e: e.memset(lbt.t[:], 0.0), writes=[lbt.r])
        TT("dve", lgt.t[:, 1, :], lgt.t[:, 1, :], lgt.t[:, 0, :], ALU.subtract, [lgt.r], [lgt.r])
        ACT(lbt.t[:, 1, :], lgt.t[:, 1, :], AF.Sigmoid, [lgt.r, lbt.r], [lbt.r])
        TS("dve", oml.t[:], lbt.t[:], -1.0, 1.0, ALU.mult, ALU.add, [lbt.r], [oml.r])
        S.op("dve", lambda e: e.memset(V1all.t[:], 1.0), writes=V1all.rq)

        def group_tiles(g):
            tl = [(g * 4 + i, (g * 4 + i) * 128, 128, i * 128, i) for i in range(4)]
            if g == NG - 1:
                tl.append((16, T, NB, 512, 4))
            return tl

        def moving(g):
            mv = [(0, 512, [0, 1, 2, 3])]
            if g == NG - 1:
                mv.append((512, NB, [4]))
            return mv

        def stage_cast(src, dst, dres, a, c):
            sg = stg_take()
            view = sg.t[:, 0:a * c].rearrange("p (a c) -> p a c", a=a)
            S.dma(view, src, writes=[sg.rq[0]])
            CP(cast_eng(), dst, view, [sg.rq[0]], [dres])
            sg.pending = False

        def wq(w_rows, a, nc_, dst, res):
            return {"view": lambda sg: sg.t[:, 0:a * nc_].rearrange("p (a c) -> p a c", a=a),
                    "dmas": lambda view: [(view, w_rows.rearrange("(a p) c -> p a c", p=128))],
                    "dst": dst, "res": res}

        def rq_gen(w_ap, kcn, ncols, qa, extra_q=None):
            nq_ = (kcn + qa - 1) // qa

            def quarters_(ds):
                ql = []
                for q in range(nq_):
                    a = min(qa, kcn - qa * q)
                    ql.append(wq(w_ap[q * qa * 128:(q * qa + a) * 128, ds * ncols:(ds + 1) * ncols], a, ncols,
                                 (lambda wb, q=q, a=a: wb.t[:, qa * q:qa * q + a, :]), (lambda wb, q=q: wb.rq[q])))
                if extra_q is not None:
                    ql += extra_q(ds)
                return ql
            return quarters_

        def fq_gen(l):
            def fq_(j):
                return [wq(w_up[l, :, gv * DFF + j * 128:gv * DFF + (j + 1) * 128], 16, 128,
                           (lambda wb, gv=gv: wb.t[:, gv, :, :]), (lambda wb, gv=gv: wb.rq[gv])) for gv in range(2)]
            return fq_

        def hq_gen(l):
            def hq_(h):
                ql = []
                for q in range(4):
                    def dmas(view, q=q):
                        return [(view[:, :, sgm, :],
                                 w_in[l, q * 512:(q + 1) * 512, sgm * 1024 + h * 128:sgm * 1024 + (h + 1) * 128].rearrange("(a p) c -> p a c", p=128))
                                for sgm in range(4)]
                    ql.append({"view": lambda sg: sg.t[:, 0:2048].rearrange("p (a s c) -> p a s c", a=4, s=4),
                               "dmas": dmas,
                               "dst": (lambda wb, q=q: wb.t[:, 4 * q:4 * q + 4, :].rearrange("p a (s c) -> p a s c", s=4)),
                               "res": (lambda wb, q=q: wb.rq[q])})
                return ql
            return hq_

        PRE = {}

        def prefetch(key, qlist):
            lst = []
            for q in qlist:
                sg = stg_take()
                view = q["view"](sg)
                for k_, (dsub, src) in enumerate(q["dmas"](view)):
                    S.dma(dsub, src, writes=[sg.rq[k_]])
                lst.append((sg, view))
            PRE[key] = lst

        def run_slabs(n, quarters, compute, wring, key=None, nxt=None):
            qs = {}
            stg = {}
            wbs = {}

            def Q(i):
                if i not in qs:
                    qs[i] = quarters(i)
                return qs[i]

            def issue(i, qi):
                q = Q(i)[qi]
                sg = stg_take()
                view = q["view"](sg)
                for k_, (dsub, src) in enumerate(q["dmas"](view)):
                    S.dma(dsub, src, writes=[sg.rq[k_]])
                stg[(i, qi)] = (sg, view)

            def cast(i, qi):
                q = Q(i)[qi]
                sg, view = stg.pop((i, qi))
                CP(cast_eng(), q["dst"](wbs[i]), view, sg.rq, [q["res"](wbs[i])])
                sg.pending = False

            def cast_and_prefetch(i):
                wbs[i] = wring.next()
                nq = len(Q(i))
                nq2 = len(Q(i + 1)) if i + 1 < n else 0
                for qi in range(max(nq, nq2)):
                    if qi < nq:
                        cast(i, qi)
                    if qi < nq2:
                        issue(i + 1, qi)

            if key is not None and key in PRE:
                for qi, ent in enumerate(PRE.pop(key)):
                    stg[(0, qi)] = ent
            else:
                for qi in range(len(Q(0))):
                    issue(0, qi)
            cast_and_prefetch(0)
            for i in range(n):
                if i + 1 < n:
                    cast_and_prefetch(i + 1)
                elif nxt is not None and not S.off:
                    prefetch(nxt[0], nxt[1])
                compute(i, wbs.pop(i))

        def transposes_to(dst_fn, src_buf, src_res, rows, nblk, dres, scale_fn=None):
            for b0 in range(0, nblk, 4):
                bi = pring_tp.next()
                bk, br = bank(bi)
                n = min(4, nblk - b0)
                for a in range(n):
                    TP(bk[:, a * 128:a * 128 + rows], src_buf[0:rows, (b0 + a) * 128:(b0 + a + 1) * 128], rows,
                       [src_res], [br], sig=(a == n - 1))
                for a in range(n):
                    i = b0 + a
                    eng = "act" if i % 2 == 0 else "dve"
                    src = bk[:, a * 128:a * 128 + rows]
                    if scale_fn is None:
                        CP(eng, dst_fn(i), src, [br], [dres])
                    elif eng == "act":
                        ACT(dst_fn(i), src, AF.Copy, [br, gam.r], [dres], scale=scale_fn(i))
                    else:
                        TS("dve", dst_fn(i), src, scale_fn(i), None, ALU.mult, None, [br, gam.r], [dres])

        def norm_pass(g, which):
            with contextlib.ExitStack() as ns:
                xt_ring = Ring([sbt(ns, "xt%d" % i, [128, D]) for i in range(2)])
                nj = sbt(ns, "nj", [128, D], BF16)
                for (gt, row0, rows, col0, li) in group_tiles(g):
                    xt = xt_ring.next()
                    S.dma(xt.t[0:rows, :], xbuf[row0:row0 + rows, :], reads=xb_r[gt], writes=[xt.r])
                    st = stat.next()
                    ACT(nj.t[0:rows, :], xt.t[0:rows, :], AF.Square, [xt.r], [nj.r, st.r], accum=st.t[0:rows, 0:1])
                    RSQ(st.t[0:rows, 2:3], st.t[0:rows, 0:1], 1.0 / D, rows, 1, st.r)
                    TS("dve", xt.t[0:rows, :], xt.t[0:rows, :], st.t[0:rows, 2:3], None, ALU.mult, None, [xt.r, st.r], [xt.r])
                    transposes_to(lambda i: hT.t[:, i, col0:col0 + rows], xt.t, xt.r, rows, 16, hT.rq[li],
                                  scale_fn=lambda i: gam.t[:, which, i:i + 1])
            S.barrier()

        def proj_tok(g, wb, ncols, kcn, lhs_buf, evac):
            for tl in group_tiles(g):
                (gt, row0, rows, col0, li) = tl
                bk, br = bank(pring_acc.next())
                for kc in range(kcn):
                    MM(bk[0:rows, 0:ncols], lhs_buf.t[:, kc, col0:col0 + rows], wb.t[:, kc, 0:ncols], kc == 0, kc == kcn - 1,
                       [lhs_buf.rq[li], wb.rq[kc // 4]], [br])
                evac(bk, br, tl)

        def resid_update(l, g, last, w_ap, kcn, lhs_buf, st_scope, ncols=512, extra=None, qa=4, key=None, nxt=None):
            nq = (kcn + qa - 1) // qa
            wring = Ring([sbt(st_scope, "wr%d" % i, [128, kcn, ncols], BF16, nres=nq) for i in range(2)])
            nsl = D // ncols

            quarters = rq_gen(w_ap, kcn, ncols, qa, extra[0] if extra is not None else None)

            def compute(ds, wb):
                tls = group_tiles(g)
                xos = {}
                blk_of = lambda gt: [xb_r[gt][b] for b in range(ds * ncols // 256, (ds + 1) * ncols // 256)]

                def load(i):
                    (gt, row0, rows, col0, li) = tls[i]
                    xo = xo_ring.next()
                    S.dma(xo.t[0:rows, 0:ncols], xbuf[row0:row0 + rows, ds * ncols:(ds + 1) * ncols], reads=blk_of(gt), writes=[xo.r], q=STQ)
                    xos[i] = xo
                PF = 3
                for i in range(min(PF, len(tls))):
                    load(i)
                for i, tl in enumerate(tls):
                    (gt, row0, rows, col0, li) = tl
                    bk, br = bank(pring_acc.next())
                    for kc in range(kcn):
                        MM(bk[0:rows, 0:ncols], lhs_buf.t[:, kc, col0:col0 + rows], wb.t[:, kc, 0:ncols], kc == 0, kc == kcn - 1,
                           [lhs_buf.rq[li], wb.rq[kc // qa]], [br])
                    if i + PF < len(tls):
                        load(i + PF)
                    xo = xos.pop(i)
                    if extra is None:
                        TT("dve", xo.t[0:rows, 0:ncols], xo.t[0:rows, 0:ncols], bk[0:rows, 0:ncols], ALU.add, [xo.r, br], [xo.r])
                    else:
                        extra[1](ds, bk, br, tl, xo)
                    if last:
                        dst = (y_p[row0:row0 + rows, ds * ncols:(ds + 1) * ncols] if gt < 16
                               else y_s[:, ds * ncols:(ds + 1) * ncols])
                        S.dma(dst, xo.t[0:rows, 0:ncols], reads=[xo.r], q=STQ)
                    else:
                        S.dma(xbuf[row0:row0 + rows, ds * ncols:(ds + 1) * ncols], xo.t[0:rows, 0:ncols], reads=[xo.r], writes=blk_of(gt), q=STQ)
            run_slabs(nsl, quarters, compute, wring, key=key, nxt=nxt)

        def mixers(l, g, st):
            has_dec = (g == NG - 1)
            tiles = group_tiles(g)
            mvs = moving(g)
            if False:
                has_dec = False
                tiles = tiles[0:4]
                mvs = mvs[0:1]
            mixT = sbt(st, "mixT", [128, 16, GW], BF16, nres=5)
            wring = Ring([sbt(st, "wi%d" % i, [128, 16, 512], BF16, nres=4) for i in range(2)])

            def epilog_tp(src, sres, rows, col0, li, fc0, nblk):
                transposes_to(lambda i: mixT.t[:, fc0 + i, col0:col0 + rows], src, sres, rows, nblk, mixT.rq[li])

            with contextlib.ExitStack() as hs:
                QT = sbt(hs, "QT", [128, GW]); FS = sbt(hs, "FS", [128, GW]); KTt = sbt(hs, "KTt", [128, GW])
                LF = sbt(hs, "LF", [128, GW]); Bc = sbt(hs, "Bc", [128, GW])
                Vb = sbt(hs, "Vb", [128, 5, 128], BF16, nres=5); GS = sbt(hs, "GS", [128, 5, 128], F32, nres=5)
                Vd = sbt(hs, "Vd", [16, 128]); sel = sbt(hs, "sel", [16, 16 * 128])
                e_ring = Ring([sbt(hs, "er%d" % i, [128, 128]) for i in range(6)])
                b_ring = Ring([sbt(hs, "br%d" % i, [128, 128], BF16) for i in range(30)])
                SC = [sbt(hs, "SC%d" % i, [128, 128], BF16) for i in range(6)]
                d_ring = Ring([sbt(hs, "d128%d" % i, [128, 4]) for i in range(2)])
                nb_ring = Ring([sbt(hs, "nb%d" % i, [128, 2]) for i in range(4)])
                oa_ring = Ring([sbt(hs, "oa%d" % i, [128, 128]) for i in range(2)])
                oa4 = [sbt(hs, "oaq%d" % i, [128, 128]) for i in range(4)]
                ei_ = [0]
                s0_ring = Ring([sbt(hs, "s0%d" % i, [128, 128]) for i in range(3)])
                sn_ring = Ring([sbt(hs, "sn%d" % i, [128, 128]) for i in range(3)])
                t1_ring = Ring([sbt(hs, "t1%d" % i, [128, 128]) for i in range(2)])
                snb_ring = Ring([sbt(hs, "snb%d" % i, [128, 128], BF16) for i in range(8)])
                QM = sbt(hs, "QM", [128, 16 * 16], BF16)
                for scb in SC:
                    S.op("pool", lambda e, scb=scb: e.memset(scb.t[:], 0.0), writes=[scb.r])
                S.op("pool", lambda e: e.memset(QM.t[:], 0.0), writes=[QM.r])
                KHX = 0
                if has_dec and not (KHX & 1):
                    S.dma(sel.t[:], c_sel, writes=[sel.r])
                sci = [0]

                hq = hq_gen(l)

                PB = [dict(QT=QT, FS=FS, Vb=Vb, GS=GS, Vd=Vd),
                      dict(QT=sbt(hs, "QT2", [128, GW]), FS=sbt(hs, "FS2", [128, GW]), Vb=sbt(hs, "Vb2", [128, 5, 128], BF16, nres=5),
                           GS=sbt(hs, "GS2", [128, 5, 128], F32, nres=5), Vd=sbt(hs, "Vd2", [16, 128]))]

                def proj_chunks(h, wb, P):
                    out = []
                    for blk, key_, fn in ((0, "QT", AF.Silu), (1, "FS", AF.Sigmoid)):
                        held = []

                        def mm(blk=blk, held=held):
                            for (c0, n, lis) in mvs:
                                bk, br = bank(pring_acc.next() if n == 512 else pring_x.next())
                                for kc in range(16):
                                    MM(bk[:, 0:n], wb.t[:, kc, blk * 128:(blk + 1) * 128], hT.t[:, kc, c0:c0 + n], kc == 0, kc == 15,
                                       [wb.rq[kc // 4]] + [hT.rq[i] for i in lis], [br])
                                held.append((bk, br, c0, n))

                        def ev(key_=key_, fn=fn, held=held):
                            dstb = P[key_]
                            for (bk, br, c0, n) in held:
                                ACT(dstb.t[:, c0:c0 + n], bk[:, 0:n], fn, [br], [dstb.r])
                        out.append((mm, ev))
                    for tsel in (tiles[0:2], tiles[2:]):
                        held = []

                        def mm(tsel=tsel, held=held):
                            for (gt, row0, rows, col0, li) in tsel:
                                bk, br = bank(pring_acc.next())
                                for kc in range(16):
                                    MM(bk[0:rows, 0:256], hT.t[:, kc, col0:col0 + rows], wb.t[:, kc, 256:512], kc == 0, kc == 15,
                                       [hT.rq[li], wb.rq[kc // 4]], [br])
                                held.append((bk, br, rows, li))

                        def ev(held=held):
                            Vb_, GS_, Vd_ = P["Vb"], P["GS"], P["Vd"]
                            for (bk, br, rows, li) in held:
                                CP("act", Vb_.t[0:rows, li, :], bk[0:rows, 0:128], [br], [Vb_.rq[li]])
                                if li == 4:
                                    CP("act", Vd_.t[0:rows, :], bk[0:rows, 0:128], [br], [Vd_.r])
                                ACT(GS_.t[0:rows, li, :], bk[0:rows, 128:256], AF.Silu, [br], [GS_.rq[li]])
                                TT("pool", GS_.t[0:rows, li, :], GS_.t[0:rows, li, :], hgbc.t[0:rows, :], ALU.mult, [GS_.rq[li], hgbc.r], [GS_.rq[li]])
                        out.append((mm, ev))
                    return out

                def rest_parts(h, P):
                    QT_, FS_, Vb_, GS_, Vd_ = P["QT"], P["FS"], P["Vb"], P["GS"], P["Vd"]
                    if g == 0:
                        S.op("pool", lambda e, h=h: e.memset(Sf.t[:, h, :], 0.0), writes=[Sf.rq[h]])
                        S.op("pool", lambda e, h=h: e.memset(Sb.t[:, h, :], 0.0), writes=[Sb.rq[h]])
                    W = GW if has_dec else 512
                    TS("dve", FS_.t[:, 0:W], FS_.t[:, 0:W], oml.t[:, l, h:h + 1], lbt.t[:, l, h:h + 1], ALU.mult, ALU.add, [FS_.r, oml.r, lbt.r], [FS_.r])
                    ACT(LF.t[:, 0:W], FS_.t[:, 0:W], AF.Ln, [FS_.r], [LF.r])
                    TS("pool", KTt.t[:, 0:W], FS_.t[:, 0:W], -1.0, 1.0, ALU.mult, ALU.add, [FS_.r], [KTt.r])
                    S.op("dve", lambda e: e.tensor_tensor_scan(out=Bc.t[:, 0:W], data0=mask01.t[:, 0:W], data1=LF.t[:, 0:W], initial=0.0,
                                                               op0=ALU.mult, op1=ALU.add), [mask01.r, LF.r], [Bc.r])
                    ptiles = [t_ for t_ in tiles if t_[4] < 4]
                    bkh, bkhr = bank(pring_tp.next())
                    bsc, bscr = bank(pring_x.next())
                    bst, bstr = bank(pring_tp.next())
                    d128 = d_ring.next()
                    per = []
                    for (gt, row0, rows, col0, li) in ptiles:
                        c0 = col0
                        cs_ = slice(li * 128, (li + 1) * 128)
                        E1 = e_ring.next(); EA = e_ring.next(); EK = e_ring.next()
                        Cb = b_ring.next(); Ab = b_ring.next(); Bb = b_ring.next(); Db = b_ring.next(); KH = b_ring.next()
                        KHT = e_ring.next()
                        nb = nb_ring.next()
                        ACT(E1.t[:], Bc.t[:, c0:c0 + 128], AF.Exp, [Bc.r], [E1.r])
                        TT("dve", Cb.t[:], QT_.t[:, c0:c0 + 128], E1.t[:], ALU.mult, [QT_.r, E1.r], [Cb.r])
                        CP("dve", d128.t[:, li:li + 1], E1.t[:, 127:128], [E1.r], [d128.r])
                        ACT(EA.t[:], Bc.t[:, c0:c0 + 128], AF.Exp, [Bc.r], [EA.r], scale=-1.0, bias=Bc.t[:, c0 + 63:c0 + 64])
                        TT("pool", Ab.t[:], KTt.t[:, c0:c0 + 128], EA.t[:], ALU.mult, [KTt.r, EA.r], [Ab.r])
                        EB = e_ring.next()
                        ACT(EB.t[:, 0:64], Bc.t[:, c0:c0 + 64], AF.Exp, [Bc.r], [EB.r], scale=-1.0)
                        TT("dve", Bb.t[:, 0:64], KTt.t[:, c0:c0 + 64], EB.t[:, 0:64], ALU.mult, [KTt.r, EB.r], [Bb.r])
                        TS("dve", nb.t[:, 0:1], Bc.t[:, c0 + 63:c0 + 64], -1.0, 0.0, ALU.mult, ALU.add, [Bc.r], [nb.r])
                        ACT(EB.t[:, 64:128], Bc.t[:, c0 + 64:c0 + 128], AF.Exp, [Bc.r, nb.r], [EB.r], bias=nb.t[:, 0:1])
                        TT("dve", Db.t[:, 0:64], QT_.t[:, c0 + 64:c0 + 128], EB.t[:, 64:128], ALU.mult, [QT_.r, EB.r], [Db.r])
                        ACT(EK.t[:], Bc.t[:, c0:c0 + 128], AF.Exp, [Bc.r], [EK.r], scale=-1.0, bias=Bc.t[:, c0 + 127:c0 + 128])
                        TT("pool", KHT.t[:], KTt.t[:, c0:c0 + 128], EK.t[:], ALU.mult, [KTt.r, EK.r], [KHT.r])
                        TP(bkh[:, cs_], KHT.t[:, :], 128, [KHT.r], [bkhr])
                        CP("dve", KH.t[:], bkh[:, cs_], [bkhr], [KH.r])
                        MM(bsc[0:64, li * 128:li * 128 + 64], Bb.t[:, 0:64], Cb.t[:, 0:64], True, True, [Bb.r, Cb.r], [bscr], sig=False)
                        MM(bsc[:, li * 128 + 64:li * 128 + 128], Ab.t[:, :], Db.t[:, 0:64], True, True, [Ab.r, Db.r], [bscr])
                        scb = SC[sci[0] % len(SC)]; sci[0] += 1
                        TT("dve", scb.t[0:64, 0:64], bsc[0:64, li * 128:li * 128 + 64], caus.t[0:64, 0:64], ALU.mult, [bscr, caus.r], [scb.r])
                        TT("dve", scb.t[:, 64:128], bsc[:, li * 128 + 64:li * 128 + 128], m2.t[:, :], ALU.mult, [bscr, m2.r], [scb.r])
                        MM(bst[:, cs_], KH.t[:, :], Vb_.t[:, li, :], True, True, [KH.r, Vb_.rq[li]], [bstr])
                        per.append((Cb, scb, li, col0, gt))
                        if li in (0, 2):
                            yield
                    bo, bor = bank(pring_x.next())
                    for (Cb, scb, li, col0, gt) in per:
                        cs_ = slice(li * 128, (li + 1) * 128)
                        MM(bo[:, cs_], scb.t[:, :], Vb_.t[:, li, :], True, False, [scb.r, Vb_.rq[li]], [bor])
                        MM(bo[:, cs_], Cb.t[:, :], Sb.t[:, h, :], False, True, [Cb.r, Sb.rq[h]], [bor])
                        STT(Sf.t[:, h, :], Sf.t[:, h, :], d128.t[:, li:li + 1], bst[:, cs_], ALU.mult, ALU.add, [Sf.rq[h], d128.r, bstr], [Sf.rq[h]])
                        CP("dve", Sb.t[:, h, :], Sf.t[:, h, :], [Sf.rq[h]], [Sb.rq[h]])
                        if gt == 15:
                            S.dma(hp_o[l, h], Sf.t[:, h, :], reads=[Sf.rq[h]], q=STQ)
                    yield
                    sts = [stat.next() for _ in per]
                    oas = [oa4[(ei_[0] + i_) % len(oa4)] for i_ in range(len(per))]
                    ei_[0] += len(per)
                    for st_, (Cb, scb, li, col0, gt) in zip(sts, per):
                        ACT(junk.t[:, 0:128], bo[:, li * 128:(li + 1) * 128], AF.Square, [bor], [junk.r, st_.r], accum=st_.t[:, 0:1])
                    for st_ in sts:
                        RSQ(st_.t[:, 2:3], st_.t[:, 0:1], 1.0 / 128, 128, 1, st_.r)
                    for st_, oa_, (Cb, scb, li, col0, gt) in zip(sts, oas, per):
                        STT(oa_.t[:, :], bo[:, li * 128:(li + 1) * 128], st_.t[:, 2:3], GS_.t[:, li, :], ALU.mult, ALU.mult, [bor, st_.r, GS_.rq[li]], [oa_.r])
                    bke, bker = bank(pring_tp.next())
                    for oa_, (Cb, scb, li, col0, gt) in zip(oas, per):
                        TP(bke[:, li * 128:(li + 1) * 128], oa_.t[:, :], 128, [oa_.r], [bker])
                    c0_ = per[0][3]
                    CP("act", mixT.t[:, h, c0_:c0_ + 128 * len(per)], bke[:, 0:128 * len(per)], [bker], [mixT.rq[p_[2]] for p_ in per])
                    if has_dec:
                        CP("dve", QM.t[:, 0:256:17], QT_.t[:, 512:528], [QT_.r], [QM.r])
                        bod, bodr = bank(pring_x.next())
                        HB = 8
                        for b0 in range(0, NB, HB):
                            snbs = []
                            for b in range(b0, b0 + HB):
                                s0 = s0_ring.next(); sn = sn_ring.next(); t1 = t1_ring.next(); snb = snb_ring.next()
                                S.dma(s0.t[:], sh[l, b, h], writes=[s0.r])
                                bb, bbr = bank(pring_tp.next())
                                MM(bb[:, 0:128], sel.t[0:16, b * 128:(b + 1) * 128], Vd_.t[0:16, :], True, True, [sel.r, Vd_.r], [bbr])
                                TS("dve", t1.t[:], bb[:, 0:128], KTt.t[:, 512 + b:513 + b], None, ALU.mult, None, [bbr, KTt.r], [t1.r])
                                STT(sn.t[:], s0.t[:], FS_.t[:, 512 + b:513 + b], t1.t[:], ALU.mult, ALU.add, [s0.r, FS_.r, t1.r], [sn.r])
                                S.dma(hs_o[l, b, h], sn.t[:], reads=[sn.r], q=STQ)
                                CP("pool", snb.t[:], sn.t[:], [sn.r], [snb.r])
                                snbs.append((b, snb))
                            for (b, snb) in snbs:
                                MM(bod[0:16, 0:128], QM.t[:, b * 16:(b + 1) * 16], snb.t[:, :], b == 0, b == NB - 1, [QM.r, snb.r], [bodr], sig=True)
                        hgrn_epilog(bod, bodr, NB, 512, 4, h, GS_, oa_ring, epilog_tp)
                    yield

                pend = [None]

                def hcompute(h, wb):
                    chunks = proj_chunks(h, wb, PB[h % 2])
                    rest = pend[0]
                    for (mm, ev) in chunks:
                        mm()
                        if rest is not None:
                            next(rest)
                        ev()
                    pend[0] = rest_parts(h, PB[h % 2])
                run_slabs(8, hq, hcompute, wring, key=("hgrn", l, g))
                for _ in pend[0]:
                    pass
            S.barrier()
            mark("hgrn")
            chk()
            with contextlib.ExitStack() as ss:
                swa(l, g, ss, wring, mixT, epilog_tp)
            S.barrier()
            mark("swa")
            chk()
            with contextlib.ExitStack() as gs:
                gmlp(l, g, gs, wring, mixT, epilog_tp)
            S.barrier()
            if KDBG:
                off = S.off
                S.off = False
                S.barrier()
                S.dma(dbg_mix[g], mixT.t[:], reads=mixT.rq)
                S.barrier()
                S.off = off
            return mixT

        def hgrn_epilog(bo, bor, rows, col0, li, h, GS, oa_ring, epilog_tp):
            st = stat.next()
            ACT(junk.t[0:rows, 0:128], bo[0:rows, 0:128], AF.Square, [bor], [junk.r, st.r], accum=st.t[0:rows, 0:1])
            RSQ(st.t[0:rows, 2:3], st.t[0:rows, 0:1], 1.0 / 128, rows, 1, st.r)
            oa = oa_ring.next()
            STT(oa.t[0:rows, :], bo[0:rows, 0:128], st.t[0:rows, 2:3], GS.t[0:rows, li, :], ALU.mult, ALU.mult, [bor, st.r, GS.rq[li]], [oa.r])
            epilog_tp(oa.t, oa.r, rows, col0, li, h, 1)

        def load_slab(l, wb, c0, ncols):
            for q in range(4):
                stage_cast(w_in[l, q * 512:(q + 1) * 512, c0:c0 + ncols].rearrange("(a p) c -> p a c", p=128),
                           wb.t[:, 4 * q:4 * q + 4, 0:ncols], wb.rq[q], 4, ncols)

        def swa(l, g, ss, wring, mixT, epilog_tp):
            has_dec = (g == NG - 1)
            tiles = group_tiles(g)
            QTp = sbt(ss, "QTp", [128, 8, GW], BF16, nres=5)
            QNP = [sbt(ss, "QNP%d" % i, [128, 8, 128]) for i in range(2)]
            KN = [sbt(ss, "KN%d" % i, [128, 128]) for i in range(2)]
            VR = [sbt(ss, "VR%d" % i, [128, 128]) for i in range(2)]
            tq = sbt(ss, "tq", [128, 8, 64]); tk = sbt(ss, "tk", [128, 128])
            KTd = sbt(ss, "KTd", [128, 16], BF16)
            pt_ring = Ring([sbt(ss, "pt%d" % i, [128, 512], BF16) for i in range(2)])
            pm_ring = Ring([sbt(ss, "pm%d" % i, [128, 4, 128], BF16) for i in range(4)])
            OB = Ring([sbt(ss, "OB%d" % i, [128, 8, 64]) for i in range(2)])
            dd = Ring([sbt(ss, "dd%d" % i, [128, 16]) for i in range(2)])
            for qn in QNP:
                S.op("pool", lambda e, qn=qn: e.memset(qn.t[:], 0.0), writes=[qn.r])
            wa = wring.next(); load_slab(l, wa, 4096, 512)
            wk = wring.next(); load_slab(l, wk, 4608, 256)
            ti = 0
            sprj = {}

            def sproj(ix):
                (gt, row0, rows, col0, li) = tiles[ix]
                bq, bqr = bank(pring_acc.next())
                for kc in range(16):
                    MM(bq[0:rows, 0:512], hT.t[:, kc, col0:col0 + rows], wa.t[:, kc, 0:512], kc == 0, kc == 15, [hT.rq[li], wa.rq[kc // 4]], [bqr])
                bk_, bkr = bank(pring_acc.next())
                for kc in range(16):
                    MM(bk_[0:rows, 0:256], hT.t[:, kc, col0:col0 + rows], wk.t[:, kc, 0:256], kc == 0, kc == 15, [hT.rq[li], wk.rq[kc // 4]], [bkr])
                sprj[ix] = (bq, bqr, bk_, bkr)
            sproj(0)
            for ix_, (gt, row0, rows, col0, li) in enumerate(tiles):
                qn = QNP[ti % 2]; kn = KN[ti % 2]; vr = VR[ti % 2]; ti += 1
                (bq, bqr, bk_, bkr) = sprj.pop(ix_)
                st = stat.next()
                ACT(tq.t[0:rows].rearrange("p a b -> p (a b)"), bq[0:rows, 0:512], AF.Square, [bqr], [tq.r])
                S.op("dve", lambda e: e.tensor_reduce(out=st.t[0:rows, 0:8], in_=tq.t[0:rows], axis=AX.X, op=ALU.add), [tq.r], [st.r])
                ACT(tk.t[0:rows, :], bk_[0:rows, 0:128], AF.Square, [bkr], [tk.r])
                S.op("dve", lambda e: e.tensor_reduce(out=st.t[0:rows, 8:10], in_=tk.t[0:rows, :].rearrange("p (a b) -> p a b", a=2), axis=AX.X, op=ALU.add), [tk.r, st.r], [st.r])
                RSQ(st.t[0:rows, 0:10], st.t[0:rows, 0:10], 1.0 / 64, rows, 10, st.r)
                TT("dve", tq.t[0:rows], bq[0:rows, 0:512].rearrange("p (a b) -> p a b", a=8), st.t[0:rows, 0:8].unsqueeze(2).to_broadcast([rows, 8, 64]), ALU.mult, [bqr, st.r], [tq.r])
                for j in range(2):
                    TT("pool", qn.t[0:rows, 4 * j:4 * j + 4, 64 * j:64 * j + 64], tq.t[0:rows, 4 * j:4 * j + 4, :],
                       qgbc.t[0:rows, :].unsqueeze(1).to_broadcast([rows, 4, 64]), ALU.mult, [tq.r, qgbc.r], [qn.r])
                TT("dve", tk.t[0:rows, :].rearrange("p (a b) -> p a b", a=2), bk_[0:rows, 0:128].rearrange("p (a b) -> p a b", a=2),
                   st.t[0:rows, 8:10].unsqueeze(2).to_broadcast([rows, 2, 64]), ALU.mult, [bkr, st.r], [tk.r])
                TT("pool", kn.t[0:rows, :].rearrange("p (a b) -> p a b", a=2), tk.t[0:rows, :].rearrange("p (a b) -> p a b", a=2),
                   kgbc.t[0:rows, :].unsqueeze(1).to_broadcast([rows, 2, 64]), ALU.mult, [tk.r, kgbc.r], [kn.r])
                CP("act", vr.t[0:rows, :], bk_[0:rows, 128:256], [bkr], [vr.r])
                if li < 4:
                    CP("pool", V1all.t[0:rows, gt, :, 0:64], vr.t[0:rows, :].rearrange("p (a b) -> p a b", a=2), [vr.r], [V1all.rq[gt]])
                if ix_ + 1 < len(tiles):
                    sproj(ix_ + 1)
                transposes_to(lambda i: QTp.t[:, i, col0:col0 + rows], qn.t[:].rearrange("p a b -> p (a b)"), qn.r, rows, 8, QTp.rq[li])
                if li < 4:
                    transposes_to(lambda i: KTall.t[:, gt * 128:gt * 128 + rows], kn.t, kn.r, rows, 1, KTall.rq[gt])
                else:
                    transposes_to(lambda i: KTd.t[:, 0:rows], kn.t, kn.r, rows, 1, KTd.r)
                if gt == 15:
                    S.dma(kp_o[l], kn.t[:], reads=[kn.r], q=STQ)
                    S.dma(vp_o[l], vr.t[:], reads=[vr.r], q=STQ)
                if li < 4:
                    ob = OB.next()
                    for j in range(2):
                        kts = ([gt - 1] if gt > 0 else []) + [gt]
                        pms = []
                        for kt in kts:
                            bs_, bsr = bank(pring_x.next())
                            MM(bs_[:, 0:512].rearrange("p (a b) -> p a b", a=4), KTall.t[:, kt * 128:(kt + 1) * 128], QTp.t[:, 4 * j:4 * j + 4, col0:col0 + 128], True, True,
                               [KTall.rq[kt], QTp.rq[li]], [bsr])
                            pt = pt_ring.next(); pm = pm_ring.next()
                            ACT(pt.t[:], bs_[:, 0:512], AF.Exp, [bsr], [pt.r], scale=0.125)
                            mk = caus if kt == gt else lows
                            TT("dve" if kt == gt else "pool", pm.t[:], pt.t[:].rearrange("p (a b) -> p a b", a=4),
                               mk.t[:].unsqueeze(1).to_broadcast([128, 4, 128]), ALU.mult, [pt.r, mk.r], [pm.r])
                            pms.append((pm, kt))
                        bo, bor = bank(pring_acc.next())
                        for gq in range(4):
                            for ii, (pm, kt) in enumerate(pms):
                                MM(bo[:, gq * 65:(gq + 1) * 65], pm.t[:, gq, :], V1all.t[:, kt, j, :], ii == 0, ii == len(pms) - 1,
                                   [pm.r, V1all.rq[kt]], [bor], sig=(gq == 3 and ii == len(pms) - 1))
                        d_ = dd.next()
                        bov = bo[:, 0:260].rearrange("p (a b) -> p a b", a=4)
                        TT("dve", d_.t[:, 0:4], bov[:, :, 64], esink.t[:, 4 * j:4 * j + 4], ALU.add, [bor, esink.r], [d_.r])
                        S.op("dve", lambda e, d_=d_: e.reciprocal(out=d_.t[:, 4:8], in_=d_.t[:, 0:4]), [d_.r], [d_.r])
                        TT("dve", ob.t[:, 4 * j:4 * j + 4, :], bov[:, :, 0:64], d_.t[:, 4:8].unsqueeze(2).to_broadcast([128, 4, 64]), ALU.mult, [bor, d_.r], [ob.r])
                    epilog_tp(ob.t[:].rearrange("p a b -> p (a b)"), ob.r, 128, col0, li, 8, 4)
                else:
                    swa_dec(l, ss, qn, kn, vr, QTp, KTd, dd, OB, epilog_tp)

        def swa_dec(l, ss, qn, kn, vr, QTp, KTd, dd, OB, epilog_tp):
            CKT = sbt(ss, "CKT", [128, NB, 128], BF16); V1c = sbt(ss, "V1c", [128, NB, 2, 65], BF16)
            ld_ring = Ring([sbt(ss, "ld%d" % i, [128, 128]) for i in range(3)])
            PTd = sbt(ss, "PTd", [128, 128]); PTM = sbt(ss, "PTM", [128, 8, NB * NB], BF16)
            prod = sbt(ss, "prod", [16, 8, 64]); psf = sbt(ss, "psf", [16, 16]); t1 = sbt(ss, "t1d", [16, 8, 64])
            S.op("pool", lambda e: e.memset(V1c.t[:], 1.0), writes=[V1c.r])
            S.op("pool", lambda e: e.memset(PTM.t[:], 0.0), writes=[PTM.r])
            S.dma(ks_o[l, :, 0:127, :], ck[l, :, 1:128, :])
            S.dma(vs_o[l, :, 0:127, :], cv[l, :, 1:128, :])
            S.dma(ks_o[l, :, 127, :], kn.t[0:NB, :], reads=[kn.r], q=STQ)
            S.dma(vs_o[l, :, 127, :], vr.t[0:NB, :], reads=[vr.r], q=STQ)
            for b in range(NB):
                ckt = ld_ring.next()
                S.dma(ckt.t[:], ck[l, b], writes=[ckt.r])
                bk, br = bank(pring_tp.next())
                TP(bk[:, 0:128], ckt.t[:, :], 128, [ckt.r], [br])
                CP("act", CKT.t[:, b, :], bk[:, 0:128], [br], [CKT.r])
                cvt = ld_ring.next()
                S.dma(cvt.t[:], cv[l, b], writes=[cvt.r])
                CP("pool", V1c.t[:, b, :, 0:64], cvt.t[:, :].rearrange("p (a c) -> p a c", a=2), [cvt.r], [V1c.r])
            bsd, bsdr = bank(pring_x.next())
            for b in range(NB):
                for j in range(2):
                    MM(bsd[:, b * 8 + 4 * j:b * 8 + 4 * j + 4], CKT.t[:, b, :], QTp.t[:, 4 * j:4 * j + 4, 512 + b], True, True,
                       [CKT.r, QTp.rq[4]], [bsdr], sig=(b == NB - 1 and j == 1))
            ACT(PTd.t[:], bsd[:, 0:128], AF.Exp, [bsdr], [PTd.r], scale=0.125)
            TS("dve", PTM.t[:, :, 0:256:17], PTd.t[:].rearrange("p (b h) -> p h b", h=8), kmask.t[:, 0:1], None, ALU.mult, None,
               [PTd.r, kmask.r], [PTM.r])
            for j in range(2):
                TT("dve", prod.t[:, 4 * j:4 * j + 4, :], qn.t[0:NB, 4 * j:4 * j + 4, 64 * j:64 * j + 64],
                   kn.t[0:NB, 64 * j:64 * j + 64].unsqueeze(1).to_broadcast([NB, 4, 64]), ALU.mult, [qn.r, kn.r], [prod.r])
            S.op("dve", lambda e: e.tensor_reduce(out=psf.t[:, 0:8], in_=prod.t[:], axis=AX.X, op=ALU.add), [prod.r], [psf.r])
            ACT(psf.t[:, 8:16], psf.t[:, 0:8], AF.Exp, [psf.r], [psf.r], scale=0.125)
            vrv = vr.t[0:NB, :].rearrange("p (a b) -> p a b", a=2)
            ob = OB.next()
            for j in range(2):
                bod, bodr = bank(pring_acc.next())
                for gq in range(4):
                    h = 4 * j + gq
                    for b in range(NB):
                        MM(bod[0:NB, gq * 65:(gq + 1) * 65], PTM.t[:, h, b * NB:(b + 1) * NB], V1c.t[:, b, j, :], b == 0, b == NB - 1,
                           [PTM.r, V1c.r], [bodr], sig=(gq == 3 and b == NB - 1))
                bov = bod[0:NB, 0:260].rearrange("p (a b) -> p a b", a=4)
                TT("dve", t1.t[:, 4 * j:4 * j + 4, :], vrv[:, j:j + 1, :].to_broadcast([NB, 4, 64]),
                   psf.t[:, 8 + 4 * j:12 + 4 * j].unsqueeze(2).to_broadcast([NB, 4, 64]), ALU.mult, [vr.r, psf.r], [t1.r])
                TT("dve", t1.t[:, 4 * j:4 * j + 4, :], t1.t[:, 4 * j:4 * j + 4, :], bov[:, :, 0:64], ALU.add, [t1.r, bodr], [t1.r])
                d_ = dd.next()
                TT("dve", d_.t[0:NB, 0:4], bov[:, :, 64], psf.t[:, 8 + 4 * j:12 + 4 * j], ALU.add, [bodr, psf.r], [d_.r])
                TT("dve", d_.t[0:NB, 0:4], d_.t[0:NB, 0:4], esink.t[0:NB, 4 * j:4 * j + 4], ALU.add, [d_.r, esink.r], [d_.r])
                S.op("dve", lambda e, d_=d_: e.reciprocal(out=d_.t[0:NB, 8:12], in_=d_.t[0:NB, 0:4]), [d_.r], [d_.r])
                TT("dve", ob.t[0:NB, 4 * j:4 * j + 4, :], t1.t[:, 4 * j:4 * j + 4, :], d_.t[0:NB, 8:12].unsqueeze(2).to_broadcast([NB, 4, 64]), ALU.mult, [t1.r, d_.r], [ob.r])
            epilog_tp(ob.t[:].rearrange("p a b -> p (a b)"), ob.r, NB, 512, 4, 8, 4)

        def gmlp(l, g, gs, wring, mixT, epilog_tp):
            tiles = group_tiles(g)
            GU = Ring([sbt(gs, "GU%d" % i, [128, 512]) for i in range(2)])
            GV = Ring([sbt(gs, "GV%d" % i, [128, 512]) for i in range(2)])
            VNb = Ring([sbt(gs, "VNb%d" % i, [128, 512], BF16) for i in range(2)])
            OC = Ring([sbt(gs, "OC%d" % i, [128, 512]) for i in range(2)])
            wu = wring.next(); load_slab(l, wu, 4864, 512)
            wv = wring.next(); load_slab(l, wv, 5376, 512)
            gprj = {}

            def gproj(ix):
                (gt, row0, rows, col0, li) = tiles[ix]
                lst = []
                for wb in (wu, wv):
                    bk, br = bank(pring_acc.next())
                    for kc in range(16):
                        MM(bk[0:rows, 0:512], hT.t[:, kc, col0:col0 + rows], wb.t[:, kc, 0:512], kc == 0, kc == 15, [hT.rq[li], wb.rq[kc // 4]], [br])
                    lst.append((bk, br))
                gprj[ix] = lst
            gproj(0)
            for ix_, (gt, row0, rows, col0, li) in enumerate(tiles):
                outs = []
                for (bk, br), ring in zip(gprj.pop(ix_), (GU, GV)):
                    t = tA.next(); go = ring.next()
                    ACT(t.t[0:rows, :], bk[0:rows, 0:512], AF.Square, [br], [t.r])
                    TS("pool", t.t[0:rows, :], t.t[0:rows, :], 0.044715, 1.0, ALU.mult, ALU.add, [t.r], [t.r])
                    TT("dve", t.t[0:rows, :], t.t[0:rows, :], bk[0:rows, 0:512], ALU.mult, [t.r, br], [t.r])
                    ACT(t.t[0:rows, :], t.t[0:rows, :], AF.Sigmoid, [t.r], [t.r], scale=GELU_C)
                    TT("dve", go.t[0:rows, :], t.t[0:rows, :], bk[0:rows, 0:512], ALU.mult, [t.r, br], [go.r])
                    outs.append(go)
                gu, gv = outs
                if ix_ + 1 < len(tiles):
                    gproj(ix_ + 1)
                st = stat.next()
                S.op("dve", lambda e: e.bn_stats(out=st.t[0:rows, 0:6], in_=gv.t[0:rows, :]), [gv.r], [st.r])
                S.op("dve", lambda e: e.bn_aggr(out=st.t[0:rows, 6:8], in_=st.t[0:rows, 0:6]), [st.r], [st.r])
                RSQ(st.t[0:rows, 9:10], st.t[0:rows, 7:8], 1.0, rows, 1, st.r)
                TS("dve", gv.t[0:rows, :], gv.t[0:rows, :], st.t[0:rows, 6:7], st.t[0:rows, 9:10], ALU.subtract, ALU.mult, [gv.r, st.r], [gv.r])
                TT("pool", gv.t[0:rows, :], gv.t[0:rows, :], lgbc.t[0:rows, :], ALU.mult, [gv.r, lgbc.r], [gv.r])
                TT("pool", gv.t[0:rows, :], gv.t[0:rows, :], lbbc.t[0:rows, :], ALU.add, [gv.r, lbbc.r], [gv.r])
                oc = OC.next()
                if li < 4:
                    vb = VNb.next()
                    CP("act", vb.t[:], gv.t[:], [gv.r], [vb.r])
                    bm, bmr = bank(pring_x.next())
                    for c in range(4):
                        MM(bm[:, c * 128:(c + 1) * 128], WsT.t[:, c, :], vb.t[:, c * 128:(c + 1) * 128], True, True, [WsT.r, vb.r], [bmr], sig=(c == 3))
                    t = tA.next()
                    TT("dve", t.t[:].rearrange("p (a b) -> p a b", a=4), bm[:, 0:512].rearrange("p (a b) -> p a b", a=4),
                       bsT.t[:, :].unsqueeze(2).to_broadcast([128, 4, 128]), ALU.add, [bmr, bsT.r], [t.r])
                    TT("pool", oc.t[:], t.t[:], gu.t[:], ALU.mult, [t.r, gu.r], [oc.r])
                else:
                    S.dma(gv_o[l], gv.t[0:NB, :], reads=[gv.r], q=STQ)
                    t = tA.next()
                    TT("dve", t.t[0:NB].rearrange("p (a b) -> p a b", a=4), gv.t[0:NB].rearrange("p (a b) -> p a b", a=4),
                       w00.t[:, :].unsqueeze(2).to_broadcast([NB, 4, 128]), ALU.mult, [gv.r, w00.r], [t.r])
                    TT("dve", t.t[0:NB].rearrange("p (a b) -> p a b", a=4), t.t[0:NB].rearrange("p (a b) -> p a b", a=4),
                       bs0.t[:, :].unsqueeze(2).to_broadcast([NB, 4, 128]), ALU.add, [t.r, bs0.r], [t.r])
                    TT("pool", oc.t[0:NB], t.t[0:NB], gu.t[0:NB], ALU.mult, [t.r, gu.r], [oc.r])
                epilog_tp(oc.t, oc.r, rows, col0, li, 12, 4)

        def ffn(l, g, st):
            has_dec = (g == NG - 1)
            hid = sbt(st, "hid", [128, NJ, GW], BF16, nres=5)
            with contextlib.ExitStack() as fa:
                wring = Ring([sbt(fa, "wu%d" % i, [128, 2, 16, 128], BF16, nres=2) for i in range(2)])
                U = Ring([sbt(fa, "U%d" % i, [128, 2, 514], F32, nres=2) for i in range(2)])
                tcv = Ring([sbt(fa, "tcv%d" % i, [128, 2, 512], F32, nres=2) for i in range(2)])
                Ud = Ring([sbt(fa, "Ud%d" % i, [128, 2, NB, 3], F32, nres=2) for i in range(2)])
                upc_ring = Ring([sbt(fa, "upc%d" % i, [128, 2, NB]) for i in range(3)])
                upl_ring = Ring([sbt(fa, "upl%d" % i, [128, 2, 2]) for i in range(3)])
                ost_ring = Ring([sbt(fa, "ost%d" % i, [16, 256]) for i in range(4)])
                stt_ = Ring([sbt(fa, "stt%d" % i, [16, 2, 2, 128], F32, nres=2) for i in range(3)])
                if has_dec:
                    S.dma(cs_o[l, :, 0, :], scv[l, :, 1, :])
                fq = fq_gen(l)

                tails = []
                sxm = {}
                udm = {}

                def st_dma(j2):
                    if has_dec and j2 < NJ:
                        sx = stt_.next()
                        for gv in range(2):
                            S.dma(sx.t[:, :, gv, :], scv[l, :, :, gv * DFF + j2 * 128:gv * DFF + (j2 + 1) * 128], writes=[sx.rq[gv]])
                        sxm[j2] = sx

                def st_prep(j2):
                    if has_dec and j2 < NJ:
                        sx = sxm.pop(j2)
                        ud = Ud.next()
                        bt, btr = bank(pring_tp.next())
                        for r in range(2):
                            for gv in range(2):
                                TP(bt[:, (r * 2 + gv) * 16:(r * 2 + gv) * 16 + 16], sx.t[0:16, r, gv, :], 16, [sx.rq[gv]], [btr], sig=(r == 1 and gv == 1))
                        for gv in range(2):
                            for r in range(2):
                                CP("dve", ud.t[:, gv, :, r], bt[:, (r * 2 + gv) * 16:(r * 2 + gv) * 16 + 16], [btr], [ud.rq[gv]])
                        udm[j2] = ud

                def fcompute(j, wb):
                    if j == 0:
                        st_dma(0); st_dma(1); st_prep(0)
                    allb = []
                    for (c0, n, lis) in moving(g):
                        bks = []
                        if n != 512:
                            bkd, bkdr = bank(pring_x.next())
                        for gv in range(2):
                            if n == 512:
                                bk, br = bank(pring_acc.next())
                            else:
                                bk, br = bkd[:, gv * 32:gv * 32 + 32], bkdr
                            for kc in range(16):
                                MM(bk[:, 0:n], wb.t[:, gv, kc, :], hT.t[:, kc, c0:c0 + n], kc == 0, kc == 15,
                                   [wb.rq[gv]] + [hT.rq[i] for i in lis], [br])
                            bks.append((bk, br))
                        allb.append((n, bks))
                    for t_ in tails:
                        t_()
                    del tails[:]
                    for (n, bks) in allb:
                        if n == 512:
                            u = U.next(); tc_ = tcv.next()
                            for gv in range(2):
                                jj = gv * NJ + j
                                bk, br = bks[gv]
                                if g == 0:
                                    S.op("pool", lambda e, u=u, gv=gv: e.memset(u.t[:, gv, 0:2], 0.0), writes=[u.rq[gv]])
                                else:
                                    CP("pool", u.t[:, gv, 0:2], carry.t[:, jj, :], [carry.rq[jj]], [u.rq[gv]])
                                CP("act", u.t[:, gv, 2:514], bk[:, 0:512], [br], [u.rq[gv]])
                                CP("pool", carry.t[:, jj, :], u.t[:, gv, 512:514], [u.rq[gv]], [carry.rq[jj]])
                                if g == NG - 1:
                                    if gv == 0:
                                        upl = upl_ring.next()
                                    CP("pool", upl.t[:, gv, :], u.t[:, gv, 512:514], [u.rq[gv]], [upl.r])
                                    if gv == 1:
                                        def tail_p(upl=upl, j=j):
                                            bt3, bt3r = bank(pring_tp.next())
                                            for g2 in range(2):
                                                TP(bt3[0:2, g2 * 128:(g2 + 1) * 128], upl.t[:, g2, :], 128, [upl.r], [bt3r], sig=(g2 == 1))
                                            ost = ost_ring.next()
                                            CP("dve", ost.t[0:2, :], bt3[0:2, 0:256], [bt3r], [ost.r])
                                            S.dma(cp_o[l].rearrange("r (gg f) -> r gg f", gg=2)[:, :, j * 128:(j + 1) * 128],
                                                  ost.t[0:2, :].rearrange("r (gg f) -> r gg f", gg=2), reads=[ost.r], q=STQ)
                                        tails.append(tail_p)
                                TS("dve", tc_.t[:, gv, :], u.t[:, gv, 0:512], cw.t[:, 0, jj:jj + 1], cb.t[:, jj:jj + 1], ALU.mult, ALU.add, [u.rq[gv], cw.r, cb.r], [tc_.rq[gv]])
                                STT(tc_.t[:, gv, :], u.t[:, gv, 1:513], cw.t[:, 1, jj:jj + 1], tc_.t[:, gv, :], ALU.mult, ALU.add, [u.rq[gv], cw.r, tc_.rq[gv]], [tc_.rq[gv]])
                                STT(tc_.t[:, gv, :], u.t[:, gv, 2:514], cw.t[:, 2, jj:jj + 1], tc_.t[:, gv, :], ALU.mult, ALU.add, [u.rq[gv], cw.r, tc_.rq[gv]], [tc_.rq[gv]])
                            ACT(tc_.t[:, 0, :], tc_.t[:, 0, :], AF.Silu, [tc_.rq[0]], [tc_.rq[0]])
                            TT("pool", hid.t[:, j, 0:512], tc_.t[:, 0, :], tc_.t[:, 1, :], ALU.mult, tc_.rq, hid.rq[0:4])
                        else:
                            ud = udm.pop(j); upc = upc_ring.next()
                            tc_ = tcv.next()
                            for gv in range(2):
                                jj = gv * NJ + j
                                bk, br = bks[gv]
                                CP("act", ud.t[:, gv, :, 2], bk[:, 0:NB], [br], [ud.rq[gv]])
                                CP("act", upc.t[:, gv, :], bk[:, 0:NB], [br], [upc.r])
                                TS("dve", tc_.t[:, gv, 0:NB], ud.t[:, gv, :, 0], cw.t[:, 0, jj:jj + 1], cb.t[:, jj:jj + 1], ALU.mult, ALU.add, [ud.rq[gv], cw.r, cb.r], [tc_.rq[gv]])
                                STT(tc_.t[:, gv, 0:NB], ud.t[:, gv, :, 1], cw.t[:, 1, jj:jj + 1], tc_.t[:, gv, 0:NB], ALU.mult, ALU.add, [ud.rq[gv], cw.r, tc_.rq[gv]], [tc_.rq[gv]])
                                STT(tc_.t[:, gv, 0:NB], ud.t[:, gv, :, 2], cw.t[:, 2, jj:jj + 1], tc_.t[:, gv, 0:NB], ALU.mult, ALU.add, [ud.rq[gv], cw.r, tc_.rq[gv]], [tc_.rq[gv]])

                            def tail_d(upc=upc, j=j):
                                bt2, bt2r = bank(pring_tp.next())
                                for gv in range(2):
                                    TP(bt2[0:NB, gv * 128:(gv + 1) * 128], upc.t[:, gv, :], 128, [upc.r], [bt2r], sig=(gv == 1))
                                ost = ost_ring.next()
                                CP("dve", ost.t[0:NB, :], bt2[0:NB, 0:256], [bt2r], [ost.r])
                                S.dma(cs_o[l, :, 1, :].rearrange("b (gg f) -> b gg f", gg=2)[:, :, j * 128:(j + 1) * 128],
                                      ost.t[0:NB, :].rearrange("b (gg f) -> b gg f", gg=2), reads=[ost.r], q=STQ)
                            tails.append(tail_d)
                            ACT(tc_.t[:, 0, 0:NB], tc_.t[:, 0, 0:NB], AF.Silu, [tc_.rq[0]], [tc_.rq[0]])
                            TT("pool", hid.t[:, j, 512:528], tc_.t[:, 0, 0:NB], tc_.t[:, 1, 0:NB], ALU.mult, tc_.rq, [hid.rq[4]])
                    st_dma(j + 2)
                    st_prep(j + 1)
                run_slabs(NJ, fq, fcompute, wring, key=("ffnA", l, g), nxt=(("ffnB", l, g), rq_gen(w_down[l], NJ, 256, 8)(0)))
                for t_ in tails:
                    t_()
                del tails[:]
            S.barrier()
            mark("ffnA")
            with contextlib.ExitStack() as fb:
                resid_update(l, g, False, w_down[l], NJ, hid, fb, ncols=256, qa=8, key=("ffnB", l, g),
                             nxt=(("ple", l, g), rq_gen(w_pg[l], 16, 512, 4, lambda ds: [wq(w_pp[l, :, ds * 512:(ds + 1) * 512], 2, 512, None, None)])(0)))
                S.barrier()

        def ple(l, g, st):
            pT = sbt(st, "pT", [128, 2, GW], BF16, nres=5)
            pl = Ring([sbt(st, "pl%d" % i, [128, PLE]) for i in range(2)])
            for (gt, row0, rows, col0, li) in group_tiles(g):
                p_ = pl.next()
                src = pp[l, row0:row0 + rows, :] if gt < 16 else psm[l]
                S.dma(p_.t[0:rows, :], src, writes=[p_.r])
                transposes_to(lambda i: pT.t[:, i, col0:col0 + rows], p_.t, p_.r, rows, 2, pT.rq[li])

            wps = [sbt(st, "wpb%d" % i, [128, 2, 512], BF16) for i in range(2)]

            def xq(ds):
                wp = wps[ds % 2]
                return [wq(w_pp[l, :, ds * 512:(ds + 1) * 512], 2, 512, (lambda wb, wp=wp: wp.t[:, :, :]), (lambda wb, wp=wp: wp.r))]

            def pre(ds, bk, br, tl, xo):
                wp = wps[ds % 2]
                (gt, row0, rows, col0, li) = tl
                bp, bpr = bank(pring_x.next())
                for a in range(2):
                    MM(bp[0:rows, 0:512], pT.t[:, a, col0:col0 + rows], wp.t[:, a, :], a == 0, a == 1, [pT.rq[li], wp.r], [bpr])
                t = tA.next()
                ACT(t.t[0:rows, :], bk[0:rows, 0:512], AF.Sigmoid, [br], [t.r])
                TT("dve", t.t[0:rows, :], t.t[0:rows, :], bp[0:rows, 0:512], ALU.mult, [t.r, bpr], [t.r])
                TT("dve", xo.t[0:rows, :], xo.t[0:rows, :], t.t[0:rows, :], ALU.add, [xo.r, t.r], [xo.r])
            nl, ng = (l, g + 1) if g + 1 < NG else (l + 1, 0)
            nx = (("hgrn", nl, ng), hq_gen(nl)(0)) if nl < DEPTH else None
            resid_update(l, g, l == DEPTH - 1, w_pg[l], 16, hT, st, ncols=512, extra=(xq, pre), key=("ple", l, g), nxt=nx)

        def load_params(l):
            for i, gsrc in enumerate((norm1_g, norm2_g, ple_g)):
                S.dma(gam.t[:, i, :], gsrc[l].rearrange("(a p) -> p a", p=128), writes=[gam.r], allow_slow_non_contiguous=True)
            S.dma(hgbc.t[:], hgn[l].partition_broadcast(128), writes=[hgbc.r])
            S.dma(qgbc.t[:], qng[l].partition_broadcast(128), writes=[qgbc.r])
            S.dma(kgbc.t[:], kng[l].partition_broadcast(128), writes=[kgbc.r])
            S.dma(esink.t[:], sinks[l].partition_broadcast(128), writes=[esink.r])
            ACT(esink.t[:], esink.t[:], AF.Exp, [esink.r], [esink.r])
            S.dma(lgbc.t[:], lng[l].partition_broadcast(128), writes=[lgbc.r])
            S.dma(lbbc.t[:], lnb[l].partition_broadcast(128), writes=[lbbc.r])
            S.dma(bsT.t[:], gbs[l].rearrange("c p -> p c"), writes=[bsT.r], allow_slow_non_contiguous=True)
            S.dma(w00.t[:], gws[l, :, 0, 0].partition_broadcast(16), writes=[w00.r], allow_slow_non_contiguous=True)
            S.dma(bs0.t[:], gbs[l, :, 0].partition_broadcast(16), writes=[bs0.r], allow_slow_non_contiguous=True)
            for r in range(3):
                S.dma(cw.t[:, r, :], conv_w[l, r].rearrange("(j p) -> p j", p=128), writes=[cw.r], allow_slow_non_contiguous=True)
            S.dma(cb.t[:], conv_b[l].rearrange("(j p) -> p j", p=128), writes=[cb.r], allow_slow_non_contiguous=True)
            for c in range(4):
                t = tA.next()
                S.dma(t.t[:, 0:128], gws[l, c], writes=[t.r])
                TT("dve", t.t[:, 0:128], t.t[:, 0:128], tril.t[:], ALU.mult, [t.r, tril.r], [t.r])
                transposes_to(lambda i: WsT.t[:, c, :], t.t, t.r, 128, 1, WsT.r)

        try:
            for l in range(DEPTH):
                chk()
                load_params(l)
                mark("params")
                for g in range(NG):
                    chk()
                    norm_pass(g, 0)
                    mark("norm0")
                    chk()
                    with contextlib.ExitStack() as st:
                        mixT = mixers(l, g, st)
                        S.barrier()
                        mark("gmlp")
                        chk()
                        with contextlib.ExitStack() as st2:
                            resid_update(l, g, False, w_out[l], 16, mixT, st2, key=("wout", l, g), nxt=(("ffnA", l, g), fq_gen(l)(0)))
                            S.barrier()
                            mark("wout")
                    S.barrier()
                    chk()
                    norm_pass(g, 1)
                    mark("norm1")
                    chk()
                    with contextlib.ExitStack() as st:
                        ffn(l, g, st)
                    S.barrier()
                    mark("ffnB")
                    chk()
                    norm_pass(g, 2)
                    mark("norm2")
                    chk()
                    with contextlib.ExitStack() as st:
                        ple(l, g, st)
                        S.barrier()
                    mark("ple")
                S.barrier()
        except _Stop:
            pass
        S.off = False
        S.barrier()
        if KDBG:
            S.dma(dbg_x, xbuf)
            S.barrier()
    return nc


_CACHE = {}
MARKS = []


def _consts():
    i = np.arange(128)
    caus = (i[:, None] <= i[None, :]).astype(np.float32)
    tril = (i[None, :] <= i[:, None]).astype(np.float32)
    lows = (i[:, None] > i[None, :]).astype(np.float32)
    i64 = np.arange(64)
    c64 = (i64[:, None] <= i64[None, :]).astype(np.float32)
    m2 = np.concatenate([np.ones((64, 64), np.float32), c64], axis=0)
    kmask = np.ones((128, 1), np.float32); kmask[0, 0] = 0.0
    mask01 = np.ones((128, GW), np.float32)
    mask01[:, 0:512:128] = 0.0
    mask01[:, 512:] = 0.0
    sel = np.zeros((16, 16, 128), np.float32)
    for b in range(16):
        sel[b, b, :] = 1.0
    return {"c_ident": np.eye(128, dtype=np.float32), "c_caus": caus, "c_tril": tril, "c_lows": lows, "c_m2": m2,
            "c_kmask": kmask, "c_mask01": mask01, "c_sel": sel.reshape(16, 2048)}


def kernel(**inp):
    if "nc" not in _CACHE:
        _CACHE["nc"] = build_program()
    nc = _CACHE["nc"]
    f = lambda a: np.ascontiguousarray(np.asarray(a, dtype=np.float32))
    wnames = ["norm1_g", "w_in", "hgrn_lb_logits", "hgrn_norm_g", "q_norm_g", "k_norm_g", "swa_sinks", "gmlp_ln_g", "gmlp_ln_b",
              "gmlp_ws", "gmlp_bs", "w_out", "norm2_g", "w_up", "conv_w", "conv_b", "w_down", "ple_norm_g", "w_ple_gate", "w_ple_proj"]
    shared = {k: f(inp[k]) for k in wnames}
    shared.update(_consts())
    x_prompt = f(inp["x_prompt"]); x_sample = f(inp["x_sample"])
    st_h = f(inp["state_hgrn"]); c_k = f(inp["cache_swa_k"]); c_v = f(inp["cache_swa_v"])
    st_c = f(inp["state_ffn_conv"]); p_p = f(inp["p_prompt"]); p_s = f(inp["p_sample"])
    in_maps = []
    for c in range(NCORES):
        sl = slice(c * NB, (c + 1) * NB)
        m = dict(shared)
        m["xp"] = x_prompt[c]
        m["xs"] = x_sample[sl, 0]
        m["sh"] = np.ascontiguousarray(st_h[:, sl])
        m["ck"] = np.ascontiguousarray(c_k[:, sl].reshape(DEPTH, NB, 128, 128))
        m["cv"] = np.ascontiguousarray(c_v[:, sl].reshape(DEPTH, NB, 128, 128))
        m["scv"] = np.ascontiguousarray(st_c[:, sl])
        m["pp"] = np.ascontiguousarray(p_p[:, c])
        m["psm"] = np.ascontiguousarray(p_s[:, sl, 0])
        in_maps.append(m)
    res = run_bass_kernel_spmd(nc, in_maps, core_ids=list(range(NCORES)))
    R = res.results
    cat = lambda k, ax: np.concatenate([np.asarray(R[c][k]) for c in range(NCORES)], axis=ax)
    stk = lambda k, ax: np.stack([np.asarray(R[c][k]) for c in range(NCORES)], axis=ax)
    y_prompt = stk("y_p", 0)
    y_sample = cat("y_s", 0).reshape(NCORES * NB, 1, D)
    hgrn_p = stk("hp_o", 1)
    hgrn_s = cat("hs_o", 1)
    kp = stk("kp_o", 1).reshape(DEPTH, NCORES, 128, 2, 64)
    vp = stk("vp_o", 1).reshape(DEPTH, NCORES, 128, 2, 64)
    ks = cat("ks_o", 1).reshape(DEPTH, NCORES * NB, 128, 2, 64)
    vs = cat("vs_o", 1).reshape(DEPTH, NCORES * NB, 128, 2, 64)
    gv = cat("gv_o", 1).reshape(DEPTH, NCORES * NB, 1, 4, 128)
    cp = stk("cp_o", 1)
    cs = cat("cs_o", 1)
    return tuple(np.ascontiguousarray(a, dtype=np.float32) for a in
                 (y_prompt, y_sample, hgrn_p, hgrn_s, kp, vp, ks, vs, gv, cp, cs))
```
